# Optimizing a Trainium2 kernel written in Bass

```python
import math
import jax, jax.numpy as jnp
from jax import lax
import numpy as np

D_MODEL = 1024
BATCH = 4
SEQ = 8192
DEPTH = 1

HEAD_DIM = 64
N_ATT_HEADS = 12
ATT_WIDTH = N_ATT_HEADS * HEAD_DIM
DILATED_PATTERNS = ((128, 1), (512, 4), (2048, 16))
ATT_BLOCK = 128
SSM_EXPAND = 2
SSM_INNER = SSM_EXPAND * D_MODEL
SSM_HEAD_DIM = 64
SSM_HEADS = SSM_INNER // SSM_HEAD_DIM
SSM_GROUPS = 8
SSM_STATE = 128
SSM_CONV = 4
SSM_CHUNK = 128
CONV_DIM = SSM_INNER + 2 * SSM_GROUPS * SSM_STATE
FFN_HIDDEN = 4 * D_MODEL
N_BRANCHES = 2
IN_SPLITS = (ATT_WIDTH, ATT_WIDTH, ATT_WIDTH, SSM_INNER, CONV_DIM, SSM_HEADS, N_BRANCHES * D_MODEL)
IN_PROJ_WIDTH = sum(IN_SPLITS)
RMS_EPS = 1e-6

kernel_name = "hybrid_dilated_attn_mamba2_gated_block"


def rmsnorm(x, w):
    x32 = x.astype(jnp.float32)
    y = x32 * lax.rsqrt(jnp.mean(x32 * x32, axis=-1, keepdims=True) + RMS_EPS)
    return (y * w.astype(jnp.float32)).astype(x.dtype)


def alibi_slopes(n):
    def pow2(m):
        start = 2.0 ** (-8.0 / m)
        return [start ** (i + 1) for i in range(m)]
    if (n & (n - 1)) == 0:
        s = pow2(n)
    else:
        c = 2 ** int(math.floor(math.log2(n)))
        s = pow2(c) + pow2(2 * c)[0::2][: n - c]
    return jnp.asarray(np.array(s, dtype=np.float32))


def dilated_window_attention(q, k, v, slopes, window, dilation):
    b, s, h, dh = q.shape
    band = window // dilation
    blk = ATT_BLOCK
    span = dilation * blk
    s_pad = -(-s // span) * span
    pad = ((0, 0), (0, s_pad - s), (0, 0), (0, 0))
    q, k, v = [jnp.pad(t, pad) for t in (q, k, v)]
    nb = s_pad // span
    qb = q.reshape(b, nb, blk, dilation, h, dh)
    kb = k.reshape(b, nb, blk, dilation, h, dh)
    vb = v.reshape(b, nb, blk, dilation, h, dh)

    def with_prev(t):
        prev = jnp.concatenate([jnp.zeros_like(t[:, :1]), t[:, :-1]], axis=1)
        return jnp.concatenate([prev, t], axis=2)

    kk, vv = with_prev(kb), with_prev(vb)
    scores = jnp.einsum('bnirhd,bnjrhd->bnrhij', qb, kk,
                        preferred_element_type=jnp.float32) * (dh ** -0.5)
    i_idx = jnp.arange(blk)[:, None]
    j_idx = jnp.arange(2 * blk)[None, :]
    dist = blk + i_idx - j_idx
    n_idx = jnp.arange(nb)[:, None, None]
    valid = (dist >= 0) & (dist <= band) & ((n_idx > 0) | (j_idx >= blk))
    bias = -slopes[:, None, None] * (dist * dilation).astype(jnp.float32)
    scores = scores + bias[None, None, None]
    scores = jnp.where(valid[None, :, None, None], scores, -jnp.inf)
    m = jnp.max(scores, axis=-1, keepdims=True)
    p = jnp.exp(scores - m)
    l = jnp.sum(p, axis=-1)
    o = jnp.einsum('bnrhij,bnjrhd->bnirhd', p, vv.astype(jnp.float32))
    l_t = jnp.transpose(l, (0, 1, 4, 2, 3))
    m_t = jnp.transpose(m[..., 0], (0, 1, 4, 2, 3))
    o = o / l_t[..., None]
    o = o.reshape(b, s_pad, h, dh)[:, :s]
    return o, m_t.reshape(b, s_pad, h)[:, :s], l_t.reshape(b, s_pad, h)[:, :s]


def ssd_chunked(x, dt, a, bmat, cmat):
    b, s, g, hg, p = x.shape
    n = bmat.shape[-1]
    ch = SSM_CHUNK
    s_pad = -(-s // ch) * ch
    padlen = s_pad - s
    x = jnp.pad(x.astype(jnp.float32), ((0, 0), (0, padlen), (0, 0), (0, 0), (0, 0)))
    dt = jnp.pad(dt, ((0, 0), (0, padlen), (0, 0), (0, 0)))
    bmat = jnp.pad(bmat.astype(jnp.float32), ((0, 0), (0, padlen), (0, 0), (0, 0)))
    cmat = jnp.pad(cmat.astype(jnp.float32), ((0, 0), (0, padlen), (0, 0), (0, 0)))
    c = s_pad // ch
    xd = (x * dt[..., None]).reshape(b, c, ch, g, hg, p)
    la = jnp.moveaxis((dt * a).reshape(b, c, ch, g, hg), 2, -1)
    bc = bmat.reshape(b, c, ch, g, n)
    cc = cmat.reshape(b, c, ch, g, n)
    a_cs = jnp.cumsum(la, axis=-1)
    tril = jnp.tril(jnp.ones((ch, ch), dtype=bool))
    decay = jnp.exp(jnp.where(tril, a_cs[..., :, None] - a_cs[..., None, :], -jnp.inf))
    cb = jnp.einsum('bclgn,bcsgn->bcgls', cc, bc)
    y_diag = jnp.einsum('bcghls,bcsghp->bclghp', cb[:, :, :, None] * decay, xd)
    decay_states = jnp.exp(a_cs[..., -1:] - a_cs)
    states = jnp.einsum('bclgn,bcghl,bclghp->bcghpn', bc, decay_states, xd)
    chunk_decay = jnp.exp(a_cs[..., -1])

    def step(hstate, inp):
        st, dec = inp
        return hstate * dec[..., None, None] + st, hstate

    init = jnp.zeros((b, g, hg, p, n), jnp.float32)
    _, prev = lax.scan(step, init, (jnp.moveaxis(states, 1, 0), jnp.moveaxis(chunk_decay, 1, 0)))
    prev = jnp.moveaxis(prev, 0, 1)
    y_off = jnp.einsum('bclgn,bcghpn,bcghl->bclghp', cc, prev, jnp.exp(a_cs))
    return (y_diag + y_off).reshape(b, s_pad, g, hg, p)[:, :s]


def mamba2_mixer(z, xbc, dt_raw, conv_w, conv_b, dt_bias, a_log, d_skip, norm_w):
    b, s, _ = z.shape
    hg = SSM_HEADS // SSM_GROUPS
    xbc = lax.conv_general_dilated(xbc, conv_w[:, None, :], window_strides=(1,),
                                   padding=[(SSM_CONV - 1, 0)],
                                   dimension_numbers=('NWC', 'WIO', 'NWC'),
                                   feature_group_count=CONV_DIM) + conv_b
    xbc = jax.nn.silu(xbc)
    xs, bm, cm = jnp.split(xbc, [SSM_INNER, SSM_INNER + SSM_GROUPS * SSM_STATE], axis=-1)
    xh = xs.reshape(b, s, SSM_GROUPS, hg, SSM_HEAD_DIM)
    bm = bm.reshape(b, s, SSM_GROUPS, SSM_STATE)
    cm = cm.reshape(b, s, SSM_GROUPS, SSM_STATE)
    dt = jax.nn.softplus((dt_raw + dt_bias).astype(jnp.float32)).reshape(b, s, SSM_GROUPS, hg)
    a = -jnp.exp(a_log.astype(jnp.float32)).reshape(SSM_GROUPS, hg)
    y = ssd_chunked(xh, dt, a, bm, cm)
    y = y + d_skip.astype(jnp.float32).reshape(SSM_GROUPS, hg)[:, :, None] * xh.astype(jnp.float32)
    y = y.reshape(b, s, SSM_INNER) * jax.nn.silu(z.astype(jnp.float32))
    yg = y.reshape(b, s, SSM_GROUPS, SSM_INNER // SSM_GROUPS)
    yg = yg * lax.rsqrt(jnp.mean(yg * yg, axis=-1, keepdims=True) + RMS_EPS)
    y = yg.reshape(b, s, SSM_INNER) * norm_w.astype(jnp.float32)
    return y.astype(z.dtype)


def setup_inputs(seed: int = 0) -> dict:
    key = jax.random.key(seed)
    ks = jax.random.split(key, 20)
    f32 = jnp.float32

    def nrm(k, shape, scale):
        return jax.random.normal(k, shape, f32) * scale

    def gain(k, width):
        return 1.0 + 0.05 * jax.random.normal(k, (DEPTH, width), f32)

    dt0 = jnp.exp(jax.random.uniform(ks[7], (DEPTH, SSM_HEADS), f32,
                                     math.log(1e-3), math.log(1e-1)))
    return {
        "x": jax.random.normal(ks[0], (BATCH, SEQ, D_MODEL), f32),
        "norm_mix_pre_w": gain(ks[1], D_MODEL),
        "w_in": nrm(ks[2], (DEPTH, D_MODEL, IN_PROJ_WIDTH), D_MODEL ** -0.5),
        "b_gate": nrm(ks[3], (DEPTH, N_BRANCHES * D_MODEL), 0.02),
        "conv_w": nrm(ks[4], (DEPTH, SSM_CONV, CONV_DIM), SSM_CONV ** -0.5),
        "conv_b": nrm(ks[5], (DEPTH, CONV_DIM), 0.02),
        "dt_bias": dt0 + jnp.log(-jnp.expm1(-dt0)),
        "a_log": jnp.log(jax.random.uniform(ks[8], (DEPTH, SSM_HEADS), f32, 1.0, 16.0)),
        "d_skip": 1.0 + 0.1 * jax.random.normal(ks[9], (DEPTH, SSM_HEADS), f32),
        "ssm_norm_w": gain(ks[10], SSM_INNER),
        "w_att_proj": nrm(ks[11], (DEPTH, ATT_WIDTH, D_MODEL), ATT_WIDTH ** -0.5),
        "w_ssm_proj": nrm(ks[12], (DEPTH, SSM_INNER, D_MODEL), SSM_INNER ** -0.5),
        "w_out": nrm(ks[13], (DEPTH, D_MODEL, D_MODEL), D_MODEL ** -0.5),
        "norm_mix_post_w": gain(ks[14], D_MODEL),
        "norm_ffn_pre_w": gain(ks[15], D_MODEL),
        "w_up": nrm(ks[16], (DEPTH, D_MODEL, FFN_HIDDEN), D_MODEL ** -0.5),
        "w_down": nrm(ks[17], (DEPTH, FFN_HIDDEN, D_MODEL), FFN_HIDDEN ** -0.5),
        "norm_ffn_post_w": gain(ks[18], D_MODEL),
    }


def reference(x, norm_mix_pre_w, w_in, b_gate, conv_w, conv_b, dt_bias, a_log, d_skip,
              ssm_norm_w, w_att_proj, w_ssm_proj, w_out, norm_mix_post_w, norm_ffn_pre_w,
              w_up, w_down, norm_ffn_post_w):
    b, s, _ = x.shape
    slopes = alibi_slopes(N_ATT_HEADS)
    offsets = [int(o) for o in np.cumsum(IN_SPLITS)[:-1]]
    h = x
    for layer in range(DEPTH):
        u = rmsnorm(h, norm_mix_pre_w[layer])
        proj = u @ w_in[layer]
        q, k, v, z, xbc, dt_raw, gate_logits = jnp.split(proj, offsets, axis=-1)
        q = q.reshape(b, s, N_ATT_HEADS, HEAD_DIM)
        k = k.reshape(b, s, N_ATT_HEADS, HEAD_DIM)
        v = v.reshape(b, s, N_ATT_HEADS, HEAD_DIM)
        outs, maxes, dens = [], [], []
        for window, dilation in DILATED_PATTERNS:
            o_g, m_g, l_g = dilated_window_attention(q, k, v, slopes, window, dilation)
            outs.append(o_g)
            maxes.append(m_g)
            dens.append(l_g)
        m_all = jnp.stack(maxes)
        wts = jnp.exp(m_all - jnp.max(m_all, axis=0, keepdims=True)) * jnp.stack(dens)
        att = jnp.sum(wts[..., None] * jnp.stack(outs), axis=0) / jnp.sum(wts, axis=0)[..., None]
        att = att.reshape(b, s, ATT_WIDTH).astype(x.dtype) @ w_att_proj[layer]
        ssm = mamba2_mixer(z, xbc, dt_raw, conv_w[layer], conv_b[layer], dt_bias[layer],
                           a_log[layer], d_skip[layer], ssm_norm_w[layer])
        ssm = ssm @ w_ssm_proj[layer]
        gates = jax.nn.sigmoid(gate_logits + b_gate[layer])
        g_att, g_ssm = jnp.split(gates, 2, axis=-1)
        mixed = (g_att * att + g_ssm * ssm) @ w_out[layer]
        h = h + rmsnorm(mixed, norm_mix_post_w[layer])
        f = rmsnorm(h, norm_ffn_pre_w[layer])
        f = jnp.square(jax.nn.relu(f @ w_up[layer])) @ w_down[layer]
        h = h + rmsnorm(f, norm_ffn_post_w[layer])
    return h
```

```python
import numpy as np
import concourse.bass as bass
import concourse.mybir as mybir
from concourse.bass_utils import run_bass_kernel_spmd

F32 = mybir.dt.float32
BF16 = mybir.dt.bfloat16
U8 = mybir.dt.uint8
AF = mybir.ActivationFunctionType
ALU = mybir.AluOpType
AX = mybir.AxisListType
ESZ = {F32: 4, BF16: 2, U8: 1}

D = 1024
SEQ = 8192
NH = 12
DH = 64
ATTW = 768
SSI = 2048
NSH = 32
NG = 8
NST = 128
CONVD = 4096
FFN = 4096
EPS = 1e-6
OFF_Q, OFF_K, OFF_V, OFF_Z, OFF_XBC, OFF_DT, OFF_G = 0, 768, 1536, 2304, 4352, 8448, 8480
INW = 10528
PATTERNS = ((128, 1), (512, 4), (2048, 16))
LT = 8192
MAIN0 = 4096


def region(ap):
    t = ap.tensor
    es = ESZ[ap.dtype]
    dims = ap.ap
    sp = str(ap.space)
    if sp in ("SB", "PSUM") or "SB" in sp or "PSUM" in sp:
        pstride = dims[0][0]
        off = ap.offset % pstride if pstride > 0 else ap.offset
        ext = sum(s * (c - 1) for s, c in dims[1:]) + 1
        if "PSUM" in sp:
            return ("ps", (off * es) // 2048 * 2048, ((off + ext) * es + 2047) // 2048 * 2048)
        return ("sb", off * es, (off + ext) * es)
    ext = sum(s * (c - 1) for s, c in dims) + 1
    return (t.name, ap.offset * es, (ap.offset + ext) * es)


class Prog:
    CE = ("pe", "act", "dve", "pool")
    ALL = ("pe", "act", "dve", "pool", "sp")
    K = 12

    def __init__(self):
        self.ops = {e: [] for e in self.ALL}
        self.acc = {}
        self.seen = {e: {f: -1 for f in self.CE} for e in self.ALL}
        self.seen_dma = {e: set() for e in self.ALL}
        self.ndma = {e: 0 for e in self.ALL}

    def _dep(self, eng, rec, deps, dma_deps):
        peng, pidx, pdma = rec[2], rec[3], rec[5]
        if pdma:
            dma_deps.add((peng, pidx))
        else:
            if peng == "pe" and eng == "pe":
                return
            deps[peng] = max(deps.get(peng, -1), pidx)

    def add(self, eng, fn, reads=(), writes=(), dma=False):
        idx = len(self.ops[eng])
        deps, dma_deps = {}, set()
        rr = [region(a) for a in reads]
        ww = [region(a) for a in writes]
        for key, lo, hi in rr:
            for rec in self.acc.get(key, ()):
                if rec[4] and rec[0] < hi and lo < rec[1]:
                    self._dep(eng, rec, deps, dma_deps)
        for key, lo, hi in ww:
            for rec in self.acc.get(key, ()):
                if rec[0] < hi and lo < rec[1]:
                    self._dep(eng, rec, deps, dma_deps)
        for key, lo, hi in ww:
            lst = self.acc.setdefault(key, [])
            lst[:] = [r for r in lst if not (lo <= r[0] and r[1] <= hi)]
            lst.append((lo, hi, eng, idx, True, dma))
        for key, lo, hi in rr:
            lst = self.acc.setdefault(key, [])
            lst[:] = [r for r in lst if not ((not r[4]) and r[2] == eng and r[5] == dma and (not dma)
                                             and lo <= r[0] and r[1] <= hi)]
            lst.append((lo, hi, eng, idx, False, dma))
        waits = []
        for f, j in deps.items():
            if j <= self.seen[eng][f]:
                continue
            self.seen[eng][f] = j
            self.ops[f][j]["signal"] = True
            waits.append(("ce", f, j))
        for (q, j) in dma_deps:
            if (q, j) in self.seen_dma[eng]:
                continue
            self.seen_dma[eng].add((q, j))
            waits.append(("dma", q, j))
        op = {"fn": fn, "waits": waits, "signal": False, "dma": dma}
        if dma:
            k = self.ndma[eng]
            self.ndma[eng] += 1
            op["dk"] = k
        self.ops[eng].append(op)
        return idx

    def emit(self, nc, sems):
        signum = {}
        for e in self.CE:
            c = 0
            arr = []
            for op in self.ops[e]:
                if op["signal"] and not op["dma"]:
                    c += 1
                arr.append(c)
            signum[e] = arr
        prog = self
        K = self.K

        def run(e, h):
            dma_hist = []
            for op in prog.ops[e]:
                for w in op["waits"]:
                    if w[0] == "ce":
                        h.wait_ge(sems[w[1]], signum[w[1]][w[2]])
                    else:
                        pk = prog.ops[w[1]][w[2]]["dk"]
                        h.wait_ge(sems[("dma", w[1], pk % K)], 16 * (pk // K + 1))
                if op["dma"]:
                    k = op["dk"]
                    if k >= K:
                        h.wait_ge(sems[("dma", e, k % K)], 16 * (k // K))
                    ins = op["fn"](h)
                    ins.then_inc(sems[("dma", e, k % K)], 16)
                else:
                    ins = op["fn"](h)
                    if op["signal"]:
                        ins.then_inc(sems[e], 1)

        with nc.Block() as block:
            @block.tensor
            def _(h):
                run("pe", h)

            @block.scalar
            def _(h):
                run("act", h)

            @block.vector
            def _(h):
                run("dve", h)

            @block.gpsimd
            def _(h):
                run("pool", h)

            @block.sync
            def _(h):
                run("sp", h)


class Arena:
    def __init__(self, ap_u8, nbytes):
        self.ap = ap_u8
        self.n = nbytes
        self.top = 0

    def alloc(self, shape, dtype):
        n = int(np.prod(shape)) * ESZ[dtype]
        self.top = (self.top + 63) // 64 * 64
        assert self.top + n <= self.n, f"SBUF arena overflow {self.top + n} > {self.n}"
        v = self.ap[:, self.top:self.top + n]
        self.top += n
        if dtype != U8:
            v = v.bitcast(dtype)
        if len(shape) == 2:
            v = v.rearrange("p (a b) -> p a b", a=shape[0])
        elif len(shape) == 3:
            v = v.rearrange("p (a b c) -> p a b c", a=shape[0], b=shape[1])
        elif len(shape) == 4:
            v = v.rearrange("p (a b c d) -> p a b c d", a=shape[0], b=shape[1], c=shape[2])
        return v

    def mark(self):
        return self.top

    def release(self, m):
        self.top = m


class Builder:
    def __init__(self, debug=None):
        self.debug = debug
        self.nc = bass.Bass("TRN2", target_bir_lowering=False)
        self.P = Prog()
        self.rr = 0

    def dma(self, out, in_, q="sp", **kw):
        self.P.add(q, lambda h: h.dma_start(out=out, in_=in_, **kw), [in_], [out], dma=True)

    def mm(self, out, lhsT, rhs, start=True, stop=True, **kw):
        self.P.add("pe", lambda h: h.matmul(out, lhsT, rhs, start=start, stop=stop, **kw), [lhsT, rhs], [out])

    def tr(self, out, in_, ident):
        self.P.add("pe", lambda h: h.transpose(out, in_, ident), [in_, ident], [out])

    def act(self, out, in_, func, bias=None, scale=None, accum_out=None):
        kw = {}
        rd = [in_]
        wr = [out]
        if bias is not None:
            kw["bias"] = bias
            if not isinstance(bias, (int, float)):
                rd.append(bias)
        if scale is not None:
            kw["scale"] = scale
            if not isinstance(scale, (int, float)):
                rd.append(scale)
        if accum_out is not None:
            kw["accum_out"] = accum_out
            wr.append(accum_out)
        self.P.add("act", lambda h: h.activation(out, in_, func, **kw), rd, wr)

    def tt(self, eng, out, in0, in1, op):
        self.P.add(eng, lambda h: h.tensor_tensor(out, in0, in1, op), [in0, in1], [out])

    def ts(self, eng, out, in0, s1, op0, s2=None, op1=None, accum_out=None):
        rd = [in0] + [s for s in (s1, s2) if s is not None and not isinstance(s, (int, float))]
        wr = [out] + ([accum_out] if accum_out is not None else [])
        kw = {}
        if op1 is not None:
            kw["op1"] = op1
        if accum_out is not None:
            kw["accum_out"] = accum_out
        self.P.add(eng, lambda h: h.tensor_scalar(out, in0, s1, s2, op0, **kw), rd, wr)

    def stt(self, eng, out, in0, scalar, in1, op0, op1):
        rd = [in0, in1] + ([scalar] if not isinstance(scalar, (int, float)) else [])
        self.P.add(eng, lambda h: h.scalar_tensor_tensor(out, in0, scalar, in1, op0, op1), rd, [out])

    def copy(self, eng, out, in_):
        if eng == "act":
            self.P.add("act", lambda h: h.copy(out, in_), [in_], [out])
        else:
            self.P.add(eng, lambda h: h.tensor_copy(out, in_), [in_], [out])

    def memset(self, eng, out, val):
        self.P.add(eng, lambda h: h.memset(out, val), [], [out])

    def recip(self, out, in_):
        self.P.add("dve", lambda h: h.reciprocal(out, in_), [in_], [out])

    def ev(self):
        self.rr += 1
        return ("act", "dve")[self.rr % 2]


def build_program(debug=False, stages=("p1", "p2", "p3", "c1", "c2"), nblk=None):
    B = Builder()
    nc = B.nc
    P = B.P

    def din(name, shape):
        return nc.dram_tensor(name, list(shape), F32, kind="ExternalInput").ap()

    x = din("x", [LT, D])
    w_in = din("w_in", [D, INW])
    nmw = din("norm_mix_pre_w", [D])
    b_gate = din("b_gate", [2 * D])
    conv_w = din("conv_w", [4, CONVD])
    conv_b = din("conv_b", [CONVD])
    dt_bias = din("dt_bias", [1, NSH])
    a_log = din("a_log", [1, NSH])
    d_skip = din("d_skip", [1, NSH])
    ssm_nw = din("ssm_norm_w", [SSI])
    w_att = din("w_att_proj", [ATTW, D])
    w_ssm = din("w_ssm_proj", [SSI, D])
    w_out = din("w_out", [D, D])
    npost = din("norm_mix_post_w", [1, D])
    nfpre = din("norm_ffn_pre_w", [D])
    w_up = din("w_up", [D, FFN])
    w_down = din("w_down", [FFN, D])
    nfpost = din("norm_ffn_post_w", [1, D])
    c_ident = din("c_ident", [128, 128])
    c_masks = din("c_masks", [128, 3 * NH * 2 * 128])
    c_triu = din("c_triu", [128, 128])
    c_ls = din("c_ls", [128, 128])
    c_pfx = din("c_pfx", [128, 64])
    out = nc.dram_tensor("out", [MAIN0, D], F32, kind="ExternalOutput").ap()

    kind = "ExternalOutput" if debug else "Internal"

    def dscr(name, shape, dt):
        if debug:
            return nc.dram_tensor(name, list(shape), dt, kind="ExternalOutput").ap()
        return nc.dram_tensor(name, list(shape), dt).ap()

    QT = dscr("s_qt", [NH, DH, MAIN0], BF16)
    KT = dscr("s_kt", [NH, DH, 6144], BF16)
    VV = dscr("s_v", [LT, ATTW], BF16)
    ZS = dscr("s_zs", [MAIN0, SSI], F32)
    ATT = dscr("s_att", [NH, DH, MAIN0], BF16)
    YT = dscr("s_yt", [16, 128, MAIN0], BF16)
    HS = dscr("s_h", [MAIN0, D], F32)

    NB = 207 * 1024
    import contextlib
    with contextlib.ExitStack() as es:
        at = es.enter_context(nc.sbuf_tensor("arena", [128, NB], U8))
        pt = es.enter_context(nc.psum_tensor("ps", [128, 4096], F32))
        A = Arena(at[:], NB)
        ps = pt[:]

        def bank(i):
            return ps[:, i * 512:(i + 1) * 512]

        idf = A.alloc([128], F32)
        idb = A.alloc([128], BF16)
        epst = A.alloc([1], F32)
        B.dma(idf, c_ident)
        B.copy("dve", idb, idf)
        B.memset("pool", epst, EPS)
        BASE = A.mark()

        ctr = [0]

        def ev2():
            ctr[0] += 1
            return ("act", "dve")[ctr[0] % 2]

        def load_weight(dst, src_rows, ncols, nk, scale_vec=None, stage=None, col0=0, engs=("dve", "pool", "act")):
            for kc in range(nk):
                for c0 in range(0, ncols, 2048):
                    c1 = min(ncols, c0 + 2048)
                    st = stage[(ctr[0]) % len(stage)]
                    ctr[0] += 1
                    B.dma(st[:, 0:c1 - c0], src_rows(kc)[:, col0 + c0:col0 + c1])
                    e = engs[ctr[0] % len(engs)]
                    if e == "act":
                        B.act(dst[:, kc, c0:c1], st[:, 0:c1 - c0], AF.Copy,
                              scale=(scale_vec[:, kc:kc + 1] if scale_vec is not None else 1.0))
                    elif scale_vec is not None:
                        B.ts(e, dst[:, kc, c0:c1], st[:, 0:c1 - c0], scale_vec[:, kc:kc + 1], ALU.mult)
                    else:
                        B.copy(e, dst[:, kc, c0:c1], st[:, 0:c1 - c0])

        def rstd_of(ssq, n, tmp, outp):
            B.act(tmp, ssq, AF.Ln, bias=epst[:, 0:1], scale=1.0 / n)
            B.act(outp, tmp, AF.Exp, scale=-0.5)

        def load_weight_cast(dst, src, nk, ncols, col0=0):
            for kc in range(nk):
                B.dma(dst[:, kc, :], src[kc * 128:(kc + 1) * 128, col0:col0 + ncols], q="pool")

        def make_xnT(tok0, ntile, xbufs, xn, junk, st4, xnT, psb, keep_x=None, wbc=None):
            for t in range(ntile):
                xt = xbufs[t % len(xbufs)] if keep_x is None else keep_x[:, t, :]
                B.dma(xt, x[tok0 + t * 128: tok0 + (t + 1) * 128, :])
                B.act(junk, xt, AF.Square, accum_out=st4[:, 0:1])
                rstd_of(st4[:, 0:1], D, st4[:, 1:2], st4[:, 2:3])
                if wbc is None:
                    B.ts("dve", xn, xt, st4[:, 2:3], ALU.mult)
                else:
                    B.stt("dve", xn, xt, st4[:, 2:3], wbc, ALU.mult, ALU.mult)
                pst = psb.bitcast(BF16)
                for kc in range(8):
                    B.tr(pst[:, kc * 128:(kc + 1) * 128], xn[:, kc * 128:(kc + 1) * 128], idb)
                B.copy(ev2(), xnT[:, :, t * 128:(t + 1) * 128], pst.rearrange("p (k t) -> p k t", k=8))

        A.release(BASE)
        nwb = A.alloc([1024], F32)
        B.dma(nwb, nmw.rearrange("(o n) -> o n", o=1).partition_broadcast(128))
        Wq = A.alloc([8, 2304], BF16)
        Wz = A.alloc([8, 2048], BF16)
        load_weight_cast(Wq, w_in, 8, 2304, col0=0)
        load_weight_cast(Wz, w_in, 8, 2048, col0=OFF_Z)
        xbufs = [A.alloc([1024], F32) for _ in range(2)]
        xn = A.alloc([1024], BF16)
        junk = A.alloc([1024], BF16)
        st4 = A.alloc([4], F32)
        xnT = A.alloc([8, 512], BF16)
        qk_sb = [A.alloc([512], BF16) for _ in range(3)]
        v_sb = [A.alloc([768], BF16) for _ in range(2)]
        z_sb = [A.alloc([2048], F32) for _ in range(2)]
        pbi = 0
        for blk in (range(12) if "p1" in stages else ()):
            tok0 = 2048 + blk * 512
            is_main = tok0 >= MAIN0
            make_xnT(tok0, 4, xbufs, xn, junk, st4, xnT, bank(7), wbc=nwb)
            for cc in range(12):
                if cc < 6 and not is_main:
                    continue
                pb = bank(pbi % 2)
                pbi += 1
                for kc in range(8):
                    B.mm(pb, Wq[:, kc, cc * 128:(cc + 1) * 128], xnT[:, kc, :], start=(kc == 0), stop=(kc == 7))
                sb = qk_sb[cc % 3]
                if cc < 6:
                    B.act(sb, pb, AF.Copy, scale=0.125)
                    for hh in range(2):
                        B.dma(QT[2 * cc + hh, :, tok0 - MAIN0: tok0 - MAIN0 + 512], sb[64 * hh:64 * hh + 64, :], q="pool")
                else:
                    B.copy("dve", sb, pb)
                    for hh in range(2):
                        B.dma(KT[2 * (cc - 6) + hh, :, tok0 - 2048: tok0 - 2048 + 512], sb[64 * hh:64 * hh + 64, :], q="pool")
            for t in range(4):
                vs = v_sb[t % 2]
                for nb in range(2):
                    pb = bank(pbi % 2)
                    pbi += 1
                    for kc in range(8):
                        B.mm(pb[:, 0:384], xnT[:, kc, t * 128:(t + 1) * 128], Wq[:, kc, OFF_V + nb * 384: OFF_V + (nb + 1) * 384],
                             start=(kc == 0), stop=(kc == 7))
                    B.copy(ev2(), vs[:, nb * 384:(nb + 1) * 384], pb[:, 0:384])
                B.dma(VV[tok0 + t * 128: tok0 + (t + 1) * 128, :], vs, q="pool")
                if is_main:
                    zs = z_sb[t % 2]
                    for nb in range(4):
                        pb = bank(2 + pbi % 2)
                        pbi += 1
                        for kc in range(8):
                            B.mm(pb, xnT[:, kc, t * 128:(t + 1) * 128], Wz[:, kc, nb * 512:(nb + 1) * 512],
                                 start=(kc == 0), stop=(kc == 7))
                        B.act(zs[:, nb * 512:(nb + 1) * 512], pb, AF.Silu)
                    B.dma(ZS[tok0 - MAIN0 + t * 128: tok0 - MAIN0 + (t + 1) * 128, :], zs, q="pool")

        A.release(BASE)
        maskf = A.alloc([3, NH, 2, 128], F32)
        B.dma(maskf.rearrange("p a b c d -> p (a b c d)"), c_masks)
        onesb = A.alloc([64], BF16)
        pfxf = A.alloc([64], F32)
        pfxb = A.alloc([64], BF16)
        B.memset("pool", onesb, 1.0)
        B.dma(pfxf, c_pfx)
        B.copy("dve", pfxb, pfxf)
        NT = (33, 36, 48)
        Vh = [[A.alloc([NT[g], 2, 64], BF16) for g in range(3)] for _ in range(2)]
        for bsel in range(2):
            for g, (win, d) in enumerate(PATTERNS):
                B.memset("pool", Vh[bsel][g][:, :, 1, :], 1.0)
                B.copy("dve", Vh[bsel][g][:, 0:d, 1, :], pfxb.unsqueeze(1).to_broadcast([128, d, 64]))
        KThs = [A.alloc([6144], BF16) for _ in range(2)]
        QThs = [A.alloc([4096], BF16) for _ in range(2)]
        ACC = A.alloc([2048], F32)
        rcp2 = A.alloc([2048], F32)
        Ebuf = [A.alloc([2, 2, 128], F32) for _ in range(4)]
        PTb = [A.alloc([2, 2, 128], BF16) for _ in range(4)]
        rcp = A.alloc([2048], F32)
        att_sb = A.alloc([2048], BF16)
        VBASE = (3968, 3584, 2048)
        heads = list(range(NH if nblk is None else nblk)) if "p2" in stages else []

        def load_head(h):
            for g, (win, d) in enumerate(PATTERNS):
                span = 128 * d
                for m in range(NT[g] // d):
                    base = VBASE[g] + span * m
                    src = VV[base: base + span, h * 64:(h + 1) * 64].rearrange("(i r) c -> i r c", r=d)
                    B.dma(Vh[h % 2][g][:, m * d:(m + 1) * d, 0, :], src)
            B.dma(KThs[h % 2][0:64, :], KT[h])
            B.dma(QThs[h % 2][0:64, :], QT[h])

        def stage_a(tk):
            h, g, d, span, pair, ui = tk["h"], tk["g"], tk["d"], tk["span"], tk["pair"], tk["ui"]
            KTh, QTh = KThs[h % 2], QThs[h % 2]
            st = bank(ui % 4).rearrange("p (u b q) -> p u b q", u=2, b=2)
            Eb, Pb = Ebuf[ui % 4], PTb[ui % 4]
            for u, (ms, mq, r) in enumerate(pair):
                qloc = VBASE[g] + span * mq + r - MAIN0
                for bl in range(2):
                    kloc = VBASE[g] + span * (mq - 1 + bl) + r - 2048
                    B.mm(st[:, u, bl, :], KTh[0:64, kloc: kloc + d * 127 + 1: d], QTh[0:64, qloc: qloc + d * 127 + 1: d])
            B.act(Eb, st, AF.Exp)
            B.tt(("pool", "pool", "dve")[ui % 3], Pb, Eb, maskf[:, g, h, :, :].unsqueeze(1).to_broadcast([128, 2, 2, 128]), ALU.mult)

        def stage_b(tk):
            h, g, d, span, pair, ui, first = tk["h"], tk["g"], tk["d"], tk["span"], tk["pair"], tk["ui"], tk["first"]
            hq = h % 4
            Pb = PTb[ui % 4]
            ol = bank(4 + ui % 4)[:, 0:256].rearrange("p (u q) -> p u q", u=2)
            for u, (ms, mq, r) in enumerate(pair):
                for bl in range(2):
                    tile_i = (mq - 1 + bl) * d + r
                    lhsT = Vh[h % 2][g][:, tile_i, :, :].rearrange("p a b -> p (a b)")
                    B.mm(ol[:, u, :], lhsT, Pb[:, u, bl, :], start=(bl == 0), stop=(bl == 1))
            (ms0, _, r0) = pair[0]
            p0 = ms0 * span + r0
            if d == 1:
                dst = ACC[:, p0: p0 + 256].rearrange("p (u q) -> p u q", u=2)
                if first:
                    B.copy("act", dst, ol)
                else:
                    B.tt("dve", dst, ol, dst, ALU.add)
            else:
                for u in range(2):
                    av = ACC[:, p0 + u: p0 + u + d * 127 + 1: d]
                    if first:
                        B.copy("dve", av, ol[:, u, :])
                    else:
                        B.tt("dve", av, ol[:, u, :], av, ALU.add)

        ui = 0
        if heads:
            load_head(heads[0])
        for hi_, h in enumerate(heads):
            if hi_ + 1 < len(heads):
                load_head(heads[hi_ + 1])
            for sp in range(2):
                tasks = []
                for g, (win, d) in enumerate(PATTERNS):
                    span = 128 * d
                    units = []
                    for ms in range(2048 // span):
                        mq = (MAIN0 + 2048 * sp - VBASE[g]) // span + ms
                        for r in range(d):
                            units.append((ms, mq, r))
                    for u0 in range(0, len(units), 2):
                        tasks.append(dict(h=h, g=g, d=d, span=span, pair=units[u0:u0 + 2], ui=ui, first=(g == 0)))
                        ui += 1
                pend = []
                for tk in tasks:
                    stage_a(tk)
                    pend.append(tk)
                    if len(pend) > 3:
                        stage_b(pend.pop(0))
                for tk in pend:
                    stage_b(tk)
                B.recip(rcp[64:128, :], ACC[64:128, :])
                B.dma(rcp2[0:64, :], rcp[64:128, :])
                B.tt("pool", att_sb[0:64, :], ACC[0:64, :], rcp2[0:64, :], ALU.mult)
                B.dma(ATT[h, :, sp * 2048:(sp + 1) * 2048], att_sb[0:64, :], q="pool")
        return_ctx = dict(B=B, nc=nc, A=A, BASE=BASE, bank=bank, ps=ps, es=es, idb=idb, idf=idf, epst=epst,
                          ev2=ev2, load_weight=load_weight, load_weight_cast=load_weight_cast, rstd_of=rstd_of, make_xnT=make_xnT, ctr=ctr)
        loc = dict(locals())
        if "p3" in stages:
            build_ssm(loc)
        build_tail(loc)

        P.add("sp", lambda h: h.nop(), [out, QT, KT, VV, ZS, ATT, YT, HS], [])
        sems = {}
        for e in Prog.CE:
            sems[e] = es.enter_context(nc.semaphore("s_" + e))
        for q in ("sp", "pool"):
            for k in range(Prog.K):
                sems[("dma", q, k)] = es.enter_context(nc.semaphore(f"d_{q}{k}"))
        P.emit(nc, sems)
    return nc


def build_ssm(L):
    B, nc, A, bank, idb, idf, epst = L["B"], L["nc"], L["A"], L["bank"], L["idb"], L["idf"], L["epst"]
    ev2, load_weight, rstd_of, make_xnT, ctr = L["ev2"], L["load_weight"], L["rstd_of"], L["make_xnT"], L["ctr"]
    w_in, nmw, conv_w, conv_b, dt_bias, a_log, d_skip = L["w_in"], L["nmw"], L["conv_w"], L["conv_b"], L["dt_bias"], L["a_log"], L["d_skip"]
    c_triu, c_ls, c_pfx, ZS, YT, x = L["c_triu"], L["c_ls"], L["c_pfx"], L["ZS"], L["YT"], L["x"]
    A.release(L["BASE"])
    nw = A.alloc([8], F32)
    B.dma(nw, nmw.rearrange("(k p) -> p k", p=128), allow_slow_non_contiguous=True)
    cb = A.alloc([32], F32)
    B.dma(cb, conv_b.rearrange("(c p) -> p c", p=128), allow_slow_non_contiguous=True)
    Wx = A.alloc([8, 4096], BF16)
    Wdt = A.alloc([8, 32], BF16)
    DG = A.alloc([32, 4, 128], BF16)
    mk = A.mark()
    stg = [A.alloc([2048], F32) for _ in range(4)]
    rows = lambda kc: w_in[kc * 128:(kc + 1) * 128, :]
    load_weight(Wx, rows, 4096, 8, nw, stg, col0=OFF_XBC)
    load_weight(Wdt, rows, 32, 8, nw, stg, col0=OFF_DT)
    cw = A.alloc([32, 4], F32)
    for k in range(4):
        B.dma(cw[:, :, k], conv_w[k].rearrange("(c p) -> p c", p=128), allow_slow_non_contiguous=True)
    for cc in range(32):
        for k in range(4):
            B.ts(("dve", "pool")[(cc * 4 + k) % 2], DG[:, cc, k, :], idf, cw[:, cc, k:k + 1], ALU.mult)
    A.release(mk)
    onesb = A.alloc([128], BF16)
    B.memset("pool", onesb, 1.0)
    Uf = A.alloc([128], F32)
    Ub = A.alloc([128], BF16)
    LSf = A.alloc([128], F32)
    B.dma(Uf, c_triu)
    B.dma(LSf, c_ls)
    B.copy("dve", Ub, Uf)
    dtb = A.alloc([32], F32)
    aneg = A.alloc([32], F32)
    dsk = A.alloc([32], F32)
    pfx1 = A.alloc([64], F32)
    B.dma(dtb, dt_bias.partition_broadcast(128))
    B.dma(aneg, a_log.partition_broadcast(128))
    B.dma(dsk, d_skip.partition_broadcast(128))
    B.dma(pfx1, c_pfx)
    B.act(aneg, aneg, AF.Exp)
    B.ts("dve", aneg, aneg, -1.0, ALU.mult)
    H = A.alloc([2048], F32)
    Hbf = A.alloc([2048], BF16)
    B.memset("dve", H, 0.0)
    XB = A.alloc([32, 131], BF16)
    B.memset("pool", XB, 0.0)
    xbufs = [A.alloc([1024], F32)]
    st4 = A.alloc([4], F32)
    xnTs = [A.alloc([8, 128], BF16) for _ in range(2)]
    XS = A.alloc([2048], BF16)
    junkA, xn = XS[:, 0:1024], XS[:, 1024:2048]
    XCx = A.alloc([16, 128], BF16)
    LAh = A.alloc([32], BF16)
    LAl = A.alloc([32], BF16)
    XCbc = [A.alloc([16, 128], BF16) for _ in range(2)]
    XD = [A.alloc([2048], BF16) for _ in range(2)]
    XDD = [A.alloc([2048], BF16) for _ in range(2)]
    XSD = [A.alloc([2048], BF16) for _ in range(2)]
    Btok = [A.alloc([1024], BF16) for _ in range(2)]
    SM = [A.alloc([12, 32], F32) for _ in range(2)]
    CBm = A.alloc([8, 128], F32)
    RH = [A.alloc([4, 128], F32)]
    Eb = [A.alloc([4, 128], F32) for _ in range(2)]
    MT = [A.alloc([4, 128], BF16) for _ in range(2)]
    T1 = [A.alloc([256], F32) for _ in range(2)]
    zt = A.alloc([2048], F32)
    YZ = zt
    ssq = A.alloc([8], F32)
    rs8 = A.alloc([2, 8], F32)
    YN = A.alloc([2048], BF16)
    YTs = A.alloc([16, 128], BF16)
    cnt = {"pa": 0, "g": 0}
    chunks = list(range(64)) if L["nblk"] is None else [0, 1, 31, 32, 33]

    def names(p):
        sm = SM[p]
        return [sm[:, i, :] for i in range(10)]

    def stage_pro(c, delay=0):
        for _ in range(delay):
            yield
        make_xnT(c * 128, 1, xbufs, xn, junkA, st4, xnTs[c % 2], bank(2))
        yield

    def stage_a(c):
        p = c % 2
        tok0 = c * 128
        main = c >= 32
        xnT = xnTs[p]
        DTr, DT, LA, ACSs, EA, DST, DSTATE, CD, DTDS, TMP = names(p)
        b3 = bank(3)
        for kc in range(8):
            B.mm(b3[:, 0:32], xnT[:, kc, :], Wdt[:, kc, :], start=(kc == 0), stop=(kc == 7))
        B.tt("dve", DTr, b3[:, 0:32], dtb, ALU.add)
        B.act(TMP, DTr, AF.Exp)
        B.act(DT, TMP, AF.Ln, bias=1.0)
        B.tt("dve", LA, DT, aneg, ALU.mult)
        B.copy("dve", LAh, LA)
        B.tt("dve", LAl, LA, LAh, ALU.subtract)
        yield
        nc4 = 8 if c >= 31 else 6
        for c4 in range(nc4):
            pb = bank(cnt["pa"] % 2)
            cnt["pa"] += 1
            for j in range(4):
                cc = c4 * 4 + j
                for kc in range(8):
                    B.mm(pb[:, j * 128:(j + 1) * 128], Wx[:, kc, cc * 128:(cc + 1) * 128], xnT[:, kc, :], start=(kc == 0), stop=(kc == 7))
            B.copy(ev2(), XB[:, c4 * 4:(c4 + 1) * 4, 3:131], pb.rearrange("p (j t) -> p j t", j=4))
            yield
        b3 = bank(3)
        B.mm(b3[:, 64:96], Ub, LAh, start=True, stop=False)
        B.mm(b3[:, 64:96], Ub, LAl, start=False, stop=True)
        B.mm(b3[:, 96:128], onesb, LAh, start=True, stop=False)
        B.mm(b3[:, 96:128], onesb, LAl, start=False, stop=True)
        B.copy("dve", ACSs, b3[:, 64:96])
        B.tt("dve", DST, b3[:, 96:128], ACSs, ALU.subtract)
        B.act(DSTATE, DST, AF.Exp)
        B.act(CD, b3[:, 96:128], AF.Exp)
        if main:
            B.act(EA, ACSs, AF.Exp)
        B.tt("dve", DTDS, DT, DSTATE, ALU.mult)
        yield
        for c4 in range(8 if main else 6):
            pb = bank(cnt["pa"] % 2)
            cnt["pa"] += 1
            for j in range(4):
                cc = c4 * 4 + j
                for k in range(4):
                    B.mm(pb[:, j * 128:(j + 1) * 128], DG[:, cc, k, :], XB[:, cc, k:k + 128], start=(k == 0), stop=(k == 3))
            for j in range(4):
                cc = c4 * 4 + j
                dst = XCx[:, cc, :] if cc < 16 else XCbc[p][:, cc - 16, :]
                B.act(dst, pb[:, j * 128:(j + 1) * 128], AF.Silu, bias=cb[:, cc:cc + 1])
        yield
        B.copy("pool", XB[:, :, 0:3], XB[:, :, 128:131])
        for half in range(2):
            pst = bank(2).bitcast(BF16)
            for j in range(8):
                B.tr(pst[:, j * 128:(j + 1) * 128], XCx[:, half * 8 + j, :], idb)
            B.copy(ev2(), XS[:, half * 1024:(half + 1) * 1024], pst)
            yield
        pst = bank(2).bitcast(BF16)
        for j in range(8):
            B.tr(pst[:, j * 128:(j + 1) * 128], XCbc[p][:, j, :], idb)
        B.copy(ev2(), Btok[p], pst)
        yield
        xs3 = XS.rearrange("p (h e) -> p h e", h=32)
        B.tt("dve", XDD[p].rearrange("p (h e) -> p h e", h=32), xs3, DTDS.unsqueeze(2).to_broadcast([128, 32, 64]), ALU.mult)
        if main:
            B.tt("dve", XD[p].rearrange("p (h e) -> p h e", h=32), xs3, DT.unsqueeze(2).to_broadcast([128, 32, 64]), ALU.mult)
            B.tt("pool", XSD[p].rearrange("p (h e) -> p h e", h=32), xs3, dsk.unsqueeze(2).to_broadcast([128, 32, 64]), ALU.mult)
        yield

    def stage_b(c):
        p = c % 2
        tok0 = c * 128
        main = c >= 32
        DTr, DT, LA, ACSs, EA, DST, DSTATE, CD, DTDS, TMP = names(p)
        xc = XCbc[p]
        if not main:
            for _ in range(6):
                yield
        if main:
            B.dma(zt, ZS[tok0 - MAIN0: tok0 - MAIN0 + 128, :])
            for half in range(2):
                pb = bank(5 - half)
                for j in range(4):
                    g = half * 4 + j
                    B.mm(pb[:, j * 128:(j + 1) * 128], xc[:, g, :], xc[:, 8 + g, :])
                B.tt("dve", CBm[:, half * 4:(half + 1) * 4, :], pb.rearrange("p (j t) -> p j t", j=4),
                     Uf.unsqueeze(1).to_broadcast([128, 4, 128]), ALU.mult)
                yield

            def g1(g, gi):
                rh, eb, mt = RH[0], Eb[gi % 2], MT[gi % 2]
                dp = bank(5)
                for j in range(4):
                    hh = 4 * g + j
                    B.ts("dve", rh[:, j, :], Uf, LA[:, hh:hh + 1], ALU.mult)
                B.mm(dp, LSf, rh.rearrange("p j t -> p (j t)"))
                B.act(eb, dp.rearrange("p (j t) -> p j t", j=4), AF.Exp)
                B.tt("pool", mt, eb, CBm[:, g, :].unsqueeze(1).to_broadcast([128, 4, 128]), ALU.mult)

            def g2(g, gi):
                mt, t1 = MT[gi % 2], T1[gi % 2]
                yb = bank(6 + gi % 2)
                B.mm(yb[:, 0:256], idb, XSD[p][:, g * 256:(g + 1) * 256], start=True, stop=False)
                for j in range(4):
                    hh = 4 * g + j
                    B.mm(yb[:, j * 64:(j + 1) * 64], mt[:, j, :], XD[p][:, hh * 64:(hh + 1) * 64], start=False, stop=(j == 3))
                B.mm(yb[:, 256:512], xc[:, 8 + g, :], Hbf[:, g * 256:(g + 1) * 256])
                B.tt("dve", t1.rearrange("p (h e) -> p h e", h=4), yb[:, 256:512].rearrange("p (h e) -> p h e", h=4),
                     EA[:, 4 * g:4 * g + 4].unsqueeze(2).to_broadcast([128, 4, 64]), ALU.mult)
                B.tt("dve", t1, yb[:, 0:256], t1, ALU.add)
                B.tt("pool", YZ[:, g * 256:(g + 1) * 256], t1, zt[:, g * 256:(g + 1) * 256], ALU.mult)
                B.act(YN[:, 0:256], YZ[:, g * 256:(g + 1) * 256], AF.Square, accum_out=ssq[:, g:g + 1])

            gi0 = cnt["g"]
            cnt["g"] += 8
            g1(0, gi0)
            for g in range(8):
                if g + 1 < 8:
                    g1(g + 1, gi0 + g + 1)
                g2(g, gi0 + g)
                yield
        if c < 63:
            for gp in range(4):
                pb = bank(4)
                for j in range(2):
                    g = gp * 2 + j
                    B.mm(pb[:, j * 256:(j + 1) * 256], Btok[p][:, g * 128:(g + 1) * 128], XDD[p][:, g * 256:(g + 1) * 256])
                hv = H[:, gp * 512:(gp + 1) * 512]
                B.tt("dve", hv.rearrange("p (h e) -> p h e", h=8), hv.rearrange("p (h e) -> p h e", h=8),
                     CD[:, gp * 8:(gp + 1) * 8].unsqueeze(2).to_broadcast([128, 8, 64]), ALU.mult)
                B.tt("dve", hv, pb, hv, ALU.add)
                yield
            if c == 31:
                B.ts("dve", H, H, pfx1[:, 0:1], ALU.mult)
            if c >= 31:
                B.copy("act", Hbf, H)
        if main:
            rstd_of(ssq, 256, rs8[:, 0, :], rs8[:, 1, :])
            B.tt("dve", YN.rearrange("p (g e) -> p g e", g=8), YZ.rearrange("p (g e) -> p g e", g=8),
                 rs8[:, 1, :].unsqueeze(2).to_broadcast([128, 8, 256]), ALU.mult)
            for half in range(2):
                pst = bank(4).bitcast(BF16)
                for j in range(8):
                    cc = half * 8 + j
                    B.tr(pst[:, j * 128:(j + 1) * 128], YN[:, cc * 128:(cc + 1) * 128], idb)
                B.copy(ev2(), YTs[:, half * 8:(half + 1) * 8, :], pst.rearrange("p (j t) -> p j t", j=8))
                yield
            B.dma(YT[:, :, tok0 - MAIN0: tok0 - MAIN0 + 128].rearrange("c p t -> p c t"), YTs, q="pool")
        yield

    def interleave(gens):
        gens = [g for g in gens if g is not None]
        while gens:
            for g in list(gens):
                try:
                    next(g)
                except StopIteration:
                    gens.remove(g)

    n = len(chunks)
    interleave([stage_pro(chunks[0])])
    interleave([stage_a(chunks[0]), stage_pro(chunks[1]) if n > 1 else None])
    for i, c in enumerate(chunks):
        interleave([stage_pro(chunks[i + 2], delay=4) if i + 2 < n else None,
                    stage_a(chunks[i + 1]) if i + 1 < n else None,
                    stage_b(c)])


def build_tail(L):
    B, nc, A, bank, idb, idf, epst = L["B"], L["nc"], L["A"], L["bank"], L["idb"], L["idf"], L["epst"]
    ev2, load_weight, rstd_of, make_xnT, ctr = L["ev2"], L["load_weight"], L["rstd_of"], L["make_xnT"], L["ctr"]
    w_in, nmw, b_gate, ssm_nw, w_att, w_ssm, w_out = L["w_in"], L["nmw"], L["b_gate"], L["ssm_nw"], L["w_att"], L["w_ssm"], L["w_out"]
    npost, nfpre, w_up, w_down, nfpost = L["npost"], L["nfpre"], L["w_up"], L["w_down"], L["nfpost"]
    ATT, YT, HS, x, out = L["ATT"], L["YT"], L["HS"], L["x"], L["out"]
    A.release(L["BASE"])
    nwb = A.alloc([1024], F32)
    B.dma(nwb, nmw.rearrange("(o n) -> o n", o=1).partition_broadcast(128))
    snw = A.alloc([16], F32)
    B.dma(snw, ssm_nw.rearrange("(k p) -> p k", p=128), allow_slow_non_contiguous=True)
    bg = A.alloc([16], F32)
    B.dma(bg, b_gate.rearrange("(k p) -> p k", p=128), allow_slow_non_contiguous=True)
    Wg = A.alloc([8, 2048], BF16)
    Watt = A.alloc([12, 1024], BF16)
    Wssm = A.alloc([16, 1024], BF16)
    Wout = A.alloc([8, 1024], BF16)
    L["load_weight_cast"](Wg, w_in, 8, 2048, col0=OFF_G)
    B.dma(Watt[0:64, :, :], w_att.rearrange("(h d) n -> d h n", d=64), q="pool")
    L["load_weight_cast"](Wout, w_out, 8, 1024)
    mk = A.mark()
    stg = [A.alloc([2048], F32) for _ in range(2)]
    load_weight(Wssm, lambda kc: w_ssm[kc * 128:(kc + 1) * 128, :], 1024, 16, snw, stg)
    A.release(mk)
    npb = A.alloc([1024], F32)
    B.dma(npb, npost.partition_broadcast(128))
    TB = 512
    NTB = TB // 128
    xk = A.alloc([NTB, 1024], F32)
    xn = A.alloc([1024], BF16)
    junk = A.alloc([1024], BF16)
    st4 = A.alloc([8], F32)
    xnT = A.alloc([8, TB], BF16)
    G = A.alloc([16, TB], BF16)
    aT = A.alloc([12, TB], BF16)
    yT = A.alloc([16, TB], BF16)
    m1 = [A.alloc([TB], F32) for _ in range(2)]
    m2 = [A.alloc([TB], F32) for _ in range(2)]
    mT = A.alloc([8, TB], BF16)
    hb = [A.alloc([1024], F32) for _ in range(1)]
    pbi = 0
    for blk in (range(MAIN0 // TB if L["nblk"] is None else L["nblk"]) if "c1" in L["stages"] else ()):
        t0 = blk * TB
        B.dma(aT[0:64, :, :], ATT[:, :, t0:t0 + TB].rearrange("h d t -> d h t"))
        B.dma(yT, YT[:, :, t0:t0 + TB].rearrange("c p t -> p c t"))
        make_xnT(MAIN0 + t0, NTB, None, xn, junk, st4, xnT, bank(7), keep_x=xk, wbc=nwb)
        for cc in range(16):
            pb = bank(pbi % 2)
            pbi += 1
            for kc in range(8):
                B.mm(pb[:, 0:TB], Wg[:, kc, cc * 128:(cc + 1) * 128], xnT[:, kc, :], start=(kc == 0), stop=(kc == 7))
            B.act(G[:, cc, :], pb[:, 0:TB], AF.Sigmoid, bias=bg[:, cc:cc + 1])
        for cc in range(8):
            pa = bank(2 + pbi % 2)
            pbs = bank(4 + pbi % 2)
            pbi += 1
            for hd in range(12):
                B.mm(pa[:, 0:TB], Watt[0:64, hd, cc * 128:(cc + 1) * 128], aT[0:64, hd, :], start=(hd == 0), stop=(hd == 11))
            for kc in range(16):
                B.mm(pbs[:, 0:TB], Wssm[:, kc, cc * 128:(cc + 1) * 128], yT[:, kc, :], start=(kc == 0), stop=(kc == 15))
            a1, a2 = m1[cc % 2], m2[cc % 2]
            B.tt("dve", a1, pa[:, 0:TB], G[:, cc, :], ALU.mult)
            B.tt("dve", a2, pbs[:, 0:TB], G[:, 8 + cc, :], ALU.mult)
            B.tt("pool", mT[:, cc, :], a1, a2, ALU.add)
        for t in range(TB // 128):
            pw = L["ps"][:, 6 * 512:8 * 512]
            for nb in range(2):
                for kc in range(8):
                    B.mm(pw[:, nb * 512:(nb + 1) * 512], mT[:, kc, t * 128:(t + 1) * 128], Wout[:, kc, nb * 512:(nb + 1) * 512],
                         start=(kc == 0), stop=(kc == 7))
            for nb in range(2):
                B.act(junk[:, nb * 512:(nb + 1) * 512], pw[:, nb * 512:(nb + 1) * 512], AF.Square, accum_out=st4[:, 3 + nb:4 + nb])
            B.tt("dve", st4[:, 5:6], st4[:, 3:4], st4[:, 4:5], ALU.add)
            rstd_of(st4[:, 5:6], D, st4[:, 6:7], st4[:, 7:8])
            hh = hb[0]
            for nb in range(2):
                B.tt("dve", hh[:, nb * 512:(nb + 1) * 512], pw[:, nb * 512:(nb + 1) * 512], npb[:, nb * 512:(nb + 1) * 512], ALU.mult)
            B.stt("dve", hh, hh, st4[:, 7:8], xk[:, t, :], ALU.mult, ALU.add)
            B.dma(HS[t0 + t * 128: t0 + (t + 1) * 128, :], hh, q="pool")
    A.release(L["BASE"])
    fwb = A.alloc([1024], F32)
    B.dma(fwb, nfpre.rearrange("(o n) -> o n", o=1).partition_broadcast(128))
    Wup = A.alloc([8, 4096], BF16)
    Wdn = A.alloc([32, 1024], BF16)
    L["load_weight_cast"](Wup, w_up, 8, 4096)
    L["load_weight_cast"](Wdn, w_down, 32, 1024)
    nfb = A.alloc([1024], F32)
    B.dma(nfb, nfpost.partition_broadcast(128))
    hk = A.alloc([NTB, 1024], F32)
    hn = A.alloc([1024], BF16)
    junk = A.alloc([1024], BF16)
    st4 = A.alloc([8], F32)
    hnT = A.alloc([8, TB], BF16)
    hid = A.alloc([32, TB], BF16)
    rl = [A.alloc([TB], F32) for _ in range(2)]
    ob = [A.alloc([1024], F32) for _ in range(1)]
    for blk in (range(MAIN0 // TB if L["nblk"] is None else L["nblk"]) if "c2" in L["stages"] else ()):
        t0 = blk * TB
        for t in range(NTB):
            ht = hk[:, t, :]
            B.dma(ht, HS[t0 + t * 128: t0 + (t + 1) * 128, :])
            B.act(junk, ht, AF.Square, accum_out=st4[:, 0:1])
            rstd_of(st4[:, 0:1], D, st4[:, 1:2], st4[:, 2:3])
            B.stt("dve", hn, ht, st4[:, 2:3], fwb, ALU.mult, ALU.mult)
            pst = bank(7).bitcast(BF16)
            for kc in range(8):
                B.tr(pst[:, kc * 128:(kc + 1) * 128], hn[:, kc * 128:(kc + 1) * 128], idb)
            B.copy(ev2(), hnT[:, :, t * 128:(t + 1) * 128], pst.rearrange("p (k t) -> p k t", k=8))
        for cc in range(32):
            pb = bank(pbi % 2)
            pbi += 1
            for kc in range(8):
                B.mm(pb[:, 0:TB], Wup[:, kc, cc * 128:(cc + 1) * 128], hnT[:, kc, :], start=(kc == 0), stop=(kc == 7))
            r = rl[cc % 2]
            B.act(r, pb[:, 0:TB], AF.Relu)
            B.tt(("dve", "pool")[cc % 2], hid[:, cc, :], r, r, ALU.mult)
        for t in range(NTB):
            pw = L["ps"][:, (2 + 2 * (t % 2)) * 512:(4 + 2 * (t % 2)) * 512]
            for nb in range(2):
                for kc in range(32):
                    B.mm(pw[:, nb * 512:(nb + 1) * 512], hid[:, kc, t * 128:(t + 1) * 128], Wdn[:, kc, nb * 512:(nb + 1) * 512],
                         start=(kc == 0), stop=(kc == 31))
            for nb in range(2):
                B.act(junk[:, nb * 512:(nb + 1) * 512], pw[:, nb * 512:(nb + 1) * 512], AF.Square, accum_out=st4[:, 3 + nb:4 + nb])
            B.tt("dve", st4[:, 5:6], st4[:, 3:4], st4[:, 4:5], ALU.add)
            rstd_of(st4[:, 5:6], D, st4[:, 6:7], st4[:, 7:8])
            oo = ob[0]
            for nb in range(2):
                B.tt("dve", oo[:, nb * 512:(nb + 1) * 512], pw[:, nb * 512:(nb + 1) * 512], nfb[:, nb * 512:(nb + 1) * 512], ALU.mult)
            B.stt("dve", oo, oo, st4[:, 7:8], hk[:, t, :], ALU.mult, ALU.add)
            B.dma(out[t0 + t * 128: t0 + (t + 1) * 128, :], oo, q="pool")


def _alibi_slopes(n):
    def pow2(m):
        start = 2.0 ** (-8.0 / m)
        return [start ** (i + 1) for i in range(m)]
    if (n & (n - 1)) == 0:
        s = pow2(n)
    else:
        c = 2 ** int(np.floor(np.log2(n)))
        s = pow2(c) + pow2(2 * c)[0::2][: n - c]
    return np.array(s, dtype=np.float32)


def _constants():
    k = np.arange(128)[:, None]
    q = np.arange(128)[None, :]
    slopes = _alibi_slopes(NH).astype(np.float64)
    masks = np.zeros((128, 3, NH, 2, 128), np.float32)
    for g, (win, d) in enumerate(PATTERNS):
        dist_prev = 128 + q - k
        dist_cur = q - k
        for bl, dist in enumerate((dist_prev, dist_cur)):
            valid = (dist >= 0) & (dist <= 128)
            for h in range(NH):
                m = np.where(valid, np.exp(-slopes[h] * np.clip(dist, 0, 128) * d), 0.0)
                masks[:, g, h, bl, :] = m.astype(np.float32)
    triu = (k <= q).astype(np.float32)
    ls = (k > q).astype(np.float32)
    return dict(c_ident=np.eye(128, dtype=np.float32), c_masks=masks.reshape(128, -1),
                c_triu=triu, c_ls=ls)


_NC_CACHE = {}


def kernel(**inputs):
    x = np.asarray(inputs["x"], dtype=np.float32)
    nb = x.shape[0]
    consts = _constants()
    shared = {}
    for name in ("w_in", "norm_mix_pre_w", "b_gate", "conv_w", "conv_b", "ssm_norm_w", "w_att_proj", "w_ssm_proj",
                 "w_out", "norm_ffn_pre_w", "w_up", "w_down"):
        shared[name] = np.ascontiguousarray(np.asarray(inputs[name], dtype=np.float32)[0])
    for name in ("dt_bias", "a_log", "d_skip", "norm_mix_post_w", "norm_ffn_post_w"):
        shared[name] = np.ascontiguousarray(np.asarray(inputs[name], dtype=np.float32)[0][None, :])
    shared.update(consts)
    in_maps = []
    for c in range(8):
        b, half = c // 2, c % 2
        xl = np.zeros((LT, D), np.float32)
        if half == 1:
            xl[:] = x[b]
        else:
            xl[MAIN0:] = x[b, :MAIN0]
        m = dict(shared)
        m["x"] = xl
        m["c_pfx"] = np.full((128, 64), float(half), np.float32)
        in_maps.append(m)
    if "nc" not in _NC_CACHE:
        _NC_CACHE["nc"] = build_program()
    nc = _NC_CACHE["nc"]
    res = run_bass_kernel_spmd(nc, in_maps, core_ids=list(range(8)))
    outp = np.zeros((nb, SEQ, D), np.float32)
    for c in range(8):
        b, half = c // 2, c % 2
        outp[b, half * MAIN0:(half + 1) * MAIN0] = res.results[c]["out"]
    return outp
```

```python
import numpy as np
import concourse.bass as bass
import concourse.mybir as mybir
from concourse.bass_utils import run_bass_kernel_spmd

F32 = mybir.dt.float32
BF16 = mybir.dt.bfloat16
U8 = mybir.dt.uint8
AF = mybir.ActivationFunctionType
ALU = mybir.AluOpType
AX = mybir.AxisListType
ESZ = {F32: 4, BF16: 2, U8: 1}

D = 1024
SEQ = 8192
NH = 12
DH = 64
ATTW = 768
SSI = 2048
NSH = 32
NG = 8
NST = 128
CONVD = 4096
FFN = 4096
EPS = 1e-6
OFF_Q, OFF_K, OFF_V, OFF_Z, OFF_XBC, OFF_DT, OFF_G = 0, 768, 1536, 2304, 4352, 8448, 8480
INW = 10528
PATTERNS = ((128, 1), (512, 4), (2048, 16))
LT = 8192
MAIN0 = 4096


def region(ap):
    t = ap.tensor
    es = ESZ[ap.dtype]
    dims = ap.ap
    sp = str(ap.space)
    if sp in ("SB", "PSUM") or "SB" in sp or "PSUM" in sp:
        pstride = dims[0][0]
        off = ap.offset % pstride if pstride > 0 else ap.offset
        ext = sum(s * (c - 1) for s, c in dims[1:]) + 1
        if "PSUM" in sp:
            return ("ps", (off * es) // 2048 * 2048, ((off + ext) * es + 2047) // 2048 * 2048)
        return ("sb", off * es, (off + ext) * es)
    ext = sum(s * (c - 1) for s, c in dims) + 1
    return (t.name, ap.offset * es, (ap.offset + ext) * es)


class Prog:
    CE = ("pe", "act", "dve", "pool")
    ALL = ("pe", "act", "dve", "pool", "sp")
    K = 12

    def __init__(self):
        self.ops = {e: [] for e in self.ALL}
        self.acc = {}
        self.seen = {e: {f: -1 for f in self.CE} for e in self.ALL}
        self.seen_dma = {e: set() for e in self.ALL}
        self.ndma = {e: 0 for e in self.ALL}

    def _dep(self, eng, rec, deps, dma_deps):
        peng, pidx, pdma = rec[2], rec[3], rec[5]
        if pdma:
            dma_deps.add((peng, pidx))
        else:
            if peng == "pe" and eng == "pe":
                return
            deps[peng] = max(deps.get(peng, -1), pidx)

    def add(self, eng, fn, reads=(), writes=(), dma=False):
        idx = len(self.ops[eng])
        deps, dma_deps = {}, set()
        rr = [region(a) for a in reads]
        ww = [region(a) for a in writes]
        for key, lo, hi in rr:
            for rec in self.acc.get(key, ()):
                if rec[4] and rec[0] < hi and lo < rec[1]:
                    self._dep(eng, rec, deps, dma_deps)
        for key, lo, hi in ww:
            for rec in self.acc.get(key, ()):
                if rec[0] < hi and lo < rec[1]:
                    self._dep(eng, rec, deps, dma_deps)
        for key, lo, hi in ww:
            lst = self.acc.setdefault(key, [])
            lst[:] = [r for r in lst if not (lo <= r[0] and r[1] <= hi)]
            lst.append((lo, hi, eng, idx, True, dma))
        for key, lo, hi in rr:
            lst = self.acc.setdefault(key, [])
            lst[:] = [r for r in lst if not ((not r[4]) and r[2] == eng and r[5] == dma and (not dma)
                                             and lo <= r[0] and r[1] <= hi)]
            lst.append((lo, hi, eng, idx, False, dma))
        waits = []
        for f, j in deps.items():
            if j <= self.seen[eng][f]:
                continue
            self.seen[eng][f] = j
            self.ops[f][j]["signal"] = True
            waits.append(("ce", f, j))
        for (q, j) in dma_deps:
            if (q, j) in self.seen_dma[eng]:
                continue
            self.seen_dma[eng].add((q, j))
            waits.append(("dma", q, j))
        op = {"fn": fn, "waits": waits, "signal": False, "dma": dma}
        if dma:
            k = self.ndma[eng]
            self.ndma[eng] += 1
            op["dk"] = k
        self.ops[eng].append(op)
        return idx

    def emit(self, nc, sems):
        signum = {}
        for e in self.CE:
            c = 0
            arr = []
            for op in self.ops[e]:
                if op["signal"] and not op["dma"]:
                    c += 1
                arr.append(c)
            signum[e] = arr
        prog = self
        K = self.K

        def run(e, h):
            dma_hist = []
            for op in prog.ops[e]:
                for w in op["waits"]:
                    if w[0] == "ce":
                        h.wait_ge(sems[w[1]], signum[w[1]][w[2]])
                    else:
                        pk = prog.ops[w[1]][w[2]]["dk"]
                        h.wait_ge(sems[("dma", w[1], pk % K)], 16 * (pk // K + 1))
                if op["dma"]:
                    k = op["dk"]
                    if k >= K:
                        h.wait_ge(sems[("dma", e, k % K)], 16 * (k // K))
                    ins = op["fn"](h)
                    ins.then_inc(sems[("dma", e, k % K)], 16)
                else:
                    ins = op["fn"](h)
                    if op["signal"]:
                        ins.then_inc(sems[e], 1)

        with nc.Block() as block:
            @block.tensor
            def _(h):
                run("pe", h)

            @block.scalar
            def _(h):
                run("act", h)

            @block.vector
            def _(h):
                run("dve", h)

            @block.gpsimd
            def _(h):
                run("pool", h)

            @block.sync
            def _(h):
                run("sp", h)


class Arena:
    def __init__(self, ap_u8, nbytes):
        self.ap = ap_u8
        self.n = nbytes
        self.top = 0

    def alloc(self, shape, dtype):
        n = int(np.prod(shape)) * ESZ[dtype]
        self.top = (self.top + 63) // 64 * 64
        assert self.top + n <= self.n, f"SBUF arena overflow {self.top + n} > {self.n}"
        v = self.ap[:, self.top:self.top + n]
        self.top += n
        if dtype != U8:
            v = v.bitcast(dtype)
        if len(shape) == 2:
            v = v.rearrange("p (a b) -> p a b", a=shape[0])
        elif len(shape) == 3:
            v = v.rearrange("p (a b c) -> p a b c", a=shape[0], b=shape[1])
        elif len(shape) == 4:
            v = v.rearrange("p (a b c d) -> p a b c d", a=shape[0], b=shape[1], c=shape[2])
        return v

    def mark(self):
        return self.top

    def release(self, m):
        self.top = m


class Builder:
    def __init__(self, debug=None):
        self.debug = debug
        self.nc = bass.Bass("TRN2", target_bir_lowering=False)
        self.P = Prog()
        self.rr = 0

    def dma(self, out, in_, q="sp", **kw):
        self.P.add(q, lambda h: h.dma_start(out=out, in_=in_, **kw), [in_], [out], dma=True)

    def mm(self, out, lhsT, rhs, start=True, stop=True, **kw):
        self.P.add("pe", lambda h: h.matmul(out, lhsT, rhs, start=start, stop=stop, **kw), [lhsT, rhs], [out])

    def tr(self, out, in_, ident):
        self.P.add("pe", lambda h: h.transpose(out, in_, ident), [in_, ident], [out])

    def act(self, out, in_, func, bias=None, scale=None, accum_out=None):
        kw = {}
        rd = [in_]
        wr = [out]
        if bias is not None:
            kw["bias"] = bias
            if not isinstance(bias, (int, float)):
                rd.append(bias)
        if scale is not None:
            kw["scale"] = scale
            if not isinstance(scale, (int, float)):
                rd.append(scale)
        if accum_out is not None:
            kw["accum_out"] = accum_out
            wr.append(accum_out)
        self.P.add("act", lambda h: h.activation(out, in_, func, **kw), rd, wr)

    def tt(self, eng, out, in0, in1, op):
        self.P.add(eng, lambda h: h.tensor_tensor(out, in0, in1, op), [in0, in1], [out])

    def ts(self, eng, out, in0, s1, op0, s2=None, op1=None, accum_out=None):
        rd = [in0] + [s for s in (s1, s2) if s is not None and not isinstance(s, (int, float))]
        wr = [out] + ([accum_out] if accum_out is not None else [])
        kw = {}
        if op1 is not None:
            kw["op1"] = op1
        if accum_out is not None:
            kw["accum_out"] = accum_out
        self.P.add(eng, lambda h: h.tensor_scalar(out, in0, s1, s2, op0, **kw), rd, wr)

    def stt(self, eng, out, in0, scalar, in1, op0, op1):
        rd = [in0, in1] + ([scalar] if not isinstance(scalar, (int, float)) else [])
        self.P.add(eng, lambda h: h.scalar_tensor_tensor(out, in0, scalar, in1, op0, op1), rd, [out])

    def copy(self, eng, out, in_):
        if eng == "act":
            self.P.add("act", lambda h: h.copy(out, in_), [in_], [out])
        else:
            self.P.add(eng, lambda h: h.tensor_copy(out, in_), [in_], [out])

    def memset(self, eng, out, val):
        self.P.add(eng, lambda h: h.memset(out, val), [], [out])

    def recip(self, out, in_):
        self.P.add("dve", lambda h: h.reciprocal(out, in_), [in_], [out])

    def ev(self):
        self.rr += 1
        return ("act", "dve")[self.rr % 2]


def build_program(debug=False, stages=("p1", "p2", "p3", "c1", "c2"), nblk=None):
    B = Builder()
    nc = B.nc
    P = B.P

    def din(name, shape):
        return nc.dram_tensor(name, list(shape), F32, kind="ExternalInput").ap()

    x = din("x", [LT, D])
    w_in = din("w_in", [D, INW])
    nmw = din("norm_mix_pre_w", [D])
    b_gate = din("b_gate", [2 * D])
    conv_w = din("conv_w", [4, CONVD])
    conv_b = din("conv_b", [CONVD])
    dt_bias = din("dt_bias", [1, NSH])
    a_log = din("a_log", [1, NSH])
    d_skip = din("d_skip", [1, NSH])
    ssm_nw = din("ssm_norm_w", [SSI])
    w_att = din("w_att_proj", [ATTW, D])
    w_ssm = din("w_ssm_proj", [SSI, D])
    w_out = din("w_out", [D, D])
    npost = din("norm_mix_post_w", [1, D])
    nfpre = din("norm_ffn_pre_w", [D])
    w_up = din("w_up", [D, FFN])
    w_down = din("w_down", [FFN, D])
    nfpost = din("norm_ffn_post_w", [1, D])
    c_ident = din("c_ident", [128, 128])
    c_masks = din("c_masks", [128, 3 * NH * 2 * 128])
    c_triu = din("c_triu", [128, 128])
    c_ls = din("c_ls", [128, 128])
    c_pfx = din("c_pfx", [128, 64])
    out = nc.dram_tensor("out", [MAIN0, D], F32, kind="ExternalOutput").ap()

    kind = "ExternalOutput" if debug else "Internal"

    def dscr(name, shape, dt):
        if debug:
            return nc.dram_tensor(name, list(shape), dt, kind="ExternalOutput").ap()
        return nc.dram_tensor(name, list(shape), dt).ap()

    QT = dscr("s_qt", [NH, DH, MAIN0], BF16)
    KT = dscr("s_kt", [NH, DH, 6144], BF16)
    VV = dscr("s_v", [LT, ATTW], BF16)
    ZS = dscr("s_zs", [MAIN0, SSI], F32)
    ATT = dscr("s_att", [NH, DH, MAIN0], BF16)
    YT = dscr("s_yt", [16, 128, MAIN0], BF16)
    HS = dscr("s_h", [MAIN0, D], F32)

    NB = 207 * 1024
    import contextlib
    with contextlib.ExitStack() as es:
        at = es.enter_context(nc.sbuf_tensor("arena", [128, NB], U8))
        pt = es.enter_context(nc.psum_tensor("ps", [128, 4096], F32))
        A = Arena(at[:], NB)
        ps = pt[:]

        def bank(i):
            return ps[:, i * 512:(i + 1) * 512]

        idf = A.alloc([128], F32)
        idb = A.alloc([128], BF16)
        epst = A.alloc([1], F32)
        B.dma(idf, c_ident)
        B.copy("dve", idb, idf)
        B.memset("pool", epst, EPS)
        BASE = A.mark()

        ctr = [0]

        def ev2():
            ctr[0] += 1
            return ("act", "dve")[ctr[0] % 2]

        def load_weight(dst, src_rows, ncols, nk, scale_vec=None, stage=None, col0=0, engs=("dve", "pool", "act")):
            for kc in range(nk):
                for c0 in range(0, ncols, 2048):
                    c1 = min(ncols, c0 + 2048)
                    st = stage[(ctr[0]) % len(stage)]
                    ctr[0] += 1
                    B.dma(st[:, 0:c1 - c0], src_rows(kc)[:, col0 + c0:col0 + c1])
                    e = engs[ctr[0] % len(engs)]
                    if e == "act":
                        B.act(dst[:, kc, c0:c1], st[:, 0:c1 - c0], AF.Copy,
                              scale=(scale_vec[:, kc:kc + 1] if scale_vec is not None else 1.0))
                    elif scale_vec is not None:
                        B.ts(e, dst[:, kc, c0:c1], st[:, 0:c1 - c0], scale_vec[:, kc:kc + 1], ALU.mult)
                    else:
                        B.copy(e, dst[:, kc, c0:c1], st[:, 0:c1 - c0])

        def rstd_of(ssq, n, tmp, outp):
            B.act(tmp, ssq, AF.Ln, bias=epst[:, 0:1], scale=1.0 / n)
            B.act(outp, tmp, AF.Exp, scale=-0.5)

        def load_weight_cast(dst, src, nk, ncols, col0=0):
            for kc in range(nk):
                B.dma(dst[:, kc, :], src[kc * 128:(kc + 1) * 128, col0:col0 + ncols], q="pool")

        def make_xnT(tok0, ntile, xbufs, xn, junk, st4, xnT, psb, keep_x=None, wbc=None):
            for t in range(ntile):
                xt = xbufs[t % len(xbufs)] if keep_x is None else keep_x[:, t, :]
                B.dma(xt, x[tok0 + t * 128: tok0 + (t + 1) * 128, :])
                B.act(junk, xt, AF.Square, accum_out=st4[:, 0:1])
                rstd_of(st4[:, 0:1], D, st4[:, 1:2], st4[:, 2:3])
                if wbc is None:
                    B.ts("dve", xn, xt, st4[:, 2:3], ALU.mult)
                else:
                    B.stt("dve", xn, xt, st4[:, 2:3], wbc, ALU.mult, ALU.mult)
                pst = psb.bitcast(BF16)
                for kc in range(8):
                    B.tr(pst[:, kc * 128:(kc + 1) * 128], xn[:, kc * 128:(kc + 1) * 128], idb)
                B.copy(ev2(), xnT[:, :, t * 128:(t + 1) * 128], pst.rearrange("p (k t) -> p k t", k=8))

        A.release(BASE)
        nwb = A.alloc([1024], F32)
        B.dma(nwb, nmw.rearrange("(o n) -> o n", o=1).partition_broadcast(128))
        Wq = A.alloc([8, 2304], BF16)
        Wz = A.alloc([8, 2048], BF16)
        load_weight_cast(Wq, w_in, 8, 2304, col0=0)
        load_weight_cast(Wz, w_in, 8, 2048, col0=OFF_Z)
        xbufs = [A.alloc([1024], F32) for _ in range(2)]
        xn = A.alloc([1024], BF16)
        junk = A.alloc([1024], BF16)
        st4 = A.alloc([4], F32)
        xnT = A.alloc([8, 512], BF16)
        qk_sb = [A.alloc([512], BF16) for _ in range(3)]
        v_sb = [A.alloc([768], BF16) for _ in range(2)]
        z_sb = [A.alloc([2048], F32) for _ in range(2)]
        pbi = 0
        for blk in (range(12) if "p1" in stages else ()):
            tok0 = 2048 + blk * 512
            is_main = tok0 >= MAIN0
            make_xnT(tok0, 4, xbufs, xn, junk, st4, xnT, bank(7), wbc=nwb)
            for cc in range(12):
                if cc < 6 and not is_main:
                    continue
                pb = bank(pbi % 2)
                pbi += 1
                for kc in range(8):
                    B.mm(pb, Wq[:, kc, cc * 128:(cc + 1) * 128], xnT[:, kc, :], start=(kc == 0), stop=(kc == 7))
                sb = qk_sb[cc % 3]
                if cc < 6:
                    B.act(sb, pb, AF.Copy, scale=0.125)
                    for hh in range(2):
                        B.dma(QT[2 * cc + hh, :, tok0 - MAIN0: tok0 - MAIN0 + 512], sb[64 * hh:64 * hh + 64, :], q="pool")
                else:
                    B.copy("dve", sb, pb)
                    for hh in range(2):
                        B.dma(KT[2 * (cc - 6) + hh, :, tok0 - 2048: tok0 - 2048 + 512], sb[64 * hh:64 * hh + 64, :], q="pool")
            for t in range(4):
                vs = v_sb[t % 2]
                for nb in range(2):
                    pb = bank(pbi % 2)
                    pbi += 1
                    for kc in range(8):
                        B.mm(pb[:, 0:384], xnT[:, kc, t * 128:(t + 1) * 128], Wq[:, kc, OFF_V + nb * 384: OFF_V + (nb + 1) * 384],
                             start=(kc == 0), stop=(kc == 7))
                    B.copy(ev2(), vs[:, nb * 384:(nb + 1) * 384], pb[:, 0:384])
                B.dma(VV[tok0 + t * 128: tok0 + (t + 1) * 128, :], vs, q="pool")
                if is_main:
                    zs = z_sb[t % 2]
                    for nb in range(4):
                        pb = bank(2 + pbi % 2)
                        pbi += 1
                        for kc in range(8):
                            B.mm(pb, xnT[:, kc, t * 128:(t + 1) * 128], Wz[:, kc, nb * 512:(nb + 1) * 512],
                                 start=(kc == 0), stop=(kc == 7))
                        B.act(zs[:, nb * 512:(nb + 1) * 512], pb, AF.Silu)
                    B.dma(ZS[tok0 - MAIN0 + t * 128: tok0 - MAIN0 + (t + 1) * 128, :], zs, q="pool")

        A.release(BASE)
        maskf = A.alloc([3, NH, 2, 128], F32)
        B.dma(maskf.rearrange("p a b c d -> p (a b c d)"), c_masks)
        onesb = A.alloc([64], BF16)
        pfxf = A.alloc([64], F32)
        pfxb = A.alloc([64], BF16)
        B.memset("pool", onesb, 1.0)
        B.dma(pfxf, c_pfx)
        B.copy("dve", pfxb, pfxf)
        NT = (33, 36, 48)
        Vh = [[A.alloc([NT[g], 2, 64], BF16) for g in range(3)] for _ in range(2)]
        for bsel in range(2):
            for g, (win, d) in enumerate(PATTERNS):
                B.memset("pool", Vh[bsel][g][:, :, 1, :], 1.0)
                B.copy("dve", Vh[bsel][g][:, 0:d, 1, :], pfxb.unsqueeze(1).to_broadcast([128, d, 64]))
        KThs = [A.alloc([6144], BF16) for _ in range(2)]
        QThs = [A.alloc([4096], BF16) for _ in range(2)]
        ACC = A.alloc([2048], F32)
        rcp2 = A.alloc([2048], F32)
        Ebuf = [A.alloc([2, 2, 128], F32) for _ in range(4)]
        PTb = [A.alloc([2, 2, 128], BF16) for _ in range(4)]
        rcp = A.alloc([2048], F32)
        att_sb = A.alloc([2048], BF16)
        VBASE = (3968, 3584, 2048)
        heads = list(range(NH if nblk is None else nblk)) if "p2" in stages else []

        def load_head(h):
            for g, (win, d) in enumerate(PATTERNS):
                span = 128 * d
                for m in range(NT[g] // d):
                    base = VBASE[g] + span * m
                    src = VV[base: base + span, h * 64:(h + 1) * 64].rearrange("(i r) c -> i r c", r=d)
                    B.dma(Vh[h % 2][g][:, m * d:(m + 1) * d, 0, :], src)
            B.dma(KThs[h % 2][0:64, :], KT[h])
            B.dma(QThs[h % 2][0:64, :], QT[h])

        def stage_a(tk):
            h, g, d, span, pair, ui = tk["h"], tk["g"], tk["d"], tk["span"], tk["pair"], tk["ui"]
            KTh, QTh = KThs[h % 2], QThs[h % 2]
            st = bank(ui % 4).rearrange("p (u b q) -> p u b q", u=2, b=2)
            Eb, Pb = Ebuf[ui % 4], PTb[ui % 4]
            for u, (ms, mq, r) in enumerate(pair):
                qloc = VBASE[g] + span * mq + r - MAIN0
                for bl in range(2):
                    kloc = VBASE[g] + span * (mq - 1 + bl) + r - 2048
                    B.mm(st[:, u, bl, :], KTh[0:64, kloc: kloc + d * 127 + 1: d], QTh[0:64, qloc: qloc + d * 127 + 1: d])
            B.act(Eb, st, AF.Exp)
            B.tt(("pool", "pool", "dve")[ui % 3], Pb, Eb, maskf[:, g, h, :, :].unsqueeze(1).to_broadcast([128, 2, 2, 128]), ALU.mult)

        def stage_b(tk):
            h, g, d, span, pair, ui, first = tk["h"], tk["g"], tk["d"], tk["span"], tk["pair"], tk["ui"], tk["first"]
            hq = h % 4
            Pb = PTb[ui % 4]
            ol = bank(4 + ui % 4)[:, 0:256].rearrange("p (u q) -> p u q", u=2)
            for u, (ms, mq, r) in enumerate(pair):
                for bl in range(2):
                    tile_i = (mq - 1 + bl) * d + r
                    lhsT = Vh[h % 2][g][:, tile_i, :, :].rearrange("p a b -> p (a b)")
                    B.mm(ol[:, u, :], lhsT, Pb[:, u, bl, :], start=(bl == 0), stop=(bl == 1))
            (ms0, _, r0) = pair[0]
            p0 = ms0 * span + r0
            if d == 1:
                dst = ACC[:, p0: p0 + 256].rearrange("p (u q) -> p u q", u=2)
                if first:
                    B.copy("act", dst, ol)
                else:
                    B.tt("dve", dst, ol, dst, ALU.add)
            else:
                for u in range(2):
                    av = ACC[:, p0 + u: p0 + u + d * 127 + 1: d]
                    if first:
                        B.copy("dve", av, ol[:, u, :])
                    else:
                        B.tt("dve", av, ol[:, u, :], av, ALU.add)

        ui = 0
        if heads:
            load_head(heads[0])
        for hi_, h in enumerate(heads):
            if hi_ + 1 < len(heads):
                load_head(heads[hi_ + 1])
            for sp in range(2):
                tasks = []
                for g, (win, d) in enumerate(PATTERNS):
                    span = 128 * d
                    units = []
                    for ms in range(2048 // span):
                        mq = (MAIN0 + 2048 * sp - VBASE[g]) // span + ms
                        for r in range(d):
                            units.append((ms, mq, r))
                    for u0 in range(0, len(units), 2):
                        tasks.append(dict(h=h, g=g, d=d, span=span, pair=units[u0:u0 + 2], ui=ui, first=(g == 0)))
                        ui += 1
                pend = []
                for tk in tasks:
                    stage_a(tk)
                    pend.append(tk)
                    if len(pend) > 3:
                        stage_b(pend.pop(0))
                for tk in pend:
                    stage_b(tk)
                B.recip(rcp[64:128, :], ACC[64:128, :])
                B.dma(rcp2[0:64, :], rcp[64:128, :])
                B.tt("pool", att_sb[0:64, :], ACC[0:64, :], rcp2[0:64, :], ALU.mult)
                B.dma(ATT[h, :, sp * 2048:(sp + 1) * 2048], att_sb[0:64, :], q="pool")
        return_ctx = dict(B=B, nc=nc, A=A, BASE=BASE, bank=bank, ps=ps, es=es, idb=idb, idf=idf, epst=epst,
                          ev2=ev2, load_weight=load_weight, load_weight_cast=load_weight_cast, rstd_of=rstd_of, make_xnT=make_xnT, ctr=ctr)
        loc = dict(locals())
        if "p3" in stages:
            build_ssm(loc)
        build_tail(loc)

        P.add("sp", lambda h: h.nop(), [out, QT, KT, VV, ZS, ATT, YT, HS], [])
        sems = {}
        for e in Prog.CE:
            sems[e] = es.enter_context(nc.semaphore("s_" + e))
        for q in ("sp", "pool"):
            for k in range(Prog.K):
                sems[("dma", q, k)] = es.enter_context(nc.semaphore(f"d_{q}{k}"))
        P.emit(nc, sems)
    return nc


def build_ssm(L):
    B, nc, A, bank, idb, idf, epst = L["B"], L["nc"], L["A"], L["bank"], L["idb"], L["idf"], L["epst"]
    ev2, load_weight, rstd_of, make_xnT, ctr = L["ev2"], L["load_weight"], L["rstd_of"], L["make_xnT"], L["ctr"]
    w_in, nmw, conv_w, conv_b, dt_bias, a_log, d_skip = L["w_in"], L["nmw"], L["conv_w"], L["conv_b"], L["dt_bias"], L["a_log"], L["d_skip"]
    c_triu, c_ls, c_pfx, ZS, YT, x = L["c_triu"], L["c_ls"], L["c_pfx"], L["ZS"], L["YT"], L["x"]
    A.release(L["BASE"])
    nw = A.alloc([8], F32)
    B.dma(nw, nmw.rearrange("(k p) -> p k", p=128), allow_slow_non_contiguous=True)
    cb = A.alloc([32], F32)
    B.dma(cb, conv_b.rearrange("(c p) -> p c", p=128), allow_slow_non_contiguous=True)
    Wx = A.alloc([8, 4096], BF16)
    Wdt = A.alloc([8, 32], BF16)
    DG = A.alloc([32, 4, 128], BF16)
    mk = A.mark()
    stg = [A.alloc([2048], F32) for _ in range(4)]
    rows = lambda kc: w_in[kc * 128:(kc + 1) * 128, :]
    load_weight(Wx, rows, 4096, 8, nw, stg, col0=OFF_XBC)
    load_weight(Wdt, rows, 32, 8, nw, stg, col0=OFF_DT)
    cw = A.alloc([32, 4], F32)
    for k in range(4):
        B.dma(cw[:, :, k], conv_w[k].rearrange("(c p) -> p c", p=128), allow_slow_non_contiguous=True)
    for cc in range(32):
        for k in range(4):
            B.ts(("dve", "pool")[(cc * 4 + k) % 2], DG[:, cc, k, :], idf, cw[:, cc, k:k + 1], ALU.mult)
    A.release(mk)
    onesb = A.alloc([128], BF16)
    B.memset("pool", onesb, 1.0)
    Uf = A.alloc([128], F32)
    Ub = A.alloc([128], BF16)
    LSf = A.alloc([128], F32)
    B.dma(Uf, c_triu)
    B.dma(LSf, c_ls)
    B.copy("dve", Ub, Uf)
    dtb = A.alloc([32], F32)
    aneg = A.alloc([32], F32)
    dsk = A.alloc([32], F32)
    pfx1 = A.alloc([64], F32)
    B.dma(dtb, dt_bias.partition_broadcast(128))
    B.dma(aneg, a_log.partition_broadcast(128))
    B.dma(dsk, d_skip.partition_broadcast(128))
    B.dma(pfx1, c_pfx)
    B.act(aneg, aneg, AF.Exp)
    B.ts("dve", aneg, aneg, -1.0, ALU.mult)
    H = A.alloc([2048], F32)
    Hbf = A.alloc([2048], BF16)
    B.memset("dve", H, 0.0)
    XB = A.alloc([32, 131], BF16)
    B.memset("pool", XB, 0.0)
    xbufs = [A.alloc([1024], F32)]
    st4 = A.alloc([4], F32)
    xnTs = [A.alloc([8, 128], BF16) for _ in range(2)]
    XS = A.alloc([2048], BF16)
    junkA, xn = XS[:, 0:1024], XS[:, 1024:2048]
    XCx = A.alloc([16, 128], BF16)
    LAh = A.alloc([32], BF16)
    LAl = A.alloc([32], BF16)
    XCbc = [A.alloc([16, 128], BF16) for _ in range(2)]
    XD = [A.alloc([2048], BF16) for _ in range(2)]
    XDD = [A.alloc([2048], BF16) for _ in range(2)]
    XSD = [A.alloc([2048], BF16) for _ in range(2)]
    Btok = [A.alloc([1024], BF16) for _ in range(2)]
    SM = [A.alloc([12, 32], F32) for _ in range(2)]
    CBm = A.alloc([8, 128], F32)
    RH = [A.alloc([4, 128], F32)]
    Eb = [A.alloc([4, 128], F32) for _ in range(2)]
    MT = [A.alloc([4, 128], BF16) for _ in range(2)]
    T1 = [A.alloc([256], F32) for _ in range(2)]
    zt = A.alloc([2048], F32)
    YZ = zt
    ssq = A.alloc([8], F32)
    rs8 = A.alloc([2, 8], F32)
    YN = A.alloc([2048], BF16)
    YTs = A.alloc([16, 128], BF16)
    cnt = {"pa": 0, "g": 0}
    chunks = list(range(64)) if L["nblk"] is None else [0, 1, 31, 32, 33]

    def names(p):
        sm = SM[p]
        return [sm[:, i, :] for i in range(10)]

    def stage_pro(c, delay=0):
        for _ in range(delay):
            yield
        make_xnT(c * 128, 1, xbufs, xn, junkA, st4, xnTs[c % 2], bank(2))
        yield

    def stage_a(c):
        p = c % 2
        tok0 = c * 128
        main = c >= 32
        xnT = xnTs[p]
        DTr, DT, LA, ACSs, EA, DST, DSTATE, CD, DTDS, TMP = names(p)
        b3 = bank(3)
        for kc in range(8):
            B.mm(b3[:, 0:32], xnT[:, kc, :], Wdt[:, kc, :], start=(kc == 0), stop=(kc == 7))
        B.tt("dve", DTr, b3[:, 0:32], dtb, ALU.add)
        B.act(TMP, DTr, AF.Exp)
        B.act(DT, TMP, AF.Ln, bias=1.0)
        B.tt("dve", LA, DT, aneg, ALU.mult)
        B.copy("dve", LAh, LA)
        B.tt("dve", LAl, LA, LAh, ALU.subtract)
        yield
        nc4 = 8 if c >= 31 else 6

        def proj(c4):
            pb = bank(cnt["pa"] % 2)
            cnt["pa"] += 1
            for j in range(4):
                cc = c4 * 4 + j
                for kc in range(8):
                    B.mm(pb[:, j * 128:(j + 1) * 128], Wx[:, kc, cc * 128:(cc + 1) * 128], xnT[:, kc, :], start=(kc == 0), stop=(kc == 7))
            B.copy(ev2(), XB[:, c4 * 4:(c4 + 1) * 4, 3:131], pb.rearrange("p (j t) -> p j t", j=4))

        def conv(c4s):
            for c4 in c4s:
                pb = bank(cnt["pa"] % 2)
                cnt["pa"] += 1
                for j in range(4):
                    cc = c4 * 4 + j
                    for k in range(4):
                        B.mm(pb[:, j * 128:(j + 1) * 128], DG[:, cc, k, :], XB[:, cc, k:k + 128], start=(k == 0), stop=(k == 3))
                for j in range(4):
                    cc = c4 * 4 + j
                    dst = XCx[:, cc, :] if cc < 16 else XCbc[p][:, cc - 16, :]
                    B.act(dst, pb[:, j * 128:(j + 1) * 128], AF.Silu, bias=cb[:, cc:cc + 1])

        for c4 in range(4):
            proj(c4)
            yield
        b3 = bank(3)
        B.mm(b3[:, 64:96], Ub, LAh, start=True, stop=False)
        B.mm(b3[:, 64:96], Ub, LAl, start=False, stop=True)
        B.mm(b3[:, 96:128], onesb, LAh, start=True, stop=False)
        B.mm(b3[:, 96:128], onesb, LAl, start=False, stop=True)
        B.copy("dve", ACSs, b3[:, 64:96])
        B.tt("dve", DST, b3[:, 96:128], ACSs, ALU.subtract)
        B.act(DSTATE, DST, AF.Exp)
        B.act(CD, b3[:, 96:128], AF.Exp)
        if main:
            B.act(EA, ACSs, AF.Exp)
        B.tt("dve", DTDS, DT, DSTATE, ALU.mult)
        yield
        conv(range(4))
        yield
        for half in range(2):
            pst = bank(2).bitcast(BF16)
            for j in range(8):
                B.tr(pst[:, j * 128:(j + 1) * 128], XCx[:, half * 8 + j, :], idb)
            B.copy("act", XS[:, half * 1024:(half + 1) * 1024], pst)
            yield
        xs3 = XS.rearrange("p (h e) -> p h e", h=32)
        B.tt("dve", XDD[p].rearrange("p (h e) -> p h e", h=32), xs3, DTDS.unsqueeze(2).to_broadcast([128, 32, 64]), ALU.mult)
        if main:
            B.tt("dve", XD[p].rearrange("p (h e) -> p h e", h=32), xs3, DT.unsqueeze(2).to_broadcast([128, 32, 64]), ALU.mult)
            B.tt("pool", XSD[p].rearrange("p (h e) -> p h e", h=32), xs3, dsk.unsqueeze(2).to_broadcast([128, 32, 64]), ALU.mult)
        yield
        for c4 in range(4, nc4):
            proj(c4)
            yield
        conv(range(4, 8 if main else 6))
        B.copy("pool", XB[:, :, 0:3], XB[:, :, 128:131])
        yield
        pst = bank(2).bitcast(BF16)
        for j in range(8):
            B.tr(pst[:, j * 128:(j + 1) * 128], XCbc[p][:, j, :], idb)
        B.copy(ev2(), Btok[p], pst)
        yield

    def stage_b(c):
        p = c % 2
        tok0 = c * 128
        main = c >= 32
        DTr, DT, LA, ACSs, EA, DST, DSTATE, CD, DTDS, TMP = names(p)
        xc = XCbc[p]
        if not main:
            for _ in range(6):
                yield
        if main:
            B.dma(zt, ZS[tok0 - MAIN0: tok0 - MAIN0 + 128, :])
            for half in range(2):
                pb = bank(5 - half)
                for j in range(4):
                    g = half * 4 + j
                    B.mm(pb[:, j * 128:(j + 1) * 128], xc[:, g, :], xc[:, 8 + g, :])
                B.tt("dve", CBm[:, half * 4:(half + 1) * 4, :], pb.rearrange("p (j t) -> p j t", j=4),
                     Uf.unsqueeze(1).to_broadcast([128, 4, 128]), ALU.mult)
                yield

            def g1(g, gi):
                rh, eb, mt = RH[0], Eb[gi % 2], MT[gi % 2]
                dp = bank(5)
                for j in range(4):
                    hh = 4 * g + j
                    B.ts("dve", rh[:, j, :], Uf, LA[:, hh:hh + 1], ALU.mult)
                B.mm(dp, LSf, rh.rearrange("p j t -> p (j t)"))
                B.act(eb, dp.rearrange("p (j t) -> p j t", j=4), AF.Exp)
                B.tt("pool", mt, eb, CBm[:, g, :].unsqueeze(1).to_broadcast([128, 4, 128]), ALU.mult)

            def g2(g, gi):
                mt, t1 = MT[gi % 2], T1[gi % 2]
                yb = bank(6 + gi % 2)
                B.mm(yb[:, 0:256], idb, XSD[p][:, g * 256:(g + 1) * 256], start=True, stop=False)
                for j in range(4):
                    hh = 4 * g + j
                    B.mm(yb[:, j * 64:(j + 1) * 64], mt[:, j, :], XD[p][:, hh * 64:(hh + 1) * 64], start=False, stop=(j == 3))
                B.mm(yb[:, 256:512], xc[:, 8 + g, :], Hbf[:, g * 256:(g + 1) * 256])
                B.tt("dve", t1.rearrange("p (h e) -> p h e", h=4), yb[:, 256:512].rearrange("p (h e) -> p h e", h=4),
                     EA[:, 4 * g:4 * g + 4].unsqueeze(2).to_broadcast([128, 4, 64]), ALU.mult)
                B.tt("dve", t1, yb[:, 0:256], t1, ALU.add)
                B.tt("pool", YZ[:, g * 256:(g + 1) * 256], t1, zt[:, g * 256:(g + 1) * 256], ALU.mult)
                B.act(YN[:, 0:256], YZ[:, g * 256:(g + 1) * 256], AF.Square, accum_out=ssq[:, g:g + 1])

            gi0 = cnt["g"]
            cnt["g"] += 8
            g1(0, gi0)
            for g in range(8):
                if g + 1 < 8:
                    g1(g + 1, gi0 + g + 1)
                g2(g, gi0 + g)
                yield
        if c < 63:
            for gp in range(4):
                pb = bank(4)
                for j in range(2):
                    g = gp * 2 + j
                    B.mm(pb[:, j * 256:(j + 1) * 256], Btok[p][:, g * 128:(g + 1) * 128], XDD[p][:, g * 256:(g + 1) * 256])
                hv = H[:, gp * 512:(gp + 1) * 512]
                B.tt("pool", hv.rearrange("p (h e) -> p h e", h=8), hv.rearrange("p (h e) -> p h e", h=8),
                     CD[:, gp * 8:(gp + 1) * 8].unsqueeze(2).to_broadcast([128, 8, 64]), ALU.mult)
                B.tt("dve", hv, pb, hv, ALU.add)
                yield
            if c == 31:
                B.ts("dve", H, H, pfx1[:, 0:1], ALU.mult)
            if c >= 31:
                B.copy("act", Hbf, H)
        if main:
            rstd_of(ssq, 256, rs8[:, 0, :], rs8[:, 1, :])
            B.tt("dve", YN.rearrange("p (g e) -> p g e", g=8), YZ.rearrange("p (g e) -> p g e", g=8),
                 rs8[:, 1, :].unsqueeze(2).to_broadcast([128, 8, 256]), ALU.mult)
            for half in range(2):
                pst = bank(4).bitcast(BF16)
                for j in range(8):
                    cc = half * 8 + j
                    B.tr(pst[:, j * 128:(j + 1) * 128], YN[:, cc * 128:(cc + 1) * 128], idb)
                B.copy(ev2(), YTs[:, half * 8:(half + 1) * 8, :], pst.rearrange("p (j t) -> p j t", j=8))
                yield
            B.dma(YT[:, :, tok0 - MAIN0: tok0 - MAIN0 + 128].rearrange("c p t -> p c t"), YTs, q="pool")
        yield

    def interleave(gens):
        gens = [g for g in gens if g is not None]
        while gens:
            for g in list(gens):
                try:
                    next(g)
                except StopIteration:
                    gens.remove(g)

    n = len(chunks)
    interleave([stage_pro(chunks[0])])
    interleave([stage_a(chunks[0]), stage_pro(chunks[1]) if n > 1 else None])
    for i, c in enumerate(chunks):
        interleave([stage_pro(chunks[i + 2], delay=4) if i + 2 < n else None,
                    stage_a(chunks[i + 1]) if i + 1 < n else None,
                    stage_b(c)])


def build_tail(L):
    B, nc, A, bank, idb, idf, epst = L["B"], L["nc"], L["A"], L["bank"], L["idb"], L["idf"], L["epst"]
    ev2, load_weight, rstd_of, make_xnT, ctr = L["ev2"], L["load_weight"], L["rstd_of"], L["make_xnT"], L["ctr"]
    w_in, nmw, b_gate, ssm_nw, w_att, w_ssm, w_out = L["w_in"], L["nmw"], L["b_gate"], L["ssm_nw"], L["w_att"], L["w_ssm"], L["w_out"]
    npost, nfpre, w_up, w_down, nfpost = L["npost"], L["nfpre"], L["w_up"], L["w_down"], L["nfpost"]
    ATT, YT, HS, x, out = L["ATT"], L["YT"], L["HS"], L["x"], L["out"]
    A.release(L["BASE"])
    nwb = A.alloc([1024], F32)
    B.dma(nwb, nmw.rearrange("(o n) -> o n", o=1).partition_broadcast(128))
    snw = A.alloc([16], F32)
    B.dma(snw, ssm_nw.rearrange("(k p) -> p k", p=128), allow_slow_non_contiguous=True)
    bg = A.alloc([16], F32)
    B.dma(bg, b_gate.rearrange("(k p) -> p k", p=128), allow_slow_non_contiguous=True)
    Wg = A.alloc([8, 2048], BF16)
    Watt = A.alloc([12, 1024], BF16)
    Wssm = A.alloc([16, 1024], BF16)
    Wout = A.alloc([8, 1024], BF16)
    L["load_weight_cast"](Wg, w_in, 8, 2048, col0=OFF_G)
    B.dma(Watt[0:64, :, :], w_att.rearrange("(h d) n -> d h n", d=64), q="pool")
    L["load_weight_cast"](Wout, w_out, 8, 1024)
    mk = A.mark()
    stg = [A.alloc([2048], F32) for _ in range(2)]
    load_weight(Wssm, lambda kc: w_ssm[kc * 128:(kc + 1) * 128, :], 1024, 16, snw, stg)
    A.release(mk)
    npb = A.alloc([1024], F32)
    B.dma(npb, npost.partition_broadcast(128))
    TB = 512
    NTB = TB // 128
    xk = A.alloc([NTB, 1024], F32)
    xn = A.alloc([1024], BF16)
    junk = A.alloc([1024], BF16)
    st4 = A.alloc([8], F32)
    xnT = A.alloc([8, TB], BF16)
    G = A.alloc([16, TB], BF16)
    aT = A.alloc([12, TB], BF16)
    yT = A.alloc([16, TB], BF16)
    m1 = [A.alloc([TB], F32) for _ in range(2)]
    m2 = [A.alloc([TB], F32) for _ in range(2)]
    mT = A.alloc([8, TB], BF16)
    hb = [A.alloc([1024], F32) for _ in range(1)]
    pbi = 0
    for blk in (range(MAIN0 // TB if L["nblk"] is None else L["nblk"]) if "c1" in L["stages"] else ()):
        t0 = blk * TB
        B.dma(aT[0:64, :, :], ATT[:, :, t0:t0 + TB].rearrange("h d t -> d h t"))
        B.dma(yT, YT[:, :, t0:t0 + TB].rearrange("c p t -> p c t"))
        make_xnT(MAIN0 + t0, NTB, None, xn, junk, st4, xnT, bank(7), keep_x=xk, wbc=nwb)
        for cc in range(16):
            pb = bank(pbi % 2)
            pbi += 1
            for kc in range(8):
                B.mm(pb[:, 0:TB], Wg[:, kc, cc * 128:(cc + 1) * 128], xnT[:, kc, :], start=(kc == 0), stop=(kc == 7))
            B.act(G[:, cc, :], pb[:, 0:TB], AF.Sigmoid, bias=bg[:, cc:cc + 1])
        for cc in range(8):
            pa = bank(2 + pbi % 2)
            pbs = bank(4 + pbi % 2)
            pbi += 1
            for hd in range(12):
                B.mm(pa[:, 0:TB], Watt[0:64, hd, cc * 128:(cc + 1) * 128], aT[0:64, hd, :], start=(hd == 0), stop=(hd == 11))
            for kc in range(16):
                B.mm(pbs[:, 0:TB], Wssm[:, kc, cc * 128:(cc + 1) * 128], yT[:, kc, :], start=(kc == 0), stop=(kc == 15))
            a1, a2 = m1[cc % 2], m2[cc % 2]
            B.tt("dve", a1, pa[:, 0:TB], G[:, cc, :], ALU.mult)
            B.tt("dve", a2, pbs[:, 0:TB], G[:, 8 + cc, :], ALU.mult)
            B.tt("pool", mT[:, cc, :], a1, a2, ALU.add)
        for t in range(TB // 128):
            pw = L["ps"][:, 6 * 512:8 * 512]
            for nb in range(2):
                for kc in range(8):
                    B.mm(pw[:, nb * 512:(nb + 1) * 512], mT[:, kc, t * 128:(t + 1) * 128], Wout[:, kc, nb * 512:(nb + 1) * 512],
                         start=(kc == 0), stop=(kc == 7))
            for nb in range(2):
                B.act(junk[:, nb * 512:(nb + 1) * 512], pw[:, nb * 512:(nb + 1) * 512], AF.Square, accum_out=st4[:, 3 + nb:4 + nb])
            B.tt("dve", st4[:, 5:6], st4[:, 3:4], st4[:, 4:5], ALU.add)
            rstd_of(st4[:, 5:6], D, st4[:, 6:7], st4[:, 7:8])
            hh = hb[0]
            for nb in range(2):
                B.tt("dve", hh[:, nb * 512:(nb + 1) * 512], pw[:, nb * 512:(nb + 1) * 512], npb[:, nb * 512:(nb + 1) * 512], ALU.mult)
            B.stt("dve", hh, hh, st4[:, 7:8], xk[:, t, :], ALU.mult, ALU.add)
            B.dma(HS[t0 + t * 128: t0 + (t + 1) * 128, :], hh, q="pool")
    A.release(L["BASE"])
    fwb = A.alloc([1024], F32)
    B.dma(fwb, nfpre.rearrange("(o n) -> o n", o=1).partition_broadcast(128))
    Wup = A.alloc([8, 4096], BF16)
    Wdn = A.alloc([32, 1024], BF16)
    L["load_weight_cast"](Wup, w_up, 8, 4096)
    L["load_weight_cast"](Wdn, w_down, 32, 1024)
    nfb = A.alloc([1024], F32)
    B.dma(nfb, nfpost.partition_broadcast(128))
    hk = A.alloc([NTB, 1024], F32)
    hn = A.alloc([1024], BF16)
    junk = A.alloc([1024], BF16)
    st4 = A.alloc([8], F32)
    hnT = A.alloc([8, TB], BF16)
    hid = A.alloc([32, TB], BF16)
    rl = [A.alloc([TB], F32) for _ in range(2)]
    ob = [A.alloc([1024], F32) for _ in range(1)]
    for blk in (range(MAIN0 // TB if L["nblk"] is None else L["nblk"]) if "c2" in L["stages"] else ()):
        t0 = blk * TB
        for t in range(NTB):
            ht = hk[:, t, :]
            B.dma(ht, HS[t0 + t * 128: t0 + (t + 1) * 128, :])
            B.act(junk, ht, AF.Square, accum_out=st4[:, 0:1])
            rstd_of(st4[:, 0:1], D, st4[:, 1:2], st4[:, 2:3])
            B.stt("dve", hn, ht, st4[:, 2:3], fwb, ALU.mult, ALU.mult)
            pst = bank(7).bitcast(BF16)
            for kc in range(8):
                B.tr(pst[:, kc * 128:(kc + 1) * 128], hn[:, kc * 128:(kc + 1) * 128], idb)
            B.copy(ev2(), hnT[:, :, t * 128:(t + 1) * 128], pst.rearrange("p (k t) -> p k t", k=8))
        for cc in range(32):
            pb = bank(pbi % 2)
            pbi += 1
            for kc in range(8):
                B.mm(pb[:, 0:TB], Wup[:, kc, cc * 128:(cc + 1) * 128], hnT[:, kc, :], start=(kc == 0), stop=(kc == 7))
            r = rl[cc % 2]
            B.act(r, pb[:, 0:TB], AF.Relu)
            B.tt(("dve", "pool")[cc % 2], hid[:, cc, :], r, r, ALU.mult)
        for t in range(NTB):
            pw = L["ps"][:, (2 + 2 * (t % 2)) * 512:(4 + 2 * (t % 2)) * 512]
            for nb in range(2):
                for kc in range(32):
                    B.mm(pw[:, nb * 512:(nb + 1) * 512], hid[:, kc, t * 128:(t + 1) * 128], Wdn[:, kc, nb * 512:(nb + 1) * 512],
                         start=(kc == 0), stop=(kc == 31))
            for nb in range(2):
                B.act(junk[:, nb * 512:(nb + 1) * 512], pw[:, nb * 512:(nb + 1) * 512], AF.Square, accum_out=st4[:, 3 + nb:4 + nb])
            B.tt("dve", st4[:, 5:6], st4[:, 3:4], st4[:, 4:5], ALU.add)
            rstd_of(st4[:, 5:6], D, st4[:, 6:7], st4[:, 7:8])
            oo = ob[0]
            for nb in range(2):
                B.tt("dve", oo[:, nb * 512:(nb + 1) * 512], pw[:, nb * 512:(nb + 1) * 512], nfb[:, nb * 512:(nb + 1) * 512], ALU.mult)
            B.stt("dve", oo, oo, st4[:, 7:8], hk[:, t, :], ALU.mult, ALU.add)
            B.dma(out[t0 + t * 128: t0 + (t + 1) * 128, :], oo, q="pool")


def _alibi_slopes(n):
    def pow2(m):
        start = 2.0 ** (-8.0 / m)
        return [start ** (i + 1) for i in range(m)]
    if (n & (n - 1)) == 0:
        s = pow2(n)
    else:
        c = 2 ** int(np.floor(np.log2(n)))
        s = pow2(c) + pow2(2 * c)[0::2][: n - c]
    return np.array(s, dtype=np.float32)


def _constants():
    k = np.arange(128)[:, None]
    q = np.arange(128)[None, :]
    slopes = _alibi_slopes(NH).astype(np.float64)
    masks = np.zeros((128, 3, NH, 2, 128), np.float32)
    for g, (win, d) in enumerate(PATTERNS):
        dist_prev = 128 + q - k
        dist_cur = q - k
        for bl, dist in enumerate((dist_prev, dist_cur)):
            valid = (dist >= 0) & (dist <= 128)
            for h in range(NH):
                m = np.where(valid, np.exp(-slopes[h] * np.clip(dist, 0, 128) * d), 0.0)
                masks[:, g, h, bl, :] = m.astype(np.float32)
    triu = (k <= q).astype(np.float32)
    ls = (k > q).astype(np.float32)
    return dict(c_ident=np.eye(128, dtype=np.float32), c_masks=masks.reshape(128, -1),
                c_triu=triu, c_ls=ls)


_NC_CACHE = {}


def kernel(**inputs):
    x = np.asarray(inputs["x"], dtype=np.float32)
    nb = x.shape[0]
    consts = _constants()
    shared = {}
    for name in ("w_in", "norm_mix_pre_w", "b_gate", "conv_w", "conv_b", "ssm_norm_w", "w_att_proj", "w_ssm_proj",
                 "w_out", "norm_ffn_pre_w", "w_up", "w_down"):
        shared[name] = np.ascontiguousarray(np.asarray(inputs[name], dtype=np.float32)[0])
    for name in ("dt_bias", "a_log", "d_skip", "norm_mix_post_w", "norm_ffn_post_w"):
        shared[name] = np.ascontiguousarray(np.asarray(inputs[name], dtype=np.float32)[0][None, :])
    shared.update(consts)
    in_maps = []
    for c in range(8):
        b, half = c // 2, c % 2
        xl = np.zeros((LT, D), np.float32)
        if half == 1:
            xl[:] = x[b]
        else:
            xl[MAIN0:] = x[b, :MAIN0]
        m = dict(shared)
        m["x"] = xl
        m["c_pfx"] = np.full((128, 64), float(half), np.float32)
        in_maps.append(m)
    if "nc" not in _NC_CACHE:
        _NC_CACHE["nc"] = build_program()
    nc = _NC_CACHE["nc"]
    res = run_bass_kernel_spmd(nc, in_maps, core_ids=list(range(8)))
    outp = np.zeros((nb, SEQ, D), np.float32)
    for c in range(8):
        b, half = c // 2, c % 2
        outp[b, half * MAIN0:(half + 1) * MAIN0] = res.results[c]["out"]
    return outp
```

```python
import numpy as np
import concourse.bass as bass
import concourse.mybir as mybir
from concourse.bass_utils import run_bass_kernel_spmd

F32 = mybir.dt.float32
BF16 = mybir.dt.bfloat16
U8 = mybir.dt.uint8
AF = mybir.ActivationFunctionType
ALU = mybir.AluOpType
AX = mybir.AxisListType
ESZ = {F32: 4, BF16: 2, U8: 1}

D = 1024
SEQ = 8192
NH = 12
DH = 64
ATTW = 768
SSI = 2048
NSH = 32
NG = 8
NST = 128
CONVD = 4096
FFN = 4096
EPS = 1e-6
OFF_Q, OFF_K, OFF_V, OFF_Z, OFF_XBC, OFF_DT, OFF_G = 0, 768, 1536, 2304, 4352, 8448, 8480
INW = 10528
PATTERNS = ((128, 1), (512, 4), (2048, 16))
LT = 8192
MAIN0 = 4096


def region(ap):
    t = ap.tensor
    es = ESZ[ap.dtype]
    dims = ap.ap
    sp = str(ap.space)
    if sp in ("SB", "PSUM") or "SB" in sp or "PSUM" in sp:
        pstride = dims[0][0]
        off = ap.offset % pstride if pstride > 0 else ap.offset
        ext = sum(s * (c - 1) for s, c in dims[1:]) + 1
        if "PSUM" in sp:
            return ("ps", (off * es) // 2048 * 2048, ((off + ext) * es + 2047) // 2048 * 2048)
        return ("sb", off * es, (off + ext) * es)
    ext = sum(s * (c - 1) for s, c in dims) + 1
    return (t.name, ap.offset * es, (ap.offset + ext) * es)


class Prog:
    CE = ("pe", "act", "dve", "pool")
    ALL = ("pe", "act", "dve", "pool", "sp")
    K = 12

    def __init__(self):
        self.ops = {e: [] for e in self.ALL}
        self.acc = {}
        self.seen = {e: {f: -1 for f in self.CE} for e in self.ALL}
        self.seen_dma = {e: set() for e in self.ALL}
        self.ndma = {e: 0 for e in self.ALL}

    def _dep(self, eng, rec, deps, dma_deps):
        peng, pidx, pdma = rec[2], rec[3], rec[5]
        if pdma:
            dma_deps.add((peng, pidx))
        else:
            if peng == "pe" and eng == "pe":
                return
            deps[peng] = max(deps.get(peng, -1), pidx)

    def add(self, eng, fn, reads=(), writes=(), dma=False):
        idx = len(self.ops[eng])
        deps, dma_deps = {}, set()
        rr = [region(a) for a in reads]
        ww = [region(a) for a in writes]
        for key, lo, hi in rr:
            for rec in self.acc.get(key, ()):
                if rec[4] and rec[0] < hi and lo < rec[1]:
                    self._dep(eng, rec, deps, dma_deps)
        for key, lo, hi in ww:
            for rec in self.acc.get(key, ()):
                if rec[0] < hi and lo < rec[1]:
                    self._dep(eng, rec, deps, dma_deps)
        for key, lo, hi in ww:
            lst = self.acc.setdefault(key, [])
            lst[:] = [r for r in lst if not (lo <= r[0] and r[1] <= hi)]
            lst.append((lo, hi, eng, idx, True, dma))
        for key, lo, hi in rr:
            lst = self.acc.setdefault(key, [])
            lst[:] = [r for r in lst if not ((not r[4]) and r[2] == eng and r[5] == dma and (not dma)
                                             and lo <= r[0] and r[1] <= hi)]
            lst.append((lo, hi, eng, idx, False, dma))
        waits = []
        for f, j in deps.items():
            if j <= self.seen[eng][f]:
                continue
            self.seen[eng][f] = j
            self.ops[f][j]["signal"] = True
            waits.append(("ce", f, j))
        for (q, j) in dma_deps:
            if (q, j) in self.seen_dma[eng]:
                continue
            self.seen_dma[eng].add((q, j))
            waits.append(("dma", q, j))
        op = {"fn": fn, "waits": waits, "signal": False, "dma": dma}
        if dma:
            k = self.ndma[eng]
            self.ndma[eng] += 1
            op["dk"] = k
        self.ops[eng].append(op)
        return idx

    def emit(self, nc, sems):
        signum = {}
        for e in self.CE:
            c = 0
            arr = []
            for op in self.ops[e]:
                if op["signal"] and not op["dma"]:
                    c += 1
                arr.append(c)
            signum[e] = arr
        prog = self
        K = self.K

        def run(e, h):
            dma_hist = []
            for op in prog.ops[e]:
                for w in op["waits"]:
                    if w[0] == "ce":
                        h.wait_ge(sems[w[1]], signum[w[1]][w[2]])
                    else:
                        pk = prog.ops[w[1]][w[2]]["dk"]
                        h.wait_ge(sems[("dma", w[1], pk % K)], 16 * (pk // K + 1))
                if op["dma"]:
                    k = op["dk"]
                    if k >= K:
                        h.wait_ge(sems[("dma", e, k % K)], 16 * (k // K))
                    ins = op["fn"](h)
                    ins.then_inc(sems[("dma", e, k % K)], 16)
                else:
                    ins = op["fn"](h)
                    if op["signal"]:
                        ins.then_inc(sems[e], 1)

        with nc.Block() as block:
            @block.tensor
            def _(h):
                run("pe", h)

            @block.scalar
            def _(h):
                run("act", h)

            @block.vector
            def _(h):
                run("dve", h)

            @block.gpsimd
            def _(h):
                run("pool", h)

            @block.sync
            def _(h):
                run("sp", h)


class Arena:
    def __init__(self, ap_u8, nbytes):
        self.ap = ap_u8
        self.n = nbytes
        self.top = 0

    def alloc(self, shape, dtype):
        n = int(np.prod(shape)) * ESZ[dtype]
        self.top = (self.top + 63) // 64 * 64
        assert self.top + n <= self.n, f"SBUF arena overflow {self.top + n} > {self.n}"
        v = self.ap[:, self.top:self.top + n]
        self.top += n
        if dtype != U8:
            v = v.bitcast(dtype)
        if len(shape) == 2:
            v = v.rearrange("p (a b) -> p a b", a=shape[0])
        elif len(shape) == 3:
            v = v.rearrange("p (a b c) -> p a b c", a=shape[0], b=shape[1])
        elif len(shape) == 4:
            v = v.rearrange("p (a b c d) -> p a b c d", a=shape[0], b=shape[1], c=shape[2])
        return v

    def mark(self):
        return self.top

    def release(self, m):
        self.top = m


class Builder:
    def __init__(self, debug=None):
        self.debug = debug
        self.nc = bass.Bass("TRN2", target_bir_lowering=False)
        self.P = Prog()
        self.rr = 0

    def dma(self, out, in_, q="sp", **kw):
        self.P.add(q, lambda h: h.dma_start(out=out, in_=in_, **kw), [in_], [out], dma=True)

    def mm(self, out, lhsT, rhs, start=True, stop=True, **kw):
        self.P.add("pe", lambda h: h.matmul(out, lhsT, rhs, start=start, stop=stop, **kw), [lhsT, rhs], [out])

    def tr(self, out, in_, ident):
        self.P.add("pe", lambda h: h.transpose(out, in_, ident), [in_, ident], [out])

    def act(self, out, in_, func, bias=None, scale=None, accum_out=None):
        kw = {}
        rd = [in_]
        wr = [out]
        if bias is not None:
            kw["bias"] = bias
            if not isinstance(bias, (int, float)):
                rd.append(bias)
        if scale is not None:
            kw["scale"] = scale
            if not isinstance(scale, (int, float)):
                rd.append(scale)
        if accum_out is not None:
            kw["accum_out"] = accum_out
            wr.append(accum_out)
        self.P.add("act", lambda h: h.activation(out, in_, func, **kw), rd, wr)

    def tt(self, eng, out, in0, in1, op):
        self.P.add(eng, lambda h: h.tensor_tensor(out, in0, in1, op), [in0, in1], [out])

    def ts(self, eng, out, in0, s1, op0, s2=None, op1=None, accum_out=None):
        rd = [in0] + [s for s in (s1, s2) if s is not None and not isinstance(s, (int, float))]
        wr = [out] + ([accum_out] if accum_out is not None else [])
        kw = {}
        if op1 is not None:
            kw["op1"] = op1
        if accum_out is not None:
            kw["accum_out"] = accum_out
        self.P.add(eng, lambda h: h.tensor_scalar(out, in0, s1, s2, op0, **kw), rd, wr)

    def stt(self, eng, out, in0, scalar, in1, op0, op1):
        rd = [in0, in1] + ([scalar] if not isinstance(scalar, (int, float)) else [])
        self.P.add(eng, lambda h: h.scalar_tensor_tensor(out, in0, scalar, in1, op0, op1), rd, [out])

    def copy(self, eng, out, in_):
        if eng == "act":
            self.P.add("act", lambda h: h.copy(out, in_), [in_], [out])
        else:
            self.P.add(eng, lambda h: h.tensor_copy(out, in_), [in_], [out])

    def memset(self, eng, out, val):
        self.P.add(eng, lambda h: h.memset(out, val), [], [out])

    def recip(self, out, in_):
        self.P.add("dve", lambda h: h.reciprocal(out, in_), [in_], [out])

    def ev(self):
        self.rr += 1
        return ("act", "dve")[self.rr % 2]


def build_program(debug=False, stages=("p1", "p2", "p3", "c1", "c2"), nblk=None):
    B = Builder()
    nc = B.nc
    P = B.P

    def din(name, shape):
        return nc.dram_tensor(name, list(shape), F32, kind="ExternalInput").ap()

    x = din("x", [LT, D])
    w_in = din("w_in", [D, INW])
    nmw = din("norm_mix_pre_w", [D])
    b_gate = din("b_gate", [2 * D])
    conv_w = din("conv_w", [4, CONVD])
    conv_b = din("conv_b", [CONVD])
    dt_bias = din("dt_bias", [1, NSH])
    a_log = din("a_log", [1, NSH])
    d_skip = din("d_skip", [1, NSH])
    ssm_nw = din("ssm_norm_w", [SSI])
    w_att = din("w_att_proj", [ATTW, D])
    w_ssm = din("w_ssm_proj", [SSI, D])
    w_out = din("w_out", [D, D])
    npost = din("norm_mix_post_w", [1, D])
    nfpre = din("norm_ffn_pre_w", [D])
    w_up = din("w_up", [D, FFN])
    w_down = din("w_down", [FFN, D])
    nfpost = din("norm_ffn_post_w", [1, D])
    c_ident = din("c_ident", [128, 128])
    c_masks = din("c_masks", [128, 3 * NH * 2 * 128])
    c_triu = din("c_triu", [128, 128])
    c_ls = din("c_ls", [128, 128])
    c_pfx = din("c_pfx", [128, 64])
    out = nc.dram_tensor("out", [MAIN0, D], F32, kind="ExternalOutput").ap()

    kind = "ExternalOutput" if debug else "Internal"

    def dscr(name, shape, dt):
        if debug:
            return nc.dram_tensor(name, list(shape), dt, kind="ExternalOutput").ap()
        return nc.dram_tensor(name, list(shape), dt).ap()

    QT = dscr("s_qt", [NH, DH, MAIN0], BF16)
    KT = dscr("s_kt", [NH, DH, 6144], BF16)
    VV = dscr("s_v", [LT, ATTW], BF16)
    ZS = dscr("s_zs", [MAIN0, SSI], F32)
    ATT = dscr("s_att", [NH, DH, MAIN0], BF16)
    YT = dscr("s_yt", [16, 128, MAIN0], BF16)
    HS = dscr("s_h", [MAIN0, D], F32)

    NB = 207 * 1024
    import contextlib
    with contextlib.ExitStack() as es:
        at = es.enter_context(nc.sbuf_tensor("arena", [128, NB], U8))
        pt = es.enter_context(nc.psum_tensor("ps", [128, 4096], F32))
        A = Arena(at[:], NB)
        ps = pt[:]

        def bank(i):
            return ps[:, i * 512:(i + 1) * 512]

        idf = A.alloc([128], F32)
        idb = A.alloc([128], BF16)
        epst = A.alloc([1], F32)
        B.dma(idf, c_ident)
        B.copy("dve", idb, idf)
        B.memset("pool", epst, EPS)
        BASE = A.mark()

        ctr = [0]

        def ev2():
            ctr[0] += 1
            return ("act", "dve")[ctr[0] % 2]

        def load_weight(dst, src_rows, ncols, nk, scale_vec=None, stage=None, col0=0, engs=("dve", "pool", "act")):
            for kc in range(nk):
                for c0 in range(0, ncols, 2048):
                    c1 = min(ncols, c0 + 2048)
                    st = stage[(ctr[0]) % len(stage)]
                    ctr[0] += 1
                    B.dma(st[:, 0:c1 - c0], src_rows(kc)[:, col0 + c0:col0 + c1])
                    e = engs[ctr[0] % len(engs)]
                    if e == "act":
                        B.act(dst[:, kc, c0:c1], st[:, 0:c1 - c0], AF.Copy,
                              scale=(scale_vec[:, kc:kc + 1] if scale_vec is not None else 1.0))
                    elif scale_vec is not None:
                        B.ts(e, dst[:, kc, c0:c1], st[:, 0:c1 - c0], scale_vec[:, kc:kc + 1], ALU.mult)
                    else:
                        B.copy(e, dst[:, kc, c0:c1], st[:, 0:c1 - c0])

        def rstd_of(ssq, n, tmp, outp):
            B.act(tmp, ssq, AF.Ln, bias=epst[:, 0:1], scale=1.0 / n)
            B.act(outp, tmp, AF.Exp, scale=-0.5)

        def load_weight_cast(dst, src, nk, ncols, col0=0):
            for kc in range(nk):
                B.dma(dst[:, kc, :], src[kc * 128:(kc + 1) * 128, col0:col0 + ncols], q="pool")

        def make_xnT(tok0, ntile, xbufs, xn, junk, st4, xnT, psb, keep_x=None, wbc=None):
            for t in range(ntile):
                xt = xbufs[t % len(xbufs)] if keep_x is None else keep_x[:, t, :]
                B.dma(xt, x[tok0 + t * 128: tok0 + (t + 1) * 128, :])
                B.act(junk, xt, AF.Square, accum_out=st4[:, 0:1])
                rstd_of(st4[:, 0:1], D, st4[:, 1:2], st4[:, 2:3])
                if wbc is None:
                    B.ts("dve", xn, xt, st4[:, 2:3], ALU.mult)
                else:
                    B.stt("dve", xn, xt, st4[:, 2:3], wbc, ALU.mult, ALU.mult)
                pst = psb.bitcast(BF16)
                for kc in range(8):
                    B.tr(pst[:, kc * 128:(kc + 1) * 128], xn[:, kc * 128:(kc + 1) * 128], idb)
                B.copy(ev2(), xnT[:, :, t * 128:(t + 1) * 128], pst.rearrange("p (k t) -> p k t", k=8))

        A.release(BASE)
        nwb = A.alloc([1024], F32)
        B.dma(nwb, nmw.rearrange("(o n) -> o n", o=1).partition_broadcast(128))
        Wq = A.alloc([8, 2304], BF16)
        Wz = A.alloc([8, 2048], BF16)
        load_weight_cast(Wq, w_in, 8, 2304, col0=0)
        load_weight_cast(Wz, w_in, 8, 2048, col0=OFF_Z)
        xbufs = [A.alloc([1024], F32) for _ in range(2)]
        xn = A.alloc([1024], BF16)
        junk = A.alloc([1024], BF16)
        st4 = A.alloc([4], F32)
        xnT = A.alloc([8, 512], BF16)
        qk_sb = [A.alloc([512], BF16) for _ in range(3)]
        v_sb = [A.alloc([768], BF16) for _ in range(2)]
        z_sb = [A.alloc([2048], F32) for _ in range(2)]
        pbi = 0
        for blk in (range(12) if "p1" in stages else ()):
            tok0 = 2048 + blk * 512
            is_main = tok0 >= MAIN0
            make_xnT(tok0, 4, xbufs, xn, junk, st4, xnT, bank(7), wbc=nwb)
            for cc in range(12):
                if cc < 6 and not is_main:
                    continue
                pb = bank(pbi % 2)
                pbi += 1
                for kc in range(8):
                    B.mm(pb, Wq[:, kc, cc * 128:(cc + 1) * 128], xnT[:, kc, :], start=(kc == 0), stop=(kc == 7))
                sb = qk_sb[cc % 3]
                if cc < 6:
                    B.act(sb, pb, AF.Copy, scale=0.125)
                    for hh in range(2):
                        B.dma(QT[2 * cc + hh, :, tok0 - MAIN0: tok0 - MAIN0 + 512], sb[64 * hh:64 * hh + 64, :], q="pool")
                else:
                    B.copy("dve", sb, pb)
                    for hh in range(2):
                        B.dma(KT[2 * (cc - 6) + hh, :, tok0 - 2048: tok0 - 2048 + 512], sb[64 * hh:64 * hh + 64, :], q="pool")
            for t in range(4):
                vs = v_sb[t % 2]
                for nb in range(2):
                    pb = bank(pbi % 2)
                    pbi += 1
                    for kc in range(8):
                        B.mm(pb[:, 0:384], xnT[:, kc, t * 128:(t + 1) * 128], Wq[:, kc, OFF_V + nb * 384: OFF_V + (nb + 1) * 384],
                             start=(kc == 0), stop=(kc == 7))
                    B.copy(ev2(), vs[:, nb * 384:(nb + 1) * 384], pb[:, 0:384])
                B.dma(VV[tok0 + t * 128: tok0 + (t + 1) * 128, :], vs, q="pool")
                if is_main:
                    zs = z_sb[t % 2]
                    for nb in range(4):
                        pb = bank(2 + pbi % 2)
                        pbi += 1
                        for kc in range(8):
                            B.mm(pb, xnT[:, kc, t * 128:(t + 1) * 128], Wz[:, kc, nb * 512:(nb + 1) * 512],
                                 start=(kc == 0), stop=(kc == 7))
                        B.act(zs[:, nb * 512:(nb + 1) * 512], pb, AF.Silu)
                    B.dma(ZS[tok0 - MAIN0 + t * 128: tok0 - MAIN0 + (t + 1) * 128, :], zs, q="pool")

        A.release(BASE)
        maskf = A.alloc([3, NH, 2, 128], F32)
        B.dma(maskf.rearrange("p a b c d -> p (a b c d)"), c_masks)
        onesb = A.alloc([64], BF16)
        pfxf = A.alloc([64], F32)
        pfxb = A.alloc([64], BF16)
        B.memset("pool", onesb, 1.0)
        B.dma(pfxf, c_pfx)
        B.copy("dve", pfxb, pfxf)
        NT = (33, 36, 48)
        Vh = [[A.alloc([NT[g], 2, 64], BF16) for g in range(3)] for _ in range(2)]
        for bsel in range(2):
            for g, (win, d) in enumerate(PATTERNS):
                B.memset("pool", Vh[bsel][g][:, :, 1, :], 1.0)
                B.copy("dve", Vh[bsel][g][:, 0:d, 1, :], pfxb.unsqueeze(1).to_broadcast([128, d, 64]))
        KThs = [A.alloc([6144], BF16) for _ in range(2)]
        QThs = [A.alloc([4096], BF16) for _ in range(2)]
        ACC = A.alloc([2048], F32)
        rcp2 = A.alloc([2048], F32)
        Ebuf = [A.alloc([2, 2, 128], F32) for _ in range(4)]
        PTb = [A.alloc([2, 2, 128], BF16) for _ in range(4)]
        rcp = A.alloc([2048], F32)
        att_sb = A.alloc([2048], BF16)
        VBASE = (3968, 3584, 2048)
        heads = list(range(NH if nblk is None else nblk)) if "p2" in stages else []

        def load_head(h):
            for g, (win, d) in enumerate(PATTERNS):
                span = 128 * d
                for m in range(NT[g] // d):
                    base = VBASE[g] + span * m
                    src = VV[base: base + span, h * 64:(h + 1) * 64].rearrange("(i r) c -> i r c", r=d)
                    B.dma(Vh[h % 2][g][:, m * d:(m + 1) * d, 0, :], src)
            B.dma(KThs[h % 2][0:64, :], KT[h])
            B.dma(QThs[h % 2][0:64, :], QT[h])

        def stage_a(tk):
            h, g, d, span, pair, ui = tk["h"], tk["g"], tk["d"], tk["span"], tk["pair"], tk["ui"]
            KTh, QTh = KThs[h % 2], QThs[h % 2]
            st = bank(ui % 4).rearrange("p (u b q) -> p u b q", u=2, b=2)
            Eb, Pb = Ebuf[ui % 4], PTb[ui % 4]
            for u, (ms, mq, r) in enumerate(pair):
                qloc = VBASE[g] + span * mq + r - MAIN0
                for bl in range(2):
                    kloc = VBASE[g] + span * (mq - 1 + bl) + r - 2048
                    B.mm(st[:, u, bl, :], KTh[0:64, kloc: kloc + d * 127 + 1: d], QTh[0:64, qloc: qloc + d * 127 + 1: d])
            B.act(Eb, st, AF.Exp)
            B.tt(("pool", "pool", "dve")[ui % 3], Pb, Eb, maskf[:, g, h, :, :].unsqueeze(1).to_broadcast([128, 2, 2, 128]), ALU.mult)

        def stage_b(tk):
            h, g, d, span, pair, ui, first = tk["h"], tk["g"], tk["d"], tk["span"], tk["pair"], tk["ui"], tk["first"]
            hq = h % 4
            Pb = PTb[ui % 4]
            ol = bank(4 + ui % 4)[:, 0:256].rearrange("p (u q) -> p u q", u=2)
            for u, (ms, mq, r) in enumerate(pair):
                for bl in range(2):
                    tile_i = (mq - 1 + bl) * d + r
                    lhsT = Vh[h % 2][g][:, tile_i, :, :].rearrange("p a b -> p (a b)")
                    B.mm(ol[:, u, :], lhsT, Pb[:, u, bl, :], start=(bl == 0), stop=(bl == 1))
            (ms0, _, r0) = pair[0]
            p0 = ms0 * span + r0
            if d == 1:
                dst = ACC[:, p0: p0 + 256].rearrange("p (u q) -> p u q", u=2)
                if first:
                    B.copy("dve", dst, ol)
                else:
                    B.tt("dve", dst, ol, dst, ALU.add)
            else:
                for u in range(2):
                    av = ACC[:, p0 + u: p0 + u + d * 127 + 1: d]
                    if first:
                        B.copy("dve", av, ol[:, u, :])
                    else:
                        B.tt("dve", av, ol[:, u, :], av, ALU.add)

        ui = 0
        if heads:
            load_head(heads[0])
        for hi_, h in enumerate(heads):
            if hi_ + 1 < len(heads):
                load_head(heads[hi_ + 1])
            for sp in range(2):
                tasks = []
                for g, (win, d) in enumerate(PATTERNS):
                    span = 128 * d
                    units = []
                    for ms in range(2048 // span):
                        mq = (MAIN0 + 2048 * sp - VBASE[g]) // span + ms
                        for r in range(d):
                            units.append((ms, mq, r))
                    for u0 in range(0, len(units), 2):
                        tasks.append(dict(h=h, g=g, d=d, span=span, pair=units[u0:u0 + 2], ui=ui, first=(g == 0)))
                        ui += 1
                pend = []
                for tk in tasks:
                    stage_a(tk)
                    pend.append(tk)
                    if len(pend) > 3:
                        stage_b(pend.pop(0))
                for tk in pend:
                    stage_b(tk)
                B.recip(rcp[64:128, :], ACC[64:128, :])
                B.dma(rcp2[0:64, :], rcp[64:128, :])
                B.tt("pool", att_sb[0:64, :], ACC[0:64, :], rcp2[0:64, :], ALU.mult)
                B.dma(ATT[h, :, sp * 2048:(sp + 1) * 2048], att_sb[0:64, :], q="pool")
        return_ctx = dict(B=B, nc=nc, A=A, BASE=BASE, bank=bank, ps=ps, es=es, idb=idb, idf=idf, epst=epst,
                          ev2=ev2, load_weight=load_weight, load_weight_cast=load_weight_cast, rstd_of=rstd_of, make_xnT=make_xnT, ctr=ctr)
        loc = dict(locals())
        if "p3" in stages:
            build_ssm(loc)
        build_tail(loc)

        P.add("sp", lambda h: h.nop(), [out, QT, KT, VV, ZS, ATT, YT, HS], [])
        sems = {}
        for e in Prog.CE:
            sems[e] = es.enter_context(nc.semaphore("s_" + e))
        for q in ("sp", "pool"):
            for k in range(Prog.K):
                sems[("dma", q, k)] = es.enter_context(nc.semaphore(f"d_{q}{k}"))
        P.emit(nc, sems)
    return nc


def build_ssm(L):
    B, nc, A, bank, idb, idf, epst = L["B"], L["nc"], L["A"], L["bank"], L["idb"], L["idf"], L["epst"]
    ev2, load_weight, rstd_of, make_xnT, ctr = L["ev2"], L["load_weight"], L["rstd_of"], L["make_xnT"], L["ctr"]
    w_in, nmw, conv_w, conv_b, dt_bias, a_log, d_skip = L["w_in"], L["nmw"], L["conv_w"], L["conv_b"], L["dt_bias"], L["a_log"], L["d_skip"]
    c_triu, c_ls, c_pfx, ZS, YT, x = L["c_triu"], L["c_ls"], L["c_pfx"], L["ZS"], L["YT"], L["x"]
    A.release(L["BASE"])
    nwb = A.alloc([1024], F32)
    B.dma(nwb, nmw.rearrange("(o n) -> o n", o=1).partition_broadcast(128))
    cb = A.alloc([32], F32)
    B.dma(cb, conv_b.rearrange("(c p) -> p c", p=128), allow_slow_non_contiguous=True)
    Wx = A.alloc([8, 4096], BF16)
    Wdt = A.alloc([8, 32], BF16)
    DG = A.alloc([32, 4, 128], BF16)
    L["load_weight_cast"](Wx, w_in, 8, 4096, col0=OFF_XBC)
    L["load_weight_cast"](Wdt, w_in, 8, 32, col0=OFF_DT)
    mk = A.mark()
    cw = A.alloc([32, 4], F32)
    for k in range(4):
        B.dma(cw[:, :, k], conv_w[k].rearrange("(c p) -> p c", p=128), allow_slow_non_contiguous=True)
    for cc in range(32):
        for k in range(4):
            B.ts(("dve", "pool")[(cc * 4 + k) % 2], DG[:, cc, k, :], idf, cw[:, cc, k:k + 1], ALU.mult)
    A.release(mk)
    onesb = A.alloc([128], BF16)
    B.memset("pool", onesb, 1.0)
    Uf = A.alloc([128], F32)
    Ub = A.alloc([128], BF16)
    LSf = A.alloc([128], F32)
    B.dma(Uf, c_triu)
    B.dma(LSf, c_ls)
    B.copy("dve", Ub, Uf)
    dtb = A.alloc([32], F32)
    aneg = A.alloc([32], F32)
    dsk = A.alloc([32], F32)
    pfx1 = A.alloc([64], F32)
    B.dma(dtb, dt_bias.partition_broadcast(128))
    B.dma(aneg, a_log.partition_broadcast(128))
    B.dma(dsk, d_skip.partition_broadcast(128))
    B.dma(pfx1, c_pfx)
    B.act(aneg, aneg, AF.Exp)
    B.ts("dve", aneg, aneg, -1.0, ALU.mult)
    H = A.alloc([2048], F32)
    Hbf = A.alloc([2048], BF16)
    B.memset("dve", H, 0.0)
    XB = A.alloc([32, 131], BF16)
    B.memset("pool", XB, 0.0)
    xbufs = [A.alloc([1024], F32)]
    st4 = A.alloc([4], F32)
    xnTs = [A.alloc([8, 128], BF16) for _ in range(2)]
    XS = A.alloc([2048], BF16)
    junkA, xn = XS[:, 0:1024], XS[:, 1024:2048]
    XCx = A.alloc([16, 128], BF16)
    LAh = A.alloc([32], BF16)
    LAl = A.alloc([32], BF16)
    XCbc = [A.alloc([16, 128], BF16) for _ in range(2)]
    XD = [A.alloc([2048], BF16) for _ in range(2)]
    XDD = [A.alloc([2048], BF16) for _ in range(2)]
    XSD = [A.alloc([2048], BF16)] * 2
    Btok = [A.alloc([1024], BF16) for _ in range(2)]
    SM = [A.alloc([12, 32], F32) for _ in range(2)]
    CBm = A.alloc([8, 128], F32)
    RH = [A.alloc([4, 128], F32)]
    Eb = [A.alloc([4, 128], F32) for _ in range(2)]
    MT = [A.alloc([4, 128], BF16) for _ in range(2)]
    T1 = [A.alloc([256], F32) for _ in range(2)]
    zt = A.alloc([2048], F32)
    YZ = zt
    ssq = A.alloc([8], F32)
    rs8 = A.alloc([2, 8], F32)
    YN = A.alloc([2048], BF16)
    YTs = A.alloc([16, 128], BF16)
    cnt = {"pa": 0, "g": 0}
    chunks = list(range(64)) if L["nblk"] is None else [0, 1, 31, 32, 33]

    def names(p):
        sm = SM[p]
        return [sm[:, i, :] for i in range(10)]

    def stage_pro(c):
        make_xnT(c * 128, 1, xbufs, xn, junkA, st4, xnTs[c % 2], bank(2), wbc=nwb)
        yield

    def stage_a(c):
        p = c % 2
        tok0 = c * 128
        main = c >= 32
        xnT = xnTs[p]
        DTr, DT, LA, ACSs, EA, DST, DSTATE, CD, DTDS, TMP = names(p)
        b3 = bank(3)
        for kc in range(8):
            B.mm(b3[:, 0:32], xnT[:, kc, :], Wdt[:, kc, :], start=(kc == 0), stop=(kc == 7))
        B.tt("dve", DTr, b3[:, 0:32], dtb, ALU.add)
        B.act(TMP, DTr, AF.Exp)
        B.act(DT, TMP, AF.Ln, bias=1.0)
        B.tt("dve", LA, DT, aneg, ALU.mult)
        B.copy("dve", LAh, LA)
        B.tt("dve", LAl, LA, LAh, ALU.subtract)
        yield
        nc4 = 8 if c >= 31 else 6
        for c4 in range(nc4):
            pb = bank(cnt["pa"] % 2)
            cnt["pa"] += 1
            for j in range(4):
                cc = c4 * 4 + j
                for kc in range(8):
                    B.mm(pb[:, j * 128:(j + 1) * 128], Wx[:, kc, cc * 128:(cc + 1) * 128], xnT[:, kc, :], start=(kc == 0), stop=(kc == 7))
            B.copy(ev2(), XB[:, c4 * 4:(c4 + 1) * 4, 3:131], pb.rearrange("p (j t) -> p j t", j=4))
            yield
        b3 = bank(3)
        B.mm(b3[:, 64:96], Ub, LAh, start=True, stop=False)
        B.mm(b3[:, 64:96], Ub, LAl, start=False, stop=True)
        B.mm(b3[:, 96:128], onesb, LAh, start=True, stop=False)
        B.mm(b3[:, 96:128], onesb, LAl, start=False, stop=True)
        B.copy("dve", ACSs, b3[:, 64:96])
        B.tt("dve", DST, b3[:, 96:128], ACSs, ALU.subtract)
        B.act(DSTATE, DST, AF.Exp)
        B.act(CD, b3[:, 96:128], AF.Exp)
        if main:
            B.act(EA, ACSs, AF.Exp)
        B.tt("dve", DTDS, DT, DSTATE, ALU.mult)
        yield
        for c4 in range(8 if main else 6):
            pb = bank(cnt["pa"] % 2)
            cnt["pa"] += 1
            for j in range(4):
                cc = c4 * 4 + j
                for k in range(4):
                    B.mm(pb[:, j * 128:(j + 1) * 128], DG[:, cc, k, :], XB[:, cc, k:k + 128], start=(k == 0), stop=(k == 3))
            for j in range(4):
                cc = c4 * 4 + j
                dst = XCx[:, cc, :] if cc < 16 else XCbc[p][:, cc - 16, :]
                B.act(dst, pb[:, j * 128:(j + 1) * 128], AF.Silu, bias=cb[:, cc:cc + 1])
        yield
        B.copy("pool", XB[:, :, 0:3], XB[:, :, 128:131])
        for half in range(2):
            pst = bank(2).bitcast(BF16)
            for j in range(8):
                B.tr(pst[:, j * 128:(j + 1) * 128], XCx[:, half * 8 + j, :], idb)
            B.copy(ev2(), XS[:, half * 1024:(half + 1) * 1024], pst)
            yield
        pst = bank(2).bitcast(BF16)
        for j in range(8):
            B.tr(pst[:, j * 128:(j + 1) * 128], XCbc[p][:, j, :], idb)
        B.copy(ev2(), Btok[p], pst)
        yield
        xs3 = XS.rearrange("p (h e) -> p h e", h=32)
        B.tt("pool", XDD[p].rearrange("p (h e) -> p h e", h=32), xs3, DTDS.unsqueeze(2).to_broadcast([128, 32, 64]), ALU.mult)
        if main:
            B.tt("dve", XD[p].rearrange("p (h e) -> p h e", h=32), xs3, DT.unsqueeze(2).to_broadcast([128, 32, 64]), ALU.mult)
            B.tt("pool", XSD[p].rearrange("p (h e) -> p h e", h=32), xs3, dsk.unsqueeze(2).to_broadcast([128, 32, 64]), ALU.mult)
        yield

    def stage_b(c):
        p = c % 2
        tok0 = c * 128
        main = c >= 32
        DTr, DT, LA, ACSs, EA, DST, DSTATE, CD, DTDS, TMP = names(p)
        xc = XCbc[p]
        if main:
            B.dma(zt, ZS[tok0 - MAIN0: tok0 - MAIN0 + 128, :])
            for half in range(2):
                pb = bank(5 - half)
                for j in range(4):
                    g = half * 4 + j
                    B.mm(pb[:, j * 128:(j + 1) * 128], xc[:, g, :], xc[:, 8 + g, :])
                B.tt("dve", CBm[:, half * 4:(half + 1) * 4, :], pb.rearrange("p (j t) -> p j t", j=4),
                     Uf.unsqueeze(1).to_broadcast([128, 4, 128]), ALU.mult)
                yield

            def g1(g, gi):
                rh, eb, mt = RH[0], Eb[gi % 2], MT[gi % 2]
                dp = bank(5)
                for j in range(4):
                    hh = 4 * g + j
                    B.ts("dve", rh[:, j, :], Uf, LA[:, hh:hh + 1], ALU.mult)
                B.mm(dp, LSf, rh.rearrange("p j t -> p (j t)"))
                B.act(eb, dp.rearrange("p (j t) -> p j t", j=4), AF.Exp)
                B.tt("pool", mt, eb, CBm[:, g, :].unsqueeze(1).to_broadcast([128, 4, 128]), ALU.mult)

            def g2(g, gi):
                mt, t1 = MT[gi % 2], T1[gi % 2]
                yb = bank(6 + gi % 2)
                B.mm(yb[:, 0:256], idb, XSD[p][:, g * 256:(g + 1) * 256], start=True, stop=False)
                for j in range(4):
                    hh = 4 * g + j
                    B.mm(yb[:, j * 64:(j + 1) * 64], mt[:, j, :], XD[p][:, hh * 64:(hh + 1) * 64], start=False, stop=(j == 3))
                B.mm(yb[:, 256:512], xc[:, 8 + g, :], Hbf[:, g * 256:(g + 1) * 256])
                B.tt("dve", t1.rearrange("p (h e) -> p h e", h=4), yb[:, 256:512].rearrange("p (h e) -> p h e", h=4),
                     EA[:, 4 * g:4 * g + 4].unsqueeze(2).to_broadcast([128, 4, 64]), ALU.mult)
                B.tt("dve", t1, yb[:, 0:256], t1, ALU.add)
                B.tt("pool", YZ[:, g * 256:(g + 1) * 256], t1, zt[:, g * 256:(g + 1) * 256], ALU.mult)
                B.act(YN[:, 0:256], YZ[:, g * 256:(g + 1) * 256], AF.Square, accum_out=ssq[:, g:g + 1])

            gi0 = cnt["g"]
            cnt["g"] += 8
            g1(0, gi0)
            for g in range(8):
                if g + 1 < 8:
                    g1(g + 1, gi0 + g + 1)
                g2(g, gi0 + g)
                yield
        if c < 63:
            for gp in range(4):
                pb = bank(4)
                for j in range(2):
                    g = gp * 2 + j
                    B.mm(pb[:, j * 256:(j + 1) * 256], Btok[p][:, g * 128:(g + 1) * 128], XDD[p][:, g * 256:(g + 1) * 256])
                hv = H[:, gp * 512:(gp + 1) * 512]
                B.tt("pool", hv.rearrange("p (h e) -> p h e", h=8), hv.rearrange("p (h e) -> p h e", h=8),
                     CD[:, gp * 8:(gp + 1) * 8].unsqueeze(2).to_broadcast([128, 8, 64]), ALU.mult)
                B.tt("dve", hv, pb, hv, ALU.add)
                yield
            if c == 31:
                B.ts("dve", H, H, pfx1[:, 0:1], ALU.mult)
            if c >= 31:
                B.copy("pool", Hbf, H)
        if main:
            rstd_of(ssq, 256, rs8[:, 0, :], rs8[:, 1, :])
            B.tt("dve", YN.rearrange("p (g e) -> p g e", g=8), YZ.rearrange("p (g e) -> p g e", g=8),
                 rs8[:, 1, :].unsqueeze(2).to_broadcast([128, 8, 256]), ALU.mult)
            for half in range(2):
                pst = bank(4).bitcast(BF16)
                for j in range(8):
                    cc = half * 8 + j
                    B.tr(pst[:, j * 128:(j + 1) * 128], YN[:, cc * 128:(cc + 1) * 128], idb)
                B.copy(ev2(), YTs[:, half * 8:(half + 1) * 8, :], pst.rearrange("p (j t) -> p j t", j=8))
                yield
            B.dma(YT[:, :, tok0 - MAIN0: tok0 - MAIN0 + 128].rearrange("c p t -> p c t"), YTs, q="pool")
        yield

    def interleave(gens):
        gens = [g for g in gens if g is not None]
        while gens:
            for g in list(gens):
                try:
                    next(g)
                except StopIteration:
                    gens.remove(g)

    n = len(chunks)
    interleave([stage_pro(chunks[0])])
    interleave([stage_a(chunks[0]), stage_pro(chunks[1]) if n > 1 else None])
    for i, c in enumerate(chunks):
        interleave([stage_pro(chunks[i + 2]) if i + 2 < n else None,
                    stage_a(chunks[i + 1]) if i + 1 < n else None,
                    stage_b(c)])


def build_tail(L):
    B, nc, A, bank, idb, idf, epst = L["B"], L["nc"], L["A"], L["bank"], L["idb"], L["idf"], L["epst"]
    ev2, load_weight, rstd_of, make_xnT, ctr = L["ev2"], L["load_weight"], L["rstd_of"], L["make_xnT"], L["ctr"]
    w_in, nmw, b_gate, ssm_nw, w_att, w_ssm, w_out = L["w_in"], L["nmw"], L["b_gate"], L["ssm_nw"], L["w_att"], L["w_ssm"], L["w_out"]
    npost, nfpre, w_up, w_down, nfpost = L["npost"], L["nfpre"], L["w_up"], L["w_down"], L["nfpost"]
    ATT, YT, HS, x, out = L["ATT"], L["YT"], L["HS"], L["x"], L["out"]
    A.release(L["BASE"])
    nwb = A.alloc([1024], F32)
    B.dma(nwb, nmw.rearrange("(o n) -> o n", o=1).partition_broadcast(128))
    snw = A.alloc([16], F32)
    B.dma(snw, ssm_nw.rearrange("(k p) -> p k", p=128), allow_slow_non_contiguous=True)
    bg = A.alloc([16], F32)
    B.dma(bg, b_gate.rearrange("(k p) -> p k", p=128), allow_slow_non_contiguous=True)
    Wg = A.alloc([8, 2048], BF16)
    Watt = A.alloc([12, 1024], BF16)
    Wssm = A.alloc([16, 1024], BF16)
    Wout = A.alloc([8, 1024], BF16)
    L["load_weight_cast"](Wg, w_in, 8, 2048, col0=OFF_G)
    B.dma(Watt[0:64, :, :], w_att.rearrange("(h d) n -> d h n", d=64), q="pool")
    L["load_weight_cast"](Wout, w_out, 8, 1024)
    mk = A.mark()
    stg = [A.alloc([2048], F32) for _ in range(2)]
    load_weight(Wssm, lambda kc: w_ssm[kc * 128:(kc + 1) * 128, :], 1024, 16, snw, stg)
    A.release(mk)
    npb = A.alloc([1024], F32)
    B.dma(npb, npost.partition_broadcast(128))
    TB = 512
    NTB = TB // 128
    xk = A.alloc([NTB, 1024], F32)
    xn = A.alloc([1024], BF16)
    junk = A.alloc([1024], BF16)
    st4 = A.alloc([8], F32)
    xnT = A.alloc([8, TB], BF16)
    G = A.alloc([16, TB], BF16)
    aT = A.alloc([12, TB], BF16)
    yT = A.alloc([16, TB], BF16)
    m1 = [A.alloc([TB], F32) for _ in range(2)]
    m2 = [A.alloc([TB], F32) for _ in range(2)]
    mT = A.alloc([8, TB], BF16)
    hb = [A.alloc([1024], F32) for _ in range(1)]
    pbi = 0
    for blk in (range(MAIN0 // TB if L["nblk"] is None else L["nblk"]) if "c1" in L["stages"] else ()):
        t0 = blk * TB
        B.dma(aT[0:64, :, :], ATT[:, :, t0:t0 + TB].rearrange("h d t -> d h t"))
        B.dma(yT, YT[:, :, t0:t0 + TB].rearrange("c p t -> p c t"))
        make_xnT(MAIN0 + t0, NTB, None, xn, junk, st4, xnT, bank(7), keep_x=xk, wbc=nwb)
        for cc in range(16):
            pb = bank(pbi % 2)
            pbi += 1
            for kc in range(8):
                B.mm(pb[:, 0:TB], Wg[:, kc, cc * 128:(cc + 1) * 128], xnT[:, kc, :], start=(kc == 0), stop=(kc == 7))
            B.act(G[:, cc, :], pb[:, 0:TB], AF.Sigmoid, bias=bg[:, cc:cc + 1])
        for cc in range(8):
            pa = bank(2 + pbi % 2)
            pbs = bank(4 + pbi % 2)
            pbi += 1
            for hd in range(12):
                B.mm(pa[:, 0:TB], Watt[0:64, hd, cc * 128:(cc + 1) * 128], aT[0:64, hd, :], start=(hd == 0), stop=(hd == 11))
            for kc in range(16):
                B.mm(pbs[:, 0:TB], Wssm[:, kc, cc * 128:(cc + 1) * 128], yT[:, kc, :], start=(kc == 0), stop=(kc == 15))
            a1, a2 = m1[cc % 2], m2[cc % 2]
            B.tt("dve", a1, pa[:, 0:TB], G[:, cc, :], ALU.mult)
            B.tt("dve", a2, pbs[:, 0:TB], G[:, 8 + cc, :], ALU.mult)
            B.tt("pool", mT[:, cc, :], a1, a2, ALU.add)
        for t in range(TB // 128):
            pw = L["ps"][:, 6 * 512:8 * 512]
            for nb in range(2):
                for kc in range(8):
                    B.mm(pw[:, nb * 512:(nb + 1) * 512], mT[:, kc, t * 128:(t + 1) * 128], Wout[:, kc, nb * 512:(nb + 1) * 512],
                         start=(kc == 0), stop=(kc == 7))
            for nb in range(2):
                B.act(junk[:, nb * 512:(nb + 1) * 512], pw[:, nb * 512:(nb + 1) * 512], AF.Square, accum_out=st4[:, 3 + nb:4 + nb])
            B.tt("dve", st4[:, 5:6], st4[:, 3:4], st4[:, 4:5], ALU.add)
            rstd_of(st4[:, 5:6], D, st4[:, 6:7], st4[:, 7:8])
            hh = hb[0]
            for nb in range(2):
                B.tt("dve", hh[:, nb * 512:(nb + 1) * 512], pw[:, nb * 512:(nb + 1) * 512], npb[:, nb * 512:(nb + 1) * 512], ALU.mult)
            B.stt("dve", hh, hh, st4[:, 7:8], xk[:, t, :], ALU.mult, ALU.add)
            B.dma(HS[t0 + t * 128: t0 + (t + 1) * 128, :], hh, q="pool")
    A.release(L["BASE"])
    fwb = A.alloc([1024], F32)
    B.dma(fwb, nfpre.rearrange("(o n) -> o n", o=1).partition_broadcast(128))
    Wup = A.alloc([8, 4096], BF16)
    Wdn = A.alloc([32, 1024], BF16)
    L["load_weight_cast"](Wup, w_up, 8, 4096)
    L["load_weight_cast"](Wdn, w_down, 32, 1024)
    nfb = A.alloc([1024], F32)
    B.dma(nfb, nfpost.partition_broadcast(128))
    hk = A.alloc([NTB, 1024], F32)
    hn = A.alloc([1024], BF16)
    junk = A.alloc([1024], BF16)
    st4 = A.alloc([8], F32)
    hnT = A.alloc([8, TB], BF16)
    hid = A.alloc([32, TB], BF16)
    rl = [A.alloc([TB], F32) for _ in range(2)]
    ob = [A.alloc([1024], F32) for _ in range(1)]
    for blk in (range(MAIN0 // TB if L["nblk"] is None else L["nblk"]) if "c2" in L["stages"] else ()):
        t0 = blk * TB
        for t in range(NTB):
            ht = hk[:, t, :]
            B.dma(ht, HS[t0 + t * 128: t0 + (t + 1) * 128, :])
            B.act(junk, ht, AF.Square, accum_out=st4[:, 0:1])
            rstd_of(st4[:, 0:1], D, st4[:, 1:2], st4[:, 2:3])
            B.stt("dve", hn, ht, st4[:, 2:3], fwb, ALU.mult, ALU.mult)
            pst = bank(7).bitcast(BF16)
            for kc in range(8):
                B.tr(pst[:, kc * 128:(kc + 1) * 128], hn[:, kc * 128:(kc + 1) * 128], idb)
            B.copy(ev2(), hnT[:, :, t * 128:(t + 1) * 128], pst.rearrange("p (k t) -> p k t", k=8))
        for cc in range(32):
            pb = bank(pbi % 2)
            pbi += 1
            for kc in range(8):
                B.mm(pb[:, 0:TB], Wup[:, kc, cc * 128:(cc + 1) * 128], hnT[:, kc, :], start=(kc == 0), stop=(kc == 7))
            r = rl[cc % 2]
            B.act(r, pb[:, 0:TB], AF.Relu)
            B.tt(("dve", "pool")[cc % 2], hid[:, cc, :], r, r, ALU.mult)
        for t in range(NTB):
            pw = L["ps"][:, (2 + 2 * (t % 2)) * 512:(4 + 2 * (t % 2)) * 512]
            for nb in range(2):
                for kc in range(32):
                    B.mm(pw[:, nb * 512:(nb + 1) * 512], hid[:, kc, t * 128:(t + 1) * 128], Wdn[:, kc, nb * 512:(nb + 1) * 512],
                         start=(kc == 0), stop=(kc == 31))
            for nb in range(2):
                B.act(junk[:, nb * 512:(nb + 1) * 512], pw[:, nb * 512:(nb + 1) * 512], AF.Square, accum_out=st4[:, 3 + nb:4 + nb])
            B.tt("dve", st4[:, 5:6], st4[:, 3:4], st4[:, 4:5], ALU.add)
            rstd_of(st4[:, 5:6], D, st4[:, 6:7], st4[:, 7:8])
            oo = ob[0]
            for nb in range(2):
                B.tt("dve", oo[:, nb * 512:(nb + 1) * 512], pw[:, nb * 512:(nb + 1) * 512], nfb[:, nb * 512:(nb + 1) * 512], ALU.mult)
            B.stt("dve", oo, oo, st4[:, 7:8], hk[:, t, :], ALU.mult, ALU.add)
            B.dma(out[t0 + t * 128: t0 + (t + 1) * 128, :], oo, q="pool")


def _alibi_slopes(n):
    def pow2(m):
        start = 2.0 ** (-8.0 / m)
        return [start ** (i + 1) for i in range(m)]
    if (n & (n - 1)) == 0:
        s = pow2(n)
    else:
        c = 2 ** int(np.floor(np.log2(n)))
        s = pow2(c) + pow2(2 * c)[0::2][: n - c]
    return np.array(s, dtype=np.float32)


def _constants():
    k = np.arange(128)[:, None]
    q = np.arange(128)[None, :]
    slopes = _alibi_slopes(NH).astype(np.float64)
    masks = np.zeros((128, 3, NH, 2, 128), np.float32)
    for g, (win, d) in enumerate(PATTERNS):
        dist_prev = 128 + q - k
        dist_cur = q - k
        for bl, dist in enumerate((dist_prev, dist_cur)):
            valid = (dist >= 0) & (dist <= 128)
            for h in range(NH):
                m = np.where(valid, np.exp(-slopes[h] * np.clip(dist, 0, 128) * d), 0.0)
                masks[:, g, h, bl, :] = m.astype(np.float32)
    triu = (k <= q).astype(np.float32)
    ls = (k > q).astype(np.float32)
    return dict(c_ident=np.eye(128, dtype=np.float32), c_masks=masks.reshape(128, -1),
                c_triu=triu, c_ls=ls)


_NC_CACHE = {}


def kernel(**inputs):
    x = np.asarray(inputs["x"], dtype=np.float32)
    nb = x.shape[0]
    consts = _constants()
    shared = {}
    for name in ("w_in", "norm_mix_pre_w", "b_gate", "conv_w", "conv_b", "ssm_norm_w", "w_att_proj", "w_ssm_proj",
                 "w_out", "norm_ffn_pre_w", "w_up", "w_down"):
        shared[name] = np.ascontiguousarray(np.asarray(inputs[name], dtype=np.float32)[0])
    for name in ("dt_bias", "a_log", "d_skip", "norm_mix_post_w", "norm_ffn_post_w"):
        shared[name] = np.ascontiguousarray(np.asarray(inputs[name], dtype=np.float32)[0][None, :])
    shared.update(consts)
    in_maps = []
    for c in range(8):
        b, half = c // 2, c % 2
        xl = np.zeros((LT, D), np.float32)
        if half == 1:
            xl[:] = x[b]
        else:
            xl[MAIN0:] = x[b, :MAIN0]
        m = dict(shared)
        m["x"] = xl
        m["c_pfx"] = np.full((128, 64), float(half), np.float32)
        in_maps.append(m)
    if "nc" not in _NC_CACHE:
        _NC_CACHE["nc"] = build_program()
    nc = _NC_CACHE["nc"]
    res = run_bass_kernel_spmd(nc, in_maps, core_ids=list(range(8)))
    outp = np.zeros((nb, SEQ, D), np.float32)
    for c in range(8):
        b, half = c // 2, c % 2
        outp[b, half * MAIN0:(half + 1) * MAIN0] = res.results[c]["out"]
    return outp
```

```python
import numpy as np
import concourse.bass as bass
import concourse.mybir as mybir
from concourse.bass_utils import run_bass_kernel_spmd

F32 = mybir.dt.float32
BF16 = mybir.dt.bfloat16
U8 = mybir.dt.uint8
AF = mybir.ActivationFunctionType
ALU = mybir.AluOpType
AX = mybir.AxisListType
ESZ = {F32: 4, BF16: 2, U8: 1}

D = 1024
SEQ = 8192
NH = 12
DH = 64
ATTW = 768
SSI = 2048
NSH = 32
NG = 8
NST = 128
CONVD = 4096
FFN = 4096
EPS = 1e-6
OFF_Q, OFF_K, OFF_V, OFF_Z, OFF_XBC, OFF_DT, OFF_G = 0, 768, 1536, 2304, 4352, 8448, 8480
INW = 10528
PATTERNS = ((128, 1), (512, 4), (2048, 16))
LT = 8192
MAIN0 = 4096


def region(ap):
    t = ap.tensor
    es = ESZ[ap.dtype]
    dims = ap.ap
    sp = str(ap.space)
    if sp in ("SB", "PSUM") or "SB" in sp or "PSUM" in sp:
        pstride = dims[0][0]
        off = ap.offset % pstride if pstride > 0 else ap.offset
        ext = sum(s * (c - 1) for s, c in dims[1:]) + 1
        if "PSUM" in sp:
            return ("ps", (off * es) // 2048 * 2048, ((off + ext) * es + 2047) // 2048 * 2048)
        return ("sb", off * es, (off + ext) * es)
    ext = sum(s * (c - 1) for s, c in dims) + 1
    return (t.name, ap.offset * es, (ap.offset + ext) * es)


class Prog:
    CE = ("pe", "act", "dve", "pool")
    ALL = ("pe", "act", "dve", "pool", "sp")
    K = 12

    def __init__(self):
        self.ops = {e: [] for e in self.ALL}
        self.acc = {}
        self.seen = {e: {f: -1 for f in self.CE} for e in self.ALL}
        self.seen_dma = {e: set() for e in self.ALL}
        self.ndma = {e: 0 for e in self.ALL}

    def _dep(self, eng, rec, deps, dma_deps):
        peng, pidx, pdma = rec[2], rec[3], rec[5]
        if pdma:
            dma_deps.add((peng, pidx))
        else:
            if peng == "pe" and eng == "pe":
                return
            deps[peng] = max(deps.get(peng, -1), pidx)

    def add(self, eng, fn, reads=(), writes=(), dma=False):
        idx = len(self.ops[eng])
        deps, dma_deps = {}, set()
        rr = [region(a) for a in reads]
        ww = [region(a) for a in writes]
        for key, lo, hi in rr:
            for rec in self.acc.get(key, ()):
                if rec[4] and rec[0] < hi and lo < rec[1]:
                    self._dep(eng, rec, deps, dma_deps)
        for key, lo, hi in ww:
            for rec in self.acc.get(key, ()):
                if rec[0] < hi and lo < rec[1]:
                    self._dep(eng, rec, deps, dma_deps)
        for key, lo, hi in ww:
            lst = self.acc.setdefault(key, [])
            lst[:] = [r for r in lst if not (lo <= r[0] and r[1] <= hi)]
            lst.append((lo, hi, eng, idx, True, dma))
        for key, lo, hi in rr:
            lst = self.acc.setdefault(key, [])
            lst[:] = [r for r in lst if not ((not r[4]) and r[2] == eng and r[5] == dma and (not dma)
                                             and lo <= r[0] and r[1] <= hi)]
            lst.append((lo, hi, eng, idx, False, dma))
        waits = []
        for f, j in deps.items():
            if j <= self.seen[eng][f]:
                continue
            self.seen[eng][f] = j
            self.ops[f][j]["signal"] = True
            waits.append(("ce", f, j))
        for (q, j) in dma_deps:
            if (q, j) in self.seen_dma[eng]:
                continue
            self.seen_dma[eng].add((q, j))
            waits.append(("dma", q, j))
        op = {"fn": fn, "waits": waits, "signal": False, "dma": dma}
        if dma:
            k = self.ndma[eng]
            self.ndma[eng] += 1
            op["dk"] = k
        self.ops[eng].append(op)
        return idx

    def emit(self, nc, sems):
        signum = {}
        for e in self.CE:
            c = 0
            arr = []
            for op in self.ops[e]:
                if op["signal"] and not op["dma"]:
                    c += 1
                arr.append(c)
            signum[e] = arr
        prog = self
        K = self.K

        def run(e, h):
            dma_hist = []
            for op in prog.ops[e]:
                for w in op["waits"]:
                    if w[0] == "ce":
                        h.wait_ge(sems[w[1]], signum[w[1]][w[2]])
                    else:
                        pk = prog.ops[w[1]][w[2]]["dk"]
                        h.wait_ge(sems[("dma", w[1], pk % K)], 16 * (pk // K + 1))
                if op["dma"]:
                    k = op["dk"]
                    if k >= K:
                        h.wait_ge(sems[("dma", e, k % K)], 16 * (k // K))
                    ins = op["fn"](h)
                    ins.then_inc(sems[("dma", e, k % K)], 16)
                else:
                    ins = op["fn"](h)
                    if op["signal"]:
                        ins.then_inc(sems[e], 1)

        with nc.Block() as block:
            @block.tensor
            def _(h):
                run("pe", h)

            @block.scalar
            def _(h):
                run("act", h)

            @block.vector
            def _(h):
                run("dve", h)

            @block.gpsimd
            def _(h):
                run("pool", h)

            @block.sync
            def _(h):
                run("sp", h)


class Arena:
    def __init__(self, ap_u8, nbytes):
        self.ap = ap_u8
        self.n = nbytes
        self.top = 0

    def alloc(self, shape, dtype):
        n = int(np.prod(shape)) * ESZ[dtype]
        self.top = (self.top + 63) // 64 * 64
        assert self.top + n <= self.n, f"SBUF arena overflow {self.top + n} > {self.n}"
        v = self.ap[:, self.top:self.top + n]
        self.top += n
        if dtype != U8:
            v = v.bitcast(dtype)
        if len(shape) == 2:
            v = v.rearrange("p (a b) -> p a b", a=shape[0])
        elif len(shape) == 3:
            v = v.rearrange("p (a b c) -> p a b c", a=shape[0], b=shape[1])
        elif len(shape) == 4:
            v = v.rearrange("p (a b c d) -> p a b c d", a=shape[0], b=shape[1], c=shape[2])
        return v

    def mark(self):
        return self.top

    def release(self, m):
        self.top = m


class Builder:
    def __init__(self, debug=None):
        self.debug = debug
        self.nc = bass.Bass("TRN2", target_bir_lowering=False)
        self.P = Prog()
        self.rr = 0

    def dma(self, out, in_, q="sp", **kw):
        self.P.add(q, lambda h: h.dma_start(out=out, in_=in_, **kw), [in_], [out], dma=True)

    def mm(self, out, lhsT, rhs, start=True, stop=True, **kw):
        self.P.add("pe", lambda h: h.matmul(out, lhsT, rhs, start=start, stop=stop, **kw), [lhsT, rhs], [out])

    def tr(self, out, in_, ident):
        self.P.add("pe", lambda h: h.transpose(out, in_, ident), [in_, ident], [out])

    def act(self, out, in_, func, bias=None, scale=None, accum_out=None):
        kw = {}
        rd = [in_]
        wr = [out]
        if bias is not None:
            kw["bias"] = bias
            if not isinstance(bias, (int, float)):
                rd.append(bias)
        if scale is not None:
            kw["scale"] = scale
            if not isinstance(scale, (int, float)):
                rd.append(scale)
        if accum_out is not None:
            kw["accum_out"] = accum_out
            wr.append(accum_out)
        self.P.add("act", lambda h: h.activation(out, in_, func, **kw), rd, wr)

    def tt(self, eng, out, in0, in1, op):
        self.P.add(eng, lambda h: h.tensor_tensor(out, in0, in1, op), [in0, in1], [out])

    def ts(self, eng, out, in0, s1, op0, s2=None, op1=None, accum_out=None):
        rd = [in0] + [s for s in (s1, s2) if s is not None and not isinstance(s, (int, float))]
        wr = [out] + ([accum_out] if accum_out is not None else [])
        kw = {}
        if op1 is not None:
            kw["op1"] = op1
        if accum_out is not None:
            kw["accum_out"] = accum_out
        self.P.add(eng, lambda h: h.tensor_scalar(out, in0, s1, s2, op0, **kw), rd, wr)

    def stt(self, eng, out, in0, scalar, in1, op0, op1):
        rd = [in0, in1] + ([scalar] if not isinstance(scalar, (int, float)) else [])
        self.P.add(eng, lambda h: h.scalar_tensor_tensor(out, in0, scalar, in1, op0, op1), rd, [out])

    def copy(self, eng, out, in_):
        if eng == "act":
            self.P.add("act", lambda h: h.copy(out, in_), [in_], [out])
        else:
            self.P.add(eng, lambda h: h.tensor_copy(out, in_), [in_], [out])

    def memset(self, eng, out, val):
        self.P.add(eng, lambda h: h.memset(out, val), [], [out])

    def recip(self, out, in_):
        self.P.add("dve", lambda h: h.reciprocal(out, in_), [in_], [out])

    def ev(self):
        self.rr += 1
        return ("act", "dve")[self.rr % 2]


def build_program(debug=False, stages=("p1", "p2", "p3", "c1", "c2"), nblk=None):
    B = Builder()
    nc = B.nc
    P = B.P

    def din(name, shape):
        return nc.dram_tensor(name, list(shape), F32, kind="ExternalInput").ap()

    x = din("x", [LT, D])
    w_in = din("w_in", [D, INW])
    nmw = din("norm_mix_pre_w", [D])
    b_gate = din("b_gate", [2 * D])
    conv_w = din("conv_w", [4, CONVD])
    conv_b = din("conv_b", [CONVD])
    dt_bias = din("dt_bias", [1, NSH])
    a_log = din("a_log", [1, NSH])
    d_skip = din("d_skip", [1, NSH])
    ssm_nw = din("ssm_norm_w", [SSI])
    w_att = din("w_att_proj", [ATTW, D])
    w_ssm = din("w_ssm_proj", [SSI, D])
    w_out = din("w_out", [D, D])
    npost = din("norm_mix_post_w", [1, D])
    nfpre = din("norm_ffn_pre_w", [D])
    w_up = din("w_up", [D, FFN])
    w_down = din("w_down", [FFN, D])
    nfpost = din("norm_ffn_post_w", [1, D])
    c_ident = din("c_ident", [128, 128])
    c_masks = din("c_masks", [128, 3 * NH * 2 * 128])
    c_triu = din("c_triu", [128, 128])
    c_ls = din("c_ls", [128, 128])
    c_pfx = din("c_pfx", [128, 64])
    out = nc.dram_tensor("out", [MAIN0, D], F32, kind="ExternalOutput").ap()

    kind = "ExternalOutput" if debug else "Internal"

    def dscr(name, shape, dt):
        if debug:
            return nc.dram_tensor(name, list(shape), dt, kind="ExternalOutput").ap()
        return nc.dram_tensor(name, list(shape), dt).ap()

    QT = dscr("s_qt", [NH, DH, MAIN0], BF16)
    KT = dscr("s_kt", [NH, DH, 6144], BF16)
    VV = dscr("s_v", [LT, ATTW], BF16)
    ZS = dscr("s_zs", [MAIN0, SSI], F32)
    ATT = dscr("s_att", [NH, DH, MAIN0], BF16)
    YT = dscr("s_yt", [16, 128, MAIN0], BF16)
    HS = dscr("s_h", [MAIN0, D], F32)

    NB = 207 * 1024
    import contextlib
    with contextlib.ExitStack() as es:
        at = es.enter_context(nc.sbuf_tensor("arena", [128, NB], U8))
        pt = es.enter_context(nc.psum_tensor("ps", [128, 4096], F32))
        A = Arena(at[:], NB)
        ps = pt[:]

        def bank(i):
            return ps[:, i * 512:(i + 1) * 512]

        idf = A.alloc([128], F32)
        idb = A.alloc([128], BF16)
        epst = A.alloc([1], F32)
        B.dma(idf, c_ident)
        B.copy("dve", idb, idf)
        B.memset("pool", epst, EPS)
        BASE = A.mark()

        ctr = [0]

        def ev2():
            ctr[0] += 1
            return ("act", "dve")[ctr[0] % 2]

        def load_weight(dst, src_rows, ncols, nk, scale_vec=None, stage=None, col0=0, engs=("dve", "pool", "act")):
            for kc in range(nk):
                for c0 in range(0, ncols, 2048):
                    c1 = min(ncols, c0 + 2048)
                    st = stage[(ctr[0]) % len(stage)]
                    ctr[0] += 1
                    B.dma(st[:, 0:c1 - c0], src_rows(kc)[:, col0 + c0:col0 + c1])
                    e = engs[ctr[0] % len(engs)]
                    if e == "act":
                        B.act(dst[:, kc, c0:c1], st[:, 0:c1 - c0], AF.Copy,
                              scale=(scale_vec[:, kc:kc + 1] if scale_vec is not None else 1.0))
                    elif scale_vec is not None:
                        B.ts(e, dst[:, kc, c0:c1], st[:, 0:c1 - c0], scale_vec[:, kc:kc + 1], ALU.mult)
                    else:
                        B.copy(e, dst[:, kc, c0:c1], st[:, 0:c1 - c0])

        def rstd_of(ssq, n, tmp, outp):
            B.act(tmp, ssq, AF.Ln, bias=epst[:, 0:1], scale=1.0 / n)
            B.act(outp, tmp, AF.Exp, scale=-0.5)

        def load_weight_cast(dst, src, nk, ncols, col0=0):
            for kc in range(nk):
                B.dma(dst[:, kc, :], src[kc * 128:(kc + 1) * 128, col0:col0 + ncols], q="pool")

        def make_xnT(tok0, ntile, xbufs, xn, junk, st4, xnT, psb, keep_x=None, wbc=None):
            for t in range(ntile):
                xt = xbufs[t % len(xbufs)] if keep_x is None else keep_x[:, t, :]
                B.dma(xt, x[tok0 + t * 128: tok0 + (t + 1) * 128, :])
                B.act(junk, xt, AF.Square, accum_out=st4[:, 0:1])
                rstd_of(st4[:, 0:1], D, st4[:, 1:2], st4[:, 2:3])
                if wbc is None:
                    B.ts("dve", xn, xt, st4[:, 2:3], ALU.mult)
                else:
                    B.stt("dve", xn, xt, st4[:, 2:3], wbc, ALU.mult, ALU.mult)
                pst = psb.bitcast(BF16)
                for kc in range(8):
                    B.tr(pst[:, kc * 128:(kc + 1) * 128], xn[:, kc * 128:(kc + 1) * 128], idb)
                B.copy(ev2(), xnT[:, :, t * 128:(t + 1) * 128], pst.rearrange("p (k t) -> p k t", k=8))

        A.release(BASE)
        nwb = A.alloc([1024], F32)
        B.dma(nwb, nmw.rearrange("(o n) -> o n", o=1).partition_broadcast(128))
        Wq = A.alloc([8, 2304], BF16)
        Wz = A.alloc([8, 2048], BF16)
        load_weight_cast(Wq, w_in, 8, 2304, col0=0)
        load_weight_cast(Wz, w_in, 8, 2048, col0=OFF_Z)
        xbufs = [A.alloc([1024], F32) for _ in range(2)]
        xn = A.alloc([1024], BF16)
        junk = A.alloc([1024], BF16)
        st4 = A.alloc([4], F32)
        xnT = A.alloc([8, 512], BF16)
        qk_sb = [A.alloc([512], BF16) for _ in range(3)]
        v_sb = [A.alloc([768], BF16) for _ in range(2)]
        z_sb = [A.alloc([2048], F32) for _ in range(2)]
        pbi = 0
        for blk in (range(12) if "p1" in stages else ()):
            tok0 = 2048 + blk * 512
            is_main = tok0 >= MAIN0
            make_xnT(tok0, 4, xbufs, xn, junk, st4, xnT, bank(7), wbc=nwb)
            for cc in range(12):
                if cc < 6 and not is_main:
                    continue
                pb = bank(pbi % 2)
                pbi += 1
                for kc in range(8):
                    B.mm(pb, Wq[:, kc, cc * 128:(cc + 1) * 128], xnT[:, kc, :], start=(kc == 0), stop=(kc == 7))
                sb = qk_sb[cc % 3]
                if cc < 6:
                    B.act(sb, pb, AF.Copy, scale=0.125)
                    for hh in range(2):
                        B.dma(QT[2 * cc + hh, :, tok0 - MAIN0: tok0 - MAIN0 + 512], sb[64 * hh:64 * hh + 64, :], q="pool")
                else:
                    B.copy("dve", sb, pb)
                    for hh in range(2):
                        B.dma(KT[2 * (cc - 6) + hh, :, tok0 - 2048: tok0 - 2048 + 512], sb[64 * hh:64 * hh + 64, :], q="pool")
            for t in range(4):
                vs = v_sb[t % 2]
                for nb in range(2):
                    pb = bank(pbi % 2)
                    pbi += 1
                    for kc in range(8):
                        B.mm(pb[:, 0:384], xnT[:, kc, t * 128:(t + 1) * 128], Wq[:, kc, OFF_V + nb * 384: OFF_V + (nb + 1) * 384],
                             start=(kc == 0), stop=(kc == 7))
                    B.copy(ev2(), vs[:, nb * 384:(nb + 1) * 384], pb[:, 0:384])
                B.dma(VV[tok0 + t * 128: tok0 + (t + 1) * 128, :], vs, q="pool")
                if is_main:
                    zs = z_sb[t % 2]
                    for nb in range(4):
                        pb = bank(2 + pbi % 2)
                        pbi += 1
                        for kc in range(8):
                            B.mm(pb, xnT[:, kc, t * 128:(t + 1) * 128], Wz[:, kc, nb * 512:(nb + 1) * 512],
                                 start=(kc == 0), stop=(kc == 7))
                        B.act(zs[:, nb * 512:(nb + 1) * 512], pb, AF.Silu)
                    B.dma(ZS[tok0 - MAIN0 + t * 128: tok0 - MAIN0 + (t + 1) * 128, :], zs, q="pool")

        A.release(BASE)
        maskf = A.alloc([3, NH, 2, 128], F32)
        B.dma(maskf.rearrange("p a b c d -> p (a b c d)"), c_masks)
        onesb = A.alloc([64], BF16)
        pfxf = A.alloc([64], F32)
        pfxb = A.alloc([64], BF16)
        B.memset("pool", onesb, 1.0)
        B.dma(pfxf, c_pfx)
        B.copy("dve", pfxb, pfxf)
        NT = (33, 36, 48)
        Vh = [[A.alloc([NT[g], 2, 64], BF16) for g in range(3)] for _ in range(2)]
        for bsel in range(2):
            for g, (win, d) in enumerate(PATTERNS):
                B.memset("pool", Vh[bsel][g][:, :, 1, :], 1.0)
                B.copy("dve", Vh[bsel][g][:, 0:d, 1, :], pfxb.unsqueeze(1).to_broadcast([128, d, 64]))
        KThs = [A.alloc([6144], BF16) for _ in range(2)]
        QThs = [A.alloc([4096], BF16) for _ in range(2)]
        ACC = A.alloc([2048], F32)
        rcp2 = A.alloc([2048], F32)
        Ebuf = [A.alloc([2, 2, 128], F32) for _ in range(4)]
        PTb = [A.alloc([2, 2, 128], BF16) for _ in range(4)]
        rcp = A.alloc([2048], F32)
        att_sb = A.alloc([2048], BF16)
        VBASE = (3968, 3584, 2048)
        heads = list(range(NH if nblk is None else nblk)) if "p2" in stages else []

        def load_head(h):
            for g, (win, d) in enumerate(PATTERNS):
                span = 128 * d
                for m in range(NT[g] // d):
                    base = VBASE[g] + span * m
                    src = VV[base: base + span, h * 64:(h + 1) * 64].rearrange("(i r) c -> i r c", r=d)
                    B.dma(Vh[h % 2][g][:, m * d:(m + 1) * d, 0, :], src)
            B.dma(KThs[h % 2][0:64, :], KT[h])
            B.dma(QThs[h % 2][0:64, :], QT[h])

        def stage_a(tk):
            h, g, d, span, pair, ui = tk["h"], tk["g"], tk["d"], tk["span"], tk["pair"], tk["ui"]
            KTh, QTh = KThs[h % 2], QThs[h % 2]
            st = bank(ui % 4).rearrange("p (u b q) -> p u b q", u=2, b=2)
            Eb, Pb = Ebuf[ui % 4], PTb[ui % 4]
            for u, (ms, mq, r) in enumerate(pair):
                qloc = VBASE[g] + span * mq + r - MAIN0
                for bl in range(2):
                    kloc = VBASE[g] + span * (mq - 1 + bl) + r - 2048
                    B.mm(st[:, u, bl, :], KTh[0:64, kloc: kloc + d * 127 + 1: d], QTh[0:64, qloc: qloc + d * 127 + 1: d])
            B.act(Eb, st, AF.Exp)
            B.tt(("pool", "pool", "dve")[ui % 3], Pb, Eb, maskf[:, g, h, :, :].unsqueeze(1).to_broadcast([128, 2, 2, 128]), ALU.mult)

        def stage_b(tk):
            h, g, d, span, pair, ui, first = tk["h"], tk["g"], tk["d"], tk["span"], tk["pair"], tk["ui"], tk["first"]
            hq = h % 4
            Pb = PTb[ui % 4]
            ol = bank(4 + ui % 4)[:, 0:256].rearrange("p (u q) -> p u q", u=2)
            for u, (ms, mq, r) in enumerate(pair):
                for bl in range(2):
                    tile_i = (mq - 1 + bl) * d + r
                    lhsT = Vh[h % 2][g][:, tile_i, :, :].rearrange("p a b -> p (a b)")
                    B.mm(ol[:, u, :], lhsT, Pb[:, u, bl, :], start=(bl == 0), stop=(bl == 1))
            (ms0, _, r0) = pair[0]
            p0 = ms0 * span + r0
            if d == 1:
                dst = ACC[:, p0: p0 + 256].rearrange("p (u q) -> p u q", u=2)
                if first:
                    B.copy("dve", dst, ol)
                else:
                    B.tt("dve", dst, ol, dst, ALU.add)
            else:
                for u in range(2):
                    av = ACC[:, p0 + u: p0 + u + d * 127 + 1: d]
                    if first:
                        B.copy("dve", av, ol[:, u, :])
                    else:
                        B.tt("dve", av, ol[:, u, :], av, ALU.add)

        ui = 0
        if heads:
            load_head(heads[0])
        for hi_, h in enumerate(heads):
            if hi_ + 1 < len(heads):
                load_head(heads[hi_ + 1])
            for sp in range(2):
                tasks = []
                for g, (win, d) in enumerate(PATTERNS):
                    span = 128 * d
                    units = []
                    for ms in range(2048 // span):
                        mq = (MAIN0 + 2048 * sp - VBASE[g]) // span + ms
                        for r in range(d):
                            units.append((ms, mq, r))
                    for u0 in range(0, len(units), 2):
                        tasks.append(dict(h=h, g=g, d=d, span=span, pair=units[u0:u0 + 2], ui=ui, first=(g == 0)))
                        ui += 1
                pend = []
                for tk in tasks:
                    stage_a(tk)
                    pend.append(tk)
                    if len(pend) > 3:
                        stage_b(pend.pop(0))
                for tk in pend:
                    stage_b(tk)
                B.recip(rcp[64:128, :], ACC[64:128, :])
                B.dma(rcp2[0:64, :], rcp[64:128, :])
                B.tt("pool", att_sb[0:64, :], ACC[0:64, :], rcp2[0:64, :], ALU.mult)
                B.dma(ATT[h, :, sp * 2048:(sp + 1) * 2048], att_sb[0:64, :], q="pool")
        return_ctx = dict(B=B, nc=nc, A=A, BASE=BASE, bank=bank, ps=ps, es=es, idb=idb, idf=idf, epst=epst,
                          ev2=ev2, load_weight=load_weight, load_weight_cast=load_weight_cast, rstd_of=rstd_of, make_xnT=make_xnT, ctr=ctr)
        loc = dict(locals())
        if "p3" in stages:
            build_ssm(loc)
        build_tail(loc)

        P.add("sp", lambda h: h.nop(), [out, QT, KT, VV, ZS, ATT, YT, HS], [])
        sems = {}
        for e in Prog.CE:
            sems[e] = es.enter_context(nc.semaphore("s_" + e))
        for q in ("sp", "pool"):
            for k in range(Prog.K):
                sems[("dma", q, k)] = es.enter_context(nc.semaphore(f"d_{q}{k}"))
        P.emit(nc, sems)
    return nc


def build_ssm(L):
    B, nc, A, bank, idb, idf, epst = L["B"], L["nc"], L["A"], L["bank"], L["idb"], L["idf"], L["epst"]
    ev2, load_weight, rstd_of, make_xnT, ctr = L["ev2"], L["load_weight"], L["rstd_of"], L["make_xnT"], L["ctr"]
    w_in, nmw, conv_w, conv_b, dt_bias, a_log, d_skip = L["w_in"], L["nmw"], L["conv_w"], L["conv_b"], L["dt_bias"], L["a_log"], L["d_skip"]
    c_triu, c_ls, c_pfx, ZS, YT, x = L["c_triu"], L["c_ls"], L["c_pfx"], L["ZS"], L["YT"], L["x"]
    A.release(L["BASE"])
    nwb = A.alloc([1024], F32)
    B.dma(nwb, nmw.rearrange("(o n) -> o n", o=1).partition_broadcast(128))
    cb = A.alloc([32], F32)
    B.dma(cb, conv_b.rearrange("(c p) -> p c", p=128), allow_slow_non_contiguous=True)
    Wx = A.alloc([8, 4096], BF16)
    Wdt = A.alloc([8, 32], BF16)
    DG = A.alloc([32, 4, 128], BF16)
    L["load_weight_cast"](Wx, w_in, 8, 4096, col0=OFF_XBC)
    L["load_weight_cast"](Wdt, w_in, 8, 32, col0=OFF_DT)
    mk = A.mark()
    cw = A.alloc([32, 4], F32)
    for k in range(4):
        B.dma(cw[:, :, k], conv_w[k].rearrange("(c p) -> p c", p=128), allow_slow_non_contiguous=True)
    for k in range(4):
        B.tt(("dve", "pool")[k % 2], DG[:, :, k, :], idf.unsqueeze(1).to_broadcast([128, 32, 128]),
             cw[:, :, k].unsqueeze(2).to_broadcast([128, 32, 128]), ALU.mult)
    A.release(mk)
    onesb = A.alloc([128], BF16)
    B.memset("pool", onesb, 1.0)
    Uf = A.alloc([128], F32)
    Ub = A.alloc([128], BF16)
    LSf = A.alloc([128], F32)
    B.dma(Uf, c_triu)
    B.dma(LSf, c_ls)
    B.copy("dve", Ub, Uf)
    dtb = A.alloc([32], F32)
    aneg = A.alloc([32], F32)
    dsk = A.alloc([32], F32)
    pfx1 = A.alloc([64], F32)
    B.dma(dtb, dt_bias.partition_broadcast(128))
    B.dma(aneg, a_log.partition_broadcast(128))
    B.dma(dsk, d_skip.partition_broadcast(128))
    B.dma(pfx1, c_pfx)
    B.act(aneg, aneg, AF.Exp)
    B.ts("dve", aneg, aneg, -1.0, ALU.mult)
    H = A.alloc([2048], F32)
    Hbf = A.alloc([2048], BF16)
    B.memset("dve", H, 0.0)
    XB = A.alloc([32, 131], BF16)
    B.memset("pool", XB, 0.0)
    xbufs = [A.alloc([1024], F32)]
    st4 = A.alloc([4], F32)
    xnTs = [A.alloc([8, 128], BF16) for _ in range(2)]
    XS = A.alloc([2048], BF16)
    junkA, xn = XS[:, 0:1024], XS[:, 1024:2048]
    XCx = A.alloc([16, 128], BF16)
    LAh = A.alloc([32], BF16)
    LAl = A.alloc([32], BF16)
    XCbc = [A.alloc([16, 128], BF16) for _ in range(2)]
    XD = [A.alloc([2048], BF16) for _ in range(2)]
    XDD = [A.alloc([2048], BF16) for _ in range(2)]
    XSD = [A.alloc([2048], BF16)] * 2
    Btok = [A.alloc([1024], BF16) for _ in range(2)]
    SM = [A.alloc([12, 32], F32) for _ in range(2)]
    CBm = A.alloc([8, 128], F32)
    RH = [A.alloc([4, 128], F32)]
    Eb = [A.alloc([4, 128], F32) for _ in range(2)]
    MT = [A.alloc([4, 128], BF16) for _ in range(2)]
    T1 = [A.alloc([256], F32) for _ in range(2)]
    zt = A.alloc([2048], F32)
    YZ = zt
    ssq = A.alloc([8], F32)
    rs8 = A.alloc([2, 8], F32)
    YN = A.alloc([2048], BF16)
    YTs = A.alloc([16, 128], BF16)
    cnt = {"pa": 0, "g": 0}
    chunks = list(range(64)) if L["nblk"] is None else [0, 1, 31, 32, 33]

    def names(p):
        sm = SM[p]
        return [sm[:, i, :] for i in range(10)]

    def stage_pro(c):
        make_xnT(c * 128, 1, xbufs, xn, junkA, st4, xnTs[c % 2], bank(2), wbc=nwb)
        yield

    def stage_a(c):
        p = c % 2
        tok0 = c * 128
        main = c >= 32
        xnT = xnTs[p]
        DTr, DT, LA, ACSs, EA, DST, DSTATE, CD, DTDS, TMP = names(p)
        b3 = bank(3)
        for kc in range(8):
            B.mm(b3[:, 0:32], xnT[:, kc, :], Wdt[:, kc, :], start=(kc == 0), stop=(kc == 7))
        B.tt("dve", DTr, b3[:, 0:32], dtb, ALU.add)
        B.act(TMP, DTr, AF.Exp)
        B.act(DT, TMP, AF.Ln, bias=1.0)
        B.tt("dve", LA, DT, aneg, ALU.mult)
        B.copy("dve", LAh, LA)
        B.tt("dve", LAl, LA, LAh, ALU.subtract)
        yield
        nc4 = 8 if c >= 31 else 6
        for c4 in range(nc4):
            pb = bank(cnt["pa"] % 2)
            cnt["pa"] += 1
            for j in range(4):
                cc = c4 * 4 + j
                for kc in range(8):
                    B.mm(pb[:, j * 128:(j + 1) * 128], Wx[:, kc, cc * 128:(cc + 1) * 128], xnT[:, kc, :], start=(kc == 0), stop=(kc == 7))
            B.copy(ev2(), XB[:, c4 * 4:(c4 + 1) * 4, 3:131], pb.rearrange("p (j t) -> p j t", j=4))
            yield
        b3 = bank(3)
        B.mm(b3[:, 64:96], Ub, LAh, start=True, stop=False)
        B.mm(b3[:, 64:96], Ub, LAl, start=False, stop=True)
        B.mm(b3[:, 96:128], onesb, LAh, start=True, stop=False)
        B.mm(b3[:, 96:128], onesb, LAl, start=False, stop=True)
        B.copy("dve", ACSs, b3[:, 64:96])
        B.tt("dve", DST, b3[:, 96:128], ACSs, ALU.subtract)
        B.act(DSTATE, DST, AF.Exp)
        B.act(CD, b3[:, 96:128], AF.Exp)
        if main:
            B.act(EA, ACSs, AF.Exp)
        B.tt("dve", DTDS, DT, DSTATE, ALU.mult)
        yield
        for c4 in range(8 if main else 6):
            pb = bank(cnt["pa"] % 2)
            cnt["pa"] += 1
            for j in range(4):
                cc = c4 * 4 + j
                for k in range(4):
                    B.mm(pb[:, j * 128:(j + 1) * 128], DG[:, cc, k, :], XB[:, cc, k:k + 128], start=(k == 0), stop=(k == 3))
            for j in range(4):
                cc = c4 * 4 + j
                dst = XCx[:, cc, :] if cc < 16 else XCbc[p][:, cc - 16, :]
                B.act(dst, pb[:, j * 128:(j + 1) * 128], AF.Silu, bias=cb[:, cc:cc + 1])
        yield
        B.copy("pool", XB[:, :, 0:3], XB[:, :, 128:131])
        for half in range(2):
            pst = bank(2).bitcast(BF16)
            for j in range(8):
                B.tr(pst[:, j * 128:(j + 1) * 128], XCx[:, half * 8 + j, :], idb)
            B.copy(ev2(), XS[:, half * 1024:(half + 1) * 1024], pst)
            yield
        pst = bank(2).bitcast(BF16)
        for j in range(8):
            B.tr(pst[:, j * 128:(j + 1) * 128], XCbc[p][:, j, :], idb)
        B.copy(ev2(), Btok[p], pst)
        yield
        xs3 = XS.rearrange("p (h e) -> p h e", h=32)
        B.tt("pool", XDD[p].rearrange("p (h e) -> p h e", h=32), xs3, DTDS.unsqueeze(2).to_broadcast([128, 32, 64]), ALU.mult)
        if main:
            B.tt("dve", XD[p].rearrange("p (h e) -> p h e", h=32), xs3, DT.unsqueeze(2).to_broadcast([128, 32, 64]), ALU.mult)
            B.tt("pool", XSD[p].rearrange("p (h e) -> p h e", h=32), xs3, dsk.unsqueeze(2).to_broadcast([128, 32, 64]), ALU.mult)
        yield

    def stage_b(c):
        p = c % 2
        tok0 = c * 128
        main = c >= 32
        DTr, DT, LA, ACSs, EA, DST, DSTATE, CD, DTDS, TMP = names(p)
        xc = XCbc[p]
        if main:
            B.dma(zt, ZS[tok0 - MAIN0: tok0 - MAIN0 + 128, :])
            for half in range(2):
                pb = bank(5 - half)
                for j in range(4):
                    g = half * 4 + j
                    B.mm(pb[:, j * 128:(j + 1) * 128], xc[:, g, :], xc[:, 8 + g, :])
                B.tt("dve", CBm[:, half * 4:(half + 1) * 4, :], pb.rearrange("p (j t) -> p j t", j=4),
                     Uf.unsqueeze(1).to_broadcast([128, 4, 128]), ALU.mult)
                yield

            def g1(g, gi):
                rh, eb, mt = RH[0], Eb[gi % 2], MT[gi % 2]
                dp = bank(5)
                for j in range(4):
                    hh = 4 * g + j
                    B.ts("dve", rh[:, j, :], Uf, LA[:, hh:hh + 1], ALU.mult)
                B.mm(dp, LSf, rh.rearrange("p j t -> p (j t)"))
                B.act(eb, dp.rearrange("p (j t) -> p j t", j=4), AF.Exp)
                B.tt("pool", mt, eb, CBm[:, g, :].unsqueeze(1).to_broadcast([128, 4, 128]), ALU.mult)

            def g2(g, gi):
                mt, t1 = MT[gi % 2], T1[gi % 2]
                yb = bank(6 + gi % 2)
                B.mm(yb[:, 0:256], idb, XSD[p][:, g * 256:(g + 1) * 256], start=True, stop=False)
                for j in range(4):
                    hh = 4 * g + j
                    B.mm(yb[:, j * 64:(j + 1) * 64], mt[:, j, :], XD[p][:, hh * 64:(hh + 1) * 64], start=False, stop=(j == 3))
                B.mm(yb[:, 256:512], xc[:, 8 + g, :], Hbf[:, g * 256:(g + 1) * 256])
                B.tt("dve", t1.rearrange("p (h e) -> p h e", h=4), yb[:, 256:512].rearrange("p (h e) -> p h e", h=4),
                     EA[:, 4 * g:4 * g + 4].unsqueeze(2).to_broadcast([128, 4, 64]), ALU.mult)
                B.tt("dve", t1, yb[:, 0:256], t1, ALU.add)
                B.tt("pool", YZ[:, g * 256:(g + 1) * 256], t1, zt[:, g * 256:(g + 1) * 256], ALU.mult)
                B.act(YN[:, 0:256], YZ[:, g * 256:(g + 1) * 256], AF.Square, accum_out=ssq[:, g:g + 1])

            gi0 = cnt["g"]
            cnt["g"] += 8
            g1(0, gi0)
            for g in range(8):
                if g + 1 < 8:
                    g1(g + 1, gi0 + g + 1)
                g2(g, gi0 + g)
                yield
        if c < 63:
            for gp in range(4):
                pb = bank(4)
                for j in range(2):
                    g = gp * 2 + j
                    B.mm(pb[:, j * 256:(j + 1) * 256], Btok[p][:, g * 128:(g + 1) * 128], XDD[p][:, g * 256:(g + 1) * 256])
                hv = H[:, gp * 512:(gp + 1) * 512]
                B.tt("pool", hv.rearrange("p (h e) -> p h e", h=8), hv.rearrange("p (h e) -> p h e", h=8),
                     CD[:, gp * 8:(gp + 1) * 8].unsqueeze(2).to_broadcast([128, 8, 64]), ALU.mult)
                B.tt("dve", hv, pb, hv, ALU.add)
                yield
            if c == 31:
                B.ts("dve", H, H, pfx1[:, 0:1], ALU.mult)
            if c >= 31:
                B.copy("pool", Hbf, H)
        if main:
            rstd_of(ssq, 256, rs8[:, 0, :], rs8[:, 1, :])
            B.tt("dve", YN.rearrange("p (g e) -> p g e", g=8), YZ.rearrange("p (g e) -> p g e", g=8),
                 rs8[:, 1, :].unsqueeze(2).to_broadcast([128, 8, 256]), ALU.mult)
            for half in range(2):
                pst = bank(4).bitcast(BF16)
                for j in range(8):
                    cc = half * 8 + j
                    B.tr(pst[:, j * 128:(j + 1) * 128], YN[:, cc * 128:(cc + 1) * 128], idb)
                B.copy(ev2(), YTs[:, half * 8:(half + 1) * 8, :], pst.rearrange("p (j t) -> p j t", j=8))
                yield
            B.dma(YT[:, :, tok0 - MAIN0: tok0 - MAIN0 + 128].rearrange("c p t -> p c t"), YTs, q="pool")
        yield

    def interleave(gens):
        gens = [g for g in gens if g is not None]
        while gens:
            for g in list(gens):
                try:
                    next(g)
                except StopIteration:
                    gens.remove(g)

    n = len(chunks)
    interleave([stage_pro(chunks[0])])
    interleave([stage_a(chunks[0]), stage_pro(chunks[1]) if n > 1 else None])
    for i, c in enumerate(chunks):
        interleave([stage_pro(chunks[i + 2]) if i + 2 < n else None,
                    stage_a(chunks[i + 1]) if i + 1 < n else None,
                    stage_b(c)])


def build_tail(L):
    B, nc, A, bank, idb, idf, epst = L["B"], L["nc"], L["A"], L["bank"], L["idb"], L["idf"], L["epst"]
    ev2, load_weight, rstd_of, make_xnT, ctr = L["ev2"], L["load_weight"], L["rstd_of"], L["make_xnT"], L["ctr"]
    w_in, nmw, b_gate, ssm_nw, w_att, w_ssm, w_out = L["w_in"], L["nmw"], L["b_gate"], L["ssm_nw"], L["w_att"], L["w_ssm"], L["w_out"]
    npost, nfpre, w_up, w_down, nfpost = L["npost"], L["nfpre"], L["w_up"], L["w_down"], L["nfpost"]
    ATT, YT, HS, x, out = L["ATT"], L["YT"], L["HS"], L["x"], L["out"]
    A.release(L["BASE"])
    nwb = A.alloc([1024], F32)
    B.dma(nwb, nmw.rearrange("(o n) -> o n", o=1).partition_broadcast(128))
    snw = A.alloc([16], F32)
    B.dma(snw, ssm_nw.rearrange("(k p) -> p k", p=128), allow_slow_non_contiguous=True)
    bg = A.alloc([16], F32)
    B.dma(bg, b_gate.rearrange("(k p) -> p k", p=128), allow_slow_non_contiguous=True)
    Wg = A.alloc([8, 2048], BF16)
    Watt = A.alloc([12, 1024], BF16)
    Wssm = A.alloc([16, 1024], BF16)
    Wout = A.alloc([8, 1024], BF16)
    L["load_weight_cast"](Wg, w_in, 8, 2048, col0=OFF_G)
    B.dma(Watt[0:64, :, :], w_att.rearrange("(h d) n -> d h n", d=64), q="pool")
    L["load_weight_cast"](Wout, w_out, 8, 1024)
    mk = A.mark()
    stg = [A.alloc([2048], F32) for _ in range(2)]
    load_weight(Wssm, lambda kc: w_ssm[kc * 128:(kc + 1) * 128, :], 1024, 16, snw, stg)
    A.release(mk)
    npb = A.alloc([1024], F32)
    B.dma(npb, npost.partition_broadcast(128))
    TB = 512
    NTB = TB // 128
    xk = A.alloc([NTB, 1024], F32)
    xn = A.alloc([1024], BF16)
    junk = A.alloc([1024], BF16)
    st4 = A.alloc([8], F32)
    xnT = A.alloc([8, TB], BF16)
    G = A.alloc([16, TB], BF16)
    aT = A.alloc([12, TB], BF16)
    yT = A.alloc([16, TB], BF16)
    m1 = [A.alloc([TB], F32) for _ in range(2)]
    m2 = [A.alloc([TB], F32) for _ in range(2)]
    mT = A.alloc([8, TB], BF16)
    hb = [A.alloc([1024], F32) for _ in range(1)]
    pbi = 0
    for blk in (range(MAIN0 // TB if L["nblk"] is None else L["nblk"]) if "c1" in L["stages"] else ()):
        t0 = blk * TB
        B.dma(aT[0:64, :, :], ATT[:, :, t0:t0 + TB].rearrange("h d t -> d h t"))
        B.dma(yT, YT[:, :, t0:t0 + TB].rearrange("c p t -> p c t"))
        make_xnT(MAIN0 + t0, NTB, None, xn, junk, st4, xnT, bank(7), keep_x=xk, wbc=nwb)
        for cc in range(16):
            pb = bank(pbi % 2)
            pbi += 1
            for kc in range(8):
                B.mm(pb[:, 0:TB], Wg[:, kc, cc * 128:(cc + 1) * 128], xnT[:, kc, :], start=(kc == 0), stop=(kc == 7))
            B.act(G[:, cc, :], pb[:, 0:TB], AF.Sigmoid, bias=bg[:, cc:cc + 1])
        for cc in range(8):
            pa = bank(2 + pbi % 2)
            pbs = bank(4 + pbi % 2)
            pbi += 1
            for hd in range(12):
                B.mm(pa[:, 0:TB], Watt[0:64, hd, cc * 128:(cc + 1) * 128], aT[0:64, hd, :], start=(hd == 0), stop=(hd == 11))
            for kc in range(16):
                B.mm(pbs[:, 0:TB], Wssm[:, kc, cc * 128:(cc + 1) * 128], yT[:, kc, :], start=(kc == 0), stop=(kc == 15))
            a1, a2 = m1[cc % 2], m2[cc % 2]
            B.tt("dve", a1, pa[:, 0:TB], G[:, cc, :], ALU.mult)
            B.tt("dve", a2, pbs[:, 0:TB], G[:, 8 + cc, :], ALU.mult)
            B.tt("pool", mT[:, cc, :], a1, a2, ALU.add)
        for t in range(TB // 128):
            pw = L["ps"][:, 6 * 512:8 * 512]
            for nb in range(2):
                for kc in range(8):
                    B.mm(pw[:, nb * 512:(nb + 1) * 512], mT[:, kc, t * 128:(t + 1) * 128], Wout[:, kc, nb * 512:(nb + 1) * 512],
                         start=(kc == 0), stop=(kc == 7))
            for nb in range(2):
                B.act(junk[:, nb * 512:(nb + 1) * 512], pw[:, nb * 512:(nb + 1) * 512], AF.Square, accum_out=st4[:, 3 + nb:4 + nb])
            B.tt("dve", st4[:, 5:6], st4[:, 3:4], st4[:, 4:5], ALU.add)
            rstd_of(st4[:, 5:6], D, st4[:, 6:7], st4[:, 7:8])
            hh = hb[0]
            for nb in range(2):
                B.tt("dve", hh[:, nb * 512:(nb + 1) * 512], pw[:, nb * 512:(nb + 1) * 512], npb[:, nb * 512:(nb + 1) * 512], ALU.mult)
            B.stt("dve", hh, hh, st4[:, 7:8], xk[:, t, :], ALU.mult, ALU.add)
            B.dma(HS[t0 + t * 128: t0 + (t + 1) * 128, :], hh, q="pool")
    A.release(L["BASE"])
    fwb = A.alloc([1024], F32)
    B.dma(fwb, nfpre.rearrange("(o n) -> o n", o=1).partition_broadcast(128))
    Wup = A.alloc([8, 4096], BF16)
    Wdn = A.alloc([32, 1024], BF16)
    L["load_weight_cast"](Wup, w_up, 8, 4096)
    L["load_weight_cast"](Wdn, w_down, 32, 1024)
    nfb = A.alloc([1024], F32)
    B.dma(nfb, nfpost.partition_broadcast(128))
    hk = A.alloc([NTB, 1024], F32)
    hn = A.alloc([1024], BF16)
    junk = A.alloc([1024], BF16)
    st4 = A.alloc([8], F32)
    hnT = A.alloc([8, TB], BF16)
    hid = A.alloc([32, TB], BF16)
    rl = [A.alloc([TB], F32) for _ in range(2)]
    ob = [A.alloc([1024], F32) for _ in range(1)]
    for blk in (range(MAIN0 // TB if L["nblk"] is None else L["nblk"]) if "c2" in L["stages"] else ()):
        t0 = blk * TB
        for t in range(NTB):
            ht = hk[:, t, :]
            B.dma(ht, HS[t0 + t * 128: t0 + (t + 1) * 128, :])
            B.act(junk, ht, AF.Square, accum_out=st4[:, 0:1])
            rstd_of(st4[:, 0:1], D, st4[:, 1:2], st4[:, 2:3])
            B.stt("dve", hn, ht, st4[:, 2:3], fwb, ALU.mult, ALU.mult)
            pst = bank(7).bitcast(BF16)
            for kc in range(8):
                B.tr(pst[:, kc * 128:(kc + 1) * 128], hn[:, kc * 128:(kc + 1) * 128], idb)
            B.copy(ev2(), hnT[:, :, t * 128:(t + 1) * 128], pst.rearrange("p (k t) -> p k t", k=8))
        for cc in range(32):
            pb = bank(pbi % 2)
            pbi += 1
            for kc in range(8):
                B.mm(pb[:, 0:TB], Wup[:, kc, cc * 128:(cc + 1) * 128], hnT[:, kc, :], start=(kc == 0), stop=(kc == 7))
            r = rl[cc % 2]
            B.act(r, pb[:, 0:TB], AF.Relu)
            B.tt(("dve", "pool")[cc % 2], hid[:, cc, :], r, r, ALU.mult)
        for t in range(NTB):
            pw = L["ps"][:, (2 + 2 * (t % 2)) * 512:(4 + 2 * (t % 2)) * 512]
            for nb in range(2):
                for kc in range(32):
                    B.mm(pw[:, nb * 512:(nb + 1) * 512], hid[:, kc, t * 128:(t + 1) * 128], Wdn[:, kc, nb * 512:(nb + 1) * 512],
                         start=(kc == 0), stop=(kc == 31))
            for nb in range(2):
                B.act(junk[:, nb * 512:(nb + 1) * 512], pw[:, nb * 512:(nb + 1) * 512], AF.Square, accum_out=st4[:, 3 + nb:4 + nb])
            B.tt("dve", st4[:, 5:6], st4[:, 3:4], st4[:, 4:5], ALU.add)
            rstd_of(st4[:, 5:6], D, st4[:, 6:7], st4[:, 7:8])
            oo = ob[0]
            for nb in range(2):
                B.tt("dve", oo[:, nb * 512:(nb + 1) * 512], pw[:, nb * 512:(nb + 1) * 512], nfb[:, nb * 512:(nb + 1) * 512], ALU.mult)
            B.stt("dve", oo, oo, st4[:, 7:8], hk[:, t, :], ALU.mult, ALU.add)
            B.dma(out[t0 + t * 128: t0 + (t + 1) * 128, :], oo, q="pool")


def _alibi_slopes(n):
    def pow2(m):
        start = 2.0 ** (-8.0 / m)
        return [start ** (i + 1) for i in range(m)]
    if (n & (n - 1)) == 0:
        s = pow2(n)
    else:
        c = 2 ** int(np.floor(np.log2(n)))
        s = pow2(c) + pow2(2 * c)[0::2][: n - c]
    return np.array(s, dtype=np.float32)


def _constants():
    k = np.arange(128)[:, None]
    q = np.arange(128)[None, :]
    slopes = _alibi_slopes(NH).astype(np.float64)
    masks = np.zeros((128, 3, NH, 2, 128), np.float32)
    for g, (win, d) in enumerate(PATTERNS):
        dist_prev = 128 + q - k
        dist_cur = q - k
        for bl, dist in enumerate((dist_prev, dist_cur)):
            valid = (dist >= 0) & (dist <= 128)
            for h in range(NH):
                m = np.where(valid, np.exp(-slopes[h] * np.clip(dist, 0, 128) * d), 0.0)
                masks[:, g, h, bl, :] = m.astype(np.float32)
    triu = (k <= q).astype(np.float32)
    ls = (k > q).astype(np.float32)
    return dict(c_ident=np.eye(128, dtype=np.float32), c_masks=masks.reshape(128, -1),
                c_triu=triu, c_ls=ls)


_NC_CACHE = {}


def kernel(**inputs):
    x = np.asarray(inputs["x"], dtype=np.float32)
    nb = x.shape[0]
    consts = _constants()
    shared = {}
    for name in ("w_in", "norm_mix_pre_w", "b_gate", "conv_w", "conv_b", "ssm_norm_w", "w_att_proj", "w_ssm_proj",
                 "w_out", "norm_ffn_pre_w", "w_up", "w_down"):
        shared[name] = np.ascontiguousarray(np.asarray(inputs[name], dtype=np.float32)[0])
    for name in ("dt_bias", "a_log", "d_skip", "norm_mix_post_w", "norm_ffn_post_w"):
        shared[name] = np.ascontiguousarray(np.asarray(inputs[name], dtype=np.float32)[0][None, :])
    shared.update(consts)
    in_maps = []
    for c in range(8):
        b, half = c // 2, c % 2
        xl = np.zeros((LT, D), np.float32)
        if half == 1:
            xl[:] = x[b]
        else:
            xl[MAIN0:] = x[b, :MAIN0]
        m = dict(shared)
        m["x"] = xl
        m["c_pfx"] = np.full((128, 64), float(half), np.float32)
        in_maps.append(m)
    if "nc" not in _NC_CACHE:
        _NC_CACHE["nc"] = build_program()
    nc = _NC_CACHE["nc"]
    res = run_bass_kernel_spmd(nc, in_maps, core_ids=list(range(8)))
    outp = np.zeros((nb, SEQ, D), np.float32)
    for c in range(8):
        b, half = c // 2, c % 2
        outp[b, half * MAIN0:(half + 1) * MAIN0] = res.results[c]["out"]
    return outp
```
